# Optimizing a Trainium2 kernel written in Bass

```python
import math
import jax, jax.numpy as jnp
from jax import lax
import numpy as np

D_MODEL = 1024
BATCH = 8
SEQ = 2048
DEPTH = 4
DEC_BATCH = 128
DEC_SEQ = 1
PAST_LEN = 8192
PAGE_SIZE = 128

HEAD_DIM = 64
N_HEADS = 8
N_KV_HEADS = 2
GROUP = N_HEADS // N_KV_HEADS
Q_WIDTH = N_HEADS * HEAD_DIM
KV_WIDTH = N_KV_HEADS * HEAD_DIM
WINDOW = 128
N_BUCKETS = 32
MAX_DISTANCE = 128
LRU_WIDTH = D_MODEL
LRU_HEADS = 8
LRU_BLOCK = LRU_WIDTH // LRU_HEADS
LRU_C = 8.0
CONV_WIDTH = 4
D_FF = -(-8 * D_MODEL // (3 * 256)) * 256
PLE_DIM = 256
IN_WIDTH = Q_WIDTH + 2 * KV_WIDTH + 2 * LRU_WIDTH + 2 * D_MODEL
EPS = 1e-6
NEG_INF = -1e30

kernel_name = 'hybrid_swa_rglru_step'


def rms_norm(x, g):
    xf = x.astype(jnp.float32)
    y = xf * lax.rsqrt(jnp.mean(xf * xf, axis=-1, keepdims=True) + EPS)
    return (y * g.astype(jnp.float32)).astype(x.dtype)


def t5_bucket(dist):
    n = jnp.maximum(dist, 0)
    max_exact = N_BUCKETS // 2
    nf = jnp.maximum(n, 1).astype(jnp.float32)
    large = max_exact + (jnp.log(nf / max_exact) / math.log(MAX_DISTANCE / max_exact)
                         * (N_BUCKETS - max_exact)).astype(jnp.int32)
    large = jnp.minimum(large, N_BUCKETS - 1)
    return jnp.where(n < max_exact, n, large)


def window_attention(q, k, v, dist, valid, t5_table, sinks):
    s = jnp.einsum('bnqkgd,bnskd->bnkgqs', q, k,
                   preferred_element_type=jnp.float32) * (HEAD_DIM ** -0.5)
    bias = jnp.take(t5_table, t5_bucket(dist), axis=0)
    bias = jnp.transpose(bias, (2, 0, 1)).reshape(N_KV_HEADS, GROUP, *dist.shape)
    s = jnp.where(valid, s + bias.astype(jnp.float32), NEG_INF)
    sink = jnp.broadcast_to(sinks.astype(jnp.float32).reshape(N_KV_HEADS, GROUP, 1, 1),
                            s.shape[:-1] + (1,))
    pr = jax.nn.softmax(jnp.concatenate([s, sink], axis=-1), axis=-1)[..., :-1]
    return jnp.einsum('bnkgqs,bnskd->bnqkgd', pr.astype(v.dtype), v)


def causal_conv(xb, buf, w, b):
    L = xb.shape[1]
    xp = jnp.concatenate([buf.astype(xb.dtype), xb], axis=1)
    y = b + sum(w[j] * xp[:, j:j + L] for j in range(CONV_WIDTH))
    return y, xp[:, L:]


def rg_lru(xb, h0, w_a, b_a, w_x, b_x, lam):
    B, L, _ = xb.shape
    f32 = jnp.float32
    xf = xb.astype(f32)
    xh = xf.reshape(B, L, LRU_HEADS, LRU_BLOCK)
    r = jax.nn.sigmoid(jnp.einsum('blhi,hij->blhj', xh, w_a.astype(f32)).reshape(B, L, LRU_WIDTH)
                       + b_a.astype(f32))
    ig = jax.nn.sigmoid(jnp.einsum('blhi,hij->blhj', xh, w_x.astype(f32)).reshape(B, L, LRU_WIDTH)
                        + b_x.astype(f32))
    log_a = -LRU_C * r * jax.nn.softplus(-lam.astype(f32))
    a = jnp.exp(log_a)
    b = jnp.sqrt(-jnp.expm1(2.0 * log_a)) * (ig * xf)
    b = b.at[:, 0].add(a[:, 0] * h0.astype(f32))

    def combine(lhs, rhs):
        a1, b1 = lhs
        a2, b2 = rhs
        return a1 * a2, a2 * b1 + b2

    _, h = lax.associative_scan(combine, (a, b), axis=1)
    return h.astype(xb.dtype), h[:, -1].astype(h0.dtype)


def decoder_layer(x, p, k_buf, v_buf, h0, conv_buf, is_prompt, t5_table, ln1, w_in, q_gain,
                  k_gain, sinks, w_o_attn, conv_w, conv_b, w_a, b_a, w_x, b_x, lam, w_o_lru,
                  w_out, ln2, w_gate, w_up, w_down, ln3, w_ple, w_ple_gate):
    B, L, _ = x.shape
    h = rms_norm(x, ln1)
    proj = h @ w_in
    splits = np.cumsum([Q_WIDTH, KV_WIDTH, KV_WIDTH, LRU_WIDTH, LRU_WIDTH, D_MODEL]).tolist()
    q, k, v, xr, xg, g_att, g_lru = jnp.split(proj, splits, axis=-1)

    q = rms_norm(q.reshape(B, L, N_KV_HEADS, GROUP, HEAD_DIM), q_gain)
    k = rms_norm(k.reshape(B, L, N_KV_HEADS, HEAD_DIM), k_gain)
    v = v.reshape(B, L, N_KV_HEADS, HEAD_DIM)
    if is_prompt:
        nb = L // WINDOW
        qb = q.reshape(B, nb, WINDOW, N_KV_HEADS, GROUP, HEAD_DIM)

        def band(t):
            tb = t.reshape(B, nb, WINDOW, N_KV_HEADS, HEAD_DIM)
            prev = jnp.concatenate([jnp.zeros_like(tb[:, :1]), tb[:, :-1]], axis=1)
            return jnp.concatenate([prev, tb], axis=2)

        kb, vb = band(k), band(v)
        qloc = WINDOW + jnp.arange(WINDOW)
        kloc = jnp.arange(2 * WINDOW)
        dist = qloc[:, None] - kloc[None, :]
        kabs = jnp.arange(nb)[:, None, None] * WINDOW - WINDOW + kloc[None, None, :]
        valid = (((dist >= 0) & (dist < WINDOW))[None] & (kabs >= 0))[:, None, None]
        o = window_attention(qb, kb, vb, dist, valid, t5_table, sinks)
        new_k, new_v = k[:, L - WINDOW:], v[:, L - WINDOW:]
    else:
        W = k_buf.shape[1]
        kc = jnp.concatenate([k_buf.astype(k.dtype), k], axis=1)
        vc = jnp.concatenate([v_buf.astype(v.dtype), v], axis=1)
        qpos = PAST_LEN + jnp.arange(L)
        kpos = jnp.concatenate([PAST_LEN - W + jnp.arange(W), qpos])
        dist = qpos[:, None] - kpos[None, :]
        valid = ((dist >= 0) & (dist < WINDOW))[None, None, None]
        o = window_attention(q[:, None], kc[:, None], vc[:, None], dist, valid, t5_table, sinks)
        new_k, new_v = kc[:, L:], vc[:, L:]
    o_att = o.reshape(B, L, Q_WIDTH) @ w_o_attn

    xc, new_conv = causal_conv(xr, conv_buf, conv_w, conv_b)
    hseq, new_h = rg_lru(xc, h0, w_a, b_a, w_x, b_x, lam)
    o_lru = (hseq * jax.nn.gelu(xg)) @ w_o_lru

    x = x + (jax.nn.sigmoid(g_att) * o_att + jax.nn.sigmoid(g_lru) * o_lru) @ w_out

    h2 = rms_norm(x, ln2)
    x = x + (jax.nn.silu(h2 @ w_gate) * (h2 @ w_up)) @ w_down

    x = x + jax.nn.sigmoid(rms_norm(x, ln3) @ w_ple_gate) * (p @ w_ple)
    return x, new_k, new_v, new_h, new_conv


def setup_inputs(seed: int = 0) -> dict:
    key = jax.random.key(seed)
    ks = jax.random.split(key, 32)
    f32 = jnp.float32
    win_buf = min(WINDOW, PAST_LEN)

    def nrm(k, shape, scale=1.0):
        return jax.random.normal(k, shape, f32) * scale

    u = jax.random.uniform(ks[20], (DEPTH, LRU_WIDTH), f32, 0.9, 0.999)
    sa = u ** (1.0 / LRU_C)
    lam = jnp.log(sa) - jnp.log1p(-sa)
    return {
        'x_prompt': nrm(ks[0], (BATCH, SEQ, D_MODEL)),
        'x_sample': nrm(ks[1], (DEC_BATCH, DEC_SEQ, D_MODEL)),
        'cache_k_win': nrm(ks[2], (DEPTH, DEC_BATCH, win_buf, N_KV_HEADS, HEAD_DIM)),
        'cache_v_win': nrm(ks[3], (DEPTH, DEC_BATCH, win_buf, N_KV_HEADS, HEAD_DIM)),
        'state_lru_h': nrm(ks[4], (DEPTH, DEC_BATCH, LRU_WIDTH), 0.5),
        'state_conv': nrm(ks[5], (DEPTH, DEC_BATCH, CONV_WIDTH - 1, LRU_WIDTH)),
        'p_prompt': nrm(ks[6], (DEPTH, BATCH, SEQ, PLE_DIM)),
        'p_sample': nrm(ks[7], (DEPTH, DEC_BATCH, DEC_SEQ, PLE_DIM)),
        't5_table': nrm(ks[8], (N_BUCKETS, N_HEADS), 0.5),
        'ln1': 1.0 + nrm(ks[9], (DEPTH, D_MODEL), 0.1),
        'w_in': nrm(ks[10], (DEPTH, D_MODEL, IN_WIDTH), D_MODEL ** -0.5),
        'q_gain': 1.0 + nrm(ks[11], (DEPTH, HEAD_DIM), 0.1),
        'k_gain': 1.0 + nrm(ks[12], (DEPTH, HEAD_DIM), 0.1),
        'sinks': nrm(ks[13], (DEPTH, N_HEADS), 1.0),
        'w_o_attn': nrm(ks[14], (DEPTH, Q_WIDTH, D_MODEL), Q_WIDTH ** -0.5),
        'conv_w': nrm(ks[15], (DEPTH, CONV_WIDTH, LRU_WIDTH), CONV_WIDTH ** -0.5),
        'conv_b': nrm(ks[16], (DEPTH, LRU_WIDTH), 0.01),
        'w_a': nrm(ks[17], (DEPTH, LRU_HEADS, LRU_BLOCK, LRU_BLOCK), LRU_BLOCK ** -0.5),
        'b_a': nrm(ks[18], (DEPTH, LRU_WIDTH), 0.01),
        'w_x': nrm(ks[19], (DEPTH, LRU_HEADS, LRU_BLOCK, LRU_BLOCK), LRU_BLOCK ** -0.5),
        'b_x': nrm(ks[21], (DEPTH, LRU_WIDTH), 0.01),
        'lam': lam,
        'w_o_lru': nrm(ks[22], (DEPTH, LRU_WIDTH, D_MODEL), LRU_WIDTH ** -0.5),
        'w_out': nrm(ks[23], (DEPTH, D_MODEL, D_MODEL), D_MODEL ** -0.5),
        'ln2': 1.0 + nrm(ks[24], (DEPTH, D_MODEL), 0.1),
        'w_gate': nrm(ks[25], (DEPTH, D_MODEL, D_FF), D_MODEL ** -0.5),
        'w_up': nrm(ks[26], (DEPTH, D_MODEL, D_FF), D_MODEL ** -0.5),
        'w_down': nrm(ks[27], (DEPTH, D_FF, D_MODEL), D_FF ** -0.5),
        'ln3': 1.0 + nrm(ks[28], (DEPTH, D_MODEL), 0.1),
        'w_ple': nrm(ks[29], (DEPTH, PLE_DIM, D_MODEL), PLE_DIM ** -0.5),
        'w_ple_gate': nrm(ks[30], (DEPTH, D_MODEL, D_MODEL), D_MODEL ** -0.5),
    }


def reference(x_prompt, x_sample, cache_k_win, cache_v_win, state_lru_h, state_conv,
              p_prompt, p_sample, t5_table, ln1, w_in, q_gain, k_gain, sinks, w_o_attn,
              conv_w, conv_b, w_a, b_a, w_x, b_x, lam, w_o_lru, w_out, ln2, w_gate, w_up,
              w_down, ln3, w_ple, w_ple_gate):
    yp, ys = x_prompt, x_sample
    bp = x_prompt.shape[0]
    h0_p = jnp.zeros((bp, LRU_WIDTH), x_prompt.dtype)
    conv0_p = jnp.zeros((bp, CONV_WIDTH - 1, LRU_WIDTH), x_prompt.dtype)
    kp_l, vp_l, hp_l, cp_l = [], [], [], []
    ks_l, vs_l, hs_l, cs_l = [], [], [], []
    for i in range(DEPTH):
        lw = (t5_table, ln1[i], w_in[i], q_gain[i], k_gain[i], sinks[i], w_o_attn[i],
              conv_w[i], conv_b[i], w_a[i], b_a[i], w_x[i], b_x[i], lam[i], w_o_lru[i],
              w_out[i], ln2[i], w_gate[i], w_up[i], w_down[i], ln3[i], w_ple[i], w_ple_gate[i])
        yp, kp, vp, hp, cp = decoder_layer(yp, p_prompt[i], None, None, h0_p, conv0_p,
                                           True, *lw)
        ys, kss, vss, hss, css = decoder_layer(ys, p_sample[i], cache_k_win[i], cache_v_win[i],
                                               state_lru_h[i], state_conv[i], False, *lw)
        kp_l.append(kp); vp_l.append(vp); hp_l.append(hp); cp_l.append(cp)
        ks_l.append(kss); vs_l.append(vss); hs_l.append(hss); cs_l.append(css)
    new_k_win_prompt = jnp.stack(kp_l)
    new_v_win_prompt = jnp.stack(vp_l)
    new_lru_h_prompt = jnp.stack(hp_l)
    new_conv_prompt = jnp.stack(cp_l)
    new_k_win_sample = jnp.stack(ks_l)
    new_v_win_sample = jnp.stack(vs_l)
    new_lru_h_sample = jnp.stack(hs_l)
    new_conv_sample = jnp.stack(cs_l)
    return (yp, ys, new_k_win_prompt, new_v_win_prompt, new_lru_h_prompt, new_conv_prompt,
            new_k_win_sample, new_v_win_sample, new_lru_h_sample, new_conv_sample)
```

```python
import math
from contextlib import ExitStack

import numpy as np
import concourse.bass as bass
import concourse.mybir as mybir
from concourse.bass_utils import run_bass_kernel_spmd

F32 = mybir.dt.float32
BF16 = mybir.dt.bfloat16
AF = mybir.ActivationFunctionType
ALU = mybir.AluOpType

D = 1024
NL = 4
SEQ = 2048
HALF = 1024
NSAMP = 16
NCOL = HALF + NSAMP
DFF = 2816
KFF = DFF // 128
EPS = 1e-6
NEG = -30000.0
NCORES = 8
NBUF = 3
SLOT = 4096


class _Op:
    __slots__ = ("eng", "fn", "deps", "dma", "sem", "sigval", "prevval", "needed")

    def __init__(self, eng, fn, deps, dma):
        self.eng = eng
        self.fn = fn
        self.deps = deps
        self.dma = dma
        self.sem = None
        self.sigval = None
        self.prevval = 0
        self.needed = False


class Prog:
    ENGS = ("pe", "act", "dve", "pool", "sp")
    NSLOT = {"sp": 8, "pool": 2}

    def __init__(self, nc):
        self.nc = nc
        self.ops = []
        self.last_writer = {}
        self.readers = {}
        self.stopped = False

    def op(self, eng, fn, reads=(), writes=(), dma=False):
        if self.stopped:
            return None
        idx = len(self.ops)
        deps = set()
        for k in reads:
            w = self.last_writer.get(k)
            if w is not None:
                deps.add(w)
        for k in writes:
            w = self.last_writer.get(k)
            if w is not None:
                deps.add(w)
            rs = self.readers.get(k)
            if rs:
                deps.update(rs)
        for k in reads:
            self.readers.setdefault(k, []).append(idx)
        for k in writes:
            self.last_writer[k] = idx
            self.readers[k] = []
        deps.discard(idx)
        self.ops.append(_Op(eng, fn, deps, dma))
        return idx

    def pe(self, fn, reads=(), writes=()):
        return self.op("pe", fn, reads, writes)

    def act(self, fn, reads=(), writes=()):
        return self.op("act", fn, reads, writes)

    def dve(self, fn, reads=(), writes=()):
        return self.op("dve", fn, reads, writes)

    def pool(self, fn, reads=(), writes=()):
        return self.op("pool", fn, reads, writes)

    def dma(self, q, fn, reads=(), writes=()):
        return self.op(q, fn, reads, writes, dma=True)

    def emit(self, final_wait_ops=()):
        nc = self.nc
        ops = self.ops
        for o in ops:
            for d in o.deps:
                ops[d].needed = True
        for i in final_wait_ops:
            ops[i].needed = True
        with ExitStack() as st:
            csem = {e: st.enter_context(nc.semaphore("c_" + e)) for e in ("pe", "act", "dve", "pool")}
            slots = {
                q: [st.enter_context(nc.semaphore("d_%s_%d" % (q, i))) for i in range(n)]
                for q, n in self.NSLOT.items()
            }
            cnt = {e: 0 for e in csem}
            nd = {q: 0 for q in slots}
            for o in ops:
                if o.dma:
                    j = nd[o.eng]
                    ns = len(slots[o.eng])
                    o.sem = slots[o.eng][j % ns]
                    o.sigval = 16 * (j // ns + 1)
                    o.prevval = 16 * (j // ns)
                    nd[o.eng] = j + 1
                elif o.needed:
                    cnt[o.eng] += 1
                    o.sem = csem[o.eng]
                    o.sigval = cnt[o.eng]
            per_eng = {e: [] for e in self.ENGS}
            for i, o in enumerate(ops):
                per_eng[o.eng].append(i)
            block = st.enter_context(nc.Block())
            final = list(final_wait_ops)

            def gen(ename, eng):
                waited = {}

                def do_wait(sem, val):
                    key = id(sem)
                    if waited.get(key, 0) >= val:
                        return
                    waited[key] = val
                    eng.wait_ge(sem, val)

                for i in per_eng[ename]:
                    o = ops[i]
                    w = {}
                    for d in o.deps:
                        do_ = ops[d]
                        if (not do_.dma) and do_.eng == ename and ename == "pe":
                            continue
                        key = id(do_.sem)
                        if key not in w or w[key][1] < do_.sigval:
                            w[key] = (do_.sem, do_.sigval)
                    if o.dma and o.prevval > 0:
                        key = id(o.sem)
                        if key not in w or w[key][1] < o.prevval:
                            w[key] = (o.sem, o.prevval)
                    for sem, val in w.values():
                        do_wait(sem, val)
                    ins = o.fn(eng)
                    if o.dma:
                        ins.then_inc(o.sem, 16)
                    elif o.needed:
                        ins.then_inc(o.sem, 1)
                if ename == "sp":
                    for i in final:
                        do_wait(ops[i].sem, ops[i].sigval)

            @block.tensor
            def _(e):
                gen("pe", e)

            @block.scalar
            def _(e):
                gen("act", e)

            @block.vector
            def _(e):
                gen("dve", e)

            @block.gpsimd
            def _(e):
                gen("pool", e)

            @block.sync
            def _(e):
                gen("sp", e)


QPERM = np.concatenate([np.concatenate([np.arange(c * 64, c * 64 + 64), np.arange((4 + c) * 64, (4 + c) * 64 + 64)]) for c in range(4)])

VEC_NAMES = ["ln1", "ln2", "ln3", "cw0", "cw1", "cw2", "cw3", "conv_b", "b_a", "b_x", "lam"]
NV = len(VEC_NAMES)


def _pack(w):
    k, n = w.shape
    kc = k // 128
    return np.ascontiguousarray(w.reshape(kc, 128, n).transpose(1, 0, 2).reshape(128, kc * n))


def _items(l, W):
    win = W["w_in"][l]
    its = []
    its.append(("q", [_pack(win[:, 0:512][:, QPERM])]))
    its.append(("kv", [_pack(win[:, 512:768])]))
    for i in range(2):
        its.append(("ga%d" % i, [_pack(win[:, 2816 + 512 * i: 2816 + 512 * (i + 1)])]))
    its.append(("woa", [_pack(W["w_o_attn"][l][QPERM, :])]))
    for i in range(2):
        its.append(("g%d" % i, [_pack(win[:, 1792 + 512 * i: 1792 + 512 * (i + 1)])]))
    for c in range(8):
        its.append(("r%d" % c, [_pack(win[:, 768 + 128 * c: 768 + 128 * (c + 1)]), W["w_a"][l][c], W["w_x"][l][c]]))
    for i in range(4):
        its.append(("ol%d" % i, [_pack(W["w_o_lru"][l][:, 256 * i: 256 * (i + 1)]), _pack(win[:, 3840 + 256 * i: 3840 + 256 * (i + 1)])]))
    for i in range(2):
        its.append(("wo%d" % i, [_pack(W["w_out"][l][:, 512 * i: 512 * (i + 1)])]))
    for i in range(11):
        its.append(("ff%d" % i, [_pack(W["w_gate"][l][:, 256 * i: 256 * (i + 1)]), _pack(W["w_up"][l][:, 256 * i: 256 * (i + 1)])]))
    for m in range(8):
        its.append(("dn%d" % m, [_pack(W["w_down"][l][:, 128 * m: 128 * (m + 1)])]))
    for i in range(3):
        c0, c1 = 3 * i, min(3 * i + 3, 8)
        its.append(("pl%d" % i, [_pack(W["w_ple_gate"][l][:, 128 * c0: 128 * c1]), _pack(W["w_ple"][l][:, 128 * c0: 128 * c1])]))
    return its


def _item_sizes():
    sz = [("q", 4096), ("kv", 2048), ("g0", 4096), ("g1", 4096), ("ga0", 4096), ("ga1", 4096), ("woa", 4096)]
    sz += [("r%d" % c, 1280) for c in range(8)]
    sz += [("ol%d" % i, 4096) for i in range(4)]
    sz += [("wo%d" % i, 4096) for i in range(2)]
    sz += [("ff%d" % i, 4096) for i in range(11)]
    sz += [("dn%d" % m, 2816) for m in range(8)]
    sz += [("pl%d" % i, 1280 * (min(3 * i + 3, 8) - 3 * i)) for i in range(3)]
    return sz


ITEM_SIZES = _item_sizes()
ITEM_OFF = {}
_o = 0
for _n, _s in ITEM_SIZES:
    ITEM_OFF[_n] = (_o, _s)
    _o += _s
TOTW = _o
ITEM_ORDER = [n for n, _ in ITEM_SIZES]


def _t5_onehot():
    oh = np.zeros((33, 384), np.float32)
    for j in range(384):
        d = j - 128
        if 0 <= d < 128:
            if d < 16:
                b = d
            else:
                nf = np.float32(max(d, 1))
                v = np.log(nf / np.float32(16)) / np.float32(math.log(128 / 16)) * np.float32(16)
                b = min(16 + int(np.float32(v)), 31)
            oh[b, j] = 1.0
        else:
            oh[32, j] = 1.0
    return oh


class _Stop(Exception):
    pass


def build(nlayers=NL, nhalves=2, stop=None):
    nc = bass.Bass("TRN2", target_bir_lowering=False)

    def din(name, shape):
        return nc.dram_tensor(name, list(shape), F32, kind="ExternalInput")

    def dout(name, shape):
        return nc.dram_tensor(name, list(shape), F32, kind="ExternalOutput")

    XP = din("xp", [SEQ, D]).ap()
    XS = din("xs", [NSAMP, D]).ap()
    PP = din("pp", [NL, SEQ, 256]).ap()
    PS_ = din("psm", [NL, NSAMP, 256]).ap()
    CK_t = din("ck", [NL, NSAMP, 128, 128])
    CV_t = din("cv", [NL, NSAMP, 128, 128])
    CK, CV = CK_t.ap(), CV_t.ap()
    SLH = din("slh", [NL, NSAMP, D]).ap()
    SCV_t = din("scv", [NL, NSAMP, 3, D])
    SCV = SCV_t.ap()
    WALL = din("wall", [NL, 128, TOTW]).ap()
    VECS = din("vecs", [128, NL * NV * 8]).ap()
    QKG = din("qkg", [128, 2 * NL]).ap()
    SINK_t = din("sinks", [1, NL * 8])
    T5 = din("t5", [32, 8]).ap()
    OH = din("oh", [33, 384]).ap()
    IDN = din("ident", [128, 128]).ap()
    scr_t = nc.dram_tensor("scr", [8, 128, 384], F32, kind="Internal")

    YP = dout("yp", [SEQ, D]).ap()
    YS = dout("ys", [NSAMP, D]).ap()
    NKP = dout("nkp", [NL, 128, 128]).ap()
    NVP = dout("nvp", [NL, 128, 128]).ap()
    NHP = dout("nhp", [NL, 8, 128]).ap()
    NCP = dout("ncp", [NL, 3, D]).ap()
    NKS_t = dout("nks", [NL, NSAMP, 128, 128])
    NVS_t = dout("nvs", [NL, NSAMP, 128, 128])
    NKS, NVS = NKS_t.ap(), NVS_t.ap()
    NHS = dout("nhs", [NL, NSAMP, D]).ap()
    NCS_t = dout("ncs", [NL, NSAMP, 3, D])
    NCS = NCS_t.ap()

    with ExitStack() as st:
        def sb(name, shape, dt):
            return st.enter_context(nc.sbuf_tensor(name, list(shape), dt))

        xres = sb("xres", [128, 8, NCOL], F32)
        hb = sb("hb", [128, 8, NCOL], BF16)
        S = sb("S", [128, 22, NCOL], BF16)
        kT = sb("kT", [128, NCOL], BF16)
        k32keep = sb("k32keep", [128, 128 + NSAMP], F32)
        vz = sb("vz", [128, 8, 192], BF16)
        kprev = sb("kprev", [128, NL, 128], BF16)
        vprev = sb("vprev", [128, NL, 192], BF16)
        v32last = sb("v32last", [128, 128], F32)
        vnew32 = sb("vnew32", [NSAMP, 128], F32)
        bias = sb("bias", [128, 8, 2, 128], F32)
        biasS = sb("biasS", [128, 8], F32)
        wring = sb("wring", [128, NBUF, SLOT], BF16)
        vecs = sb("vecs_sb", [128, NL, NV, 8], F32)
        hbias = sb("hbias", [128, NL, 2, 8], F32)
        cl = sb("cl", [128, NL, 2, 8], F32)
        qkg = sb("qkg_sb", [128, 2 * NL], F32)
        sinkexp = sb("sinkexp", [128, NL * 8], F32)
        ident = sb("ident_sb", [128, 128], F32)
        identb = sb("identb", [128, 128], BF16)
        onesn = sb("onesn", [128, 128], BF16)
        ones1 = sb("ones1", [128, 128], BF16)
        blk1 = sb("blk1", [128, 128], BF16)
        diag = sb("diag", [128, 2, 4, 128], BF16)
        xrb2 = sb("xrb2", [128, 2, 4 + NCOL], BF16)
        xrb = [xrb2[:, i, 0:3 + NCOL] for i in range(2)]
        pT = xrb2[:, :, 0:NCOL]
        xr32s = sb("xr32s", [128, 8, NSAMP], F32)
        xr32l = sb("xr32l", [128, 8, 3], F32)
        xtail = sb("xtail", [128, NL, 8, 3], BF16)
        hlast = sb("hlast", [128, NL, 8], F32)
        hs32 = sb("hs32", [128, 8, NSAMP], F32)
        h0T = sb("h0T", [128, 8, NSAMP], F32)
        cbT = sb("cbT", [128, 8, 3 * NSAMP], BF16)
        KsT = sb("KsT", [128, NSAMP, 128], BF16)
        Vs = sb("Vs", [128, NSAMP, 192], BF16)
        t5sb = sb("t5sb", [33, 8], F32)
        stage = [sb("stage%d" % i, [128, 1024], F32) for i in range(2)]
        t32 = [sb("t32_%d" % i, [128, 512], F32) for i in range(5)]
        raR = [sb("raR%d" % i, [128, 512], F32) for i in range(3)]
        igR = [sb("igR%d" % i, [128, 512], F32) for i in range(3)]
        a2R = [sb("a2R%d" % i, [128, 512], F32) for i in range(2)]
        xcbR = [sb("xcbR%d" % i, [128, 512], BF16) for i in range(3)]
        tb16 = [sb("tb16_%d" % i, [128, 512], BF16) for i in range(6)]
        psum = [st.enter_context(nc.psum_tensor("ps%d" % i, [128, 512], F32)) for i in range(8)]
        ohsb = t32[0][0:33, 0:384]
        ones33 = t32[1][0:33, 0:128]
        L33 = t32[2][0:33, 0:128]

        P = Prog(nc)
        ctr = {"ps": 0, "t32": 0, "tb": 0, "st": 0, "w": 0}

        def nps():
            b = ctr["ps"] % 8
            ctr["ps"] += 1
            return psum[b], ("ps", b)

        def nt32():
            i = ctr["t32"] % len(t32)
            ctr["t32"] += 1
            return t32[i], ("t32", i)

        def ntb():
            i = ctr["tb"] % len(tb16)
            ctr["tb"] += 1
            return tb16[i], ("tb", i)

        def nstage():
            i = ctr["st"] % len(stage)
            ctr["st"] += 1
            return stage[i], ("stage", i)

        outs = []

        def chk(name):
            if stop == name:
                P.stopped = True

        wseq = [(hf, l, n) for hf in range(nhalves) for l in range(nlayers) for n in ITEM_ORDER]
        wpos = {k: i for i, k in enumerate(wseq)}
        wissued = [0]

        def wneed(hf, l, name, ahead=NBUF - 1):
            i = wpos[(hf, l, name)]
            upto = min(len(wseq), i + 1 + ahead)
            while wissued[0] < upto:
                j = wissued[0]
                _, l2, n2 = wseq[j]
                off, sz = ITEM_OFF[n2]
                slot = j % NBUF
                P.dma("pool", lambda e, l2=l2, off=off, sz=sz, slot=slot: e.dma_start(out=wring[:, slot, 0:sz], in_=WALL[l2, :, off:off + sz]),
                      writes=[("w", slot)])
                wissued[0] += 1
            return wring[:, i % NBUF, :], ("w", i % NBUF)

        def mm(ps_ap, pairs, reads, pskey):
            n = len(pairs)
            for i, (l_, r_) in enumerate(pairs):
                P.pe(lambda e, l_=l_, r_=r_, i=i: e.matmul(ps_ap, lhsT=l_, rhs=r_, start=(i == 0), stop=(i == n - 1)),
                     reads=reads, writes=[pskey])

        def tm2fm(src_rows_ap, nrows, dst_fn, dst_keys, extra_reads=()):
            sg, sgk = nstage()
            P.dma("sp", lambda e: e.dma_start(out=sg[0:nrows, :], in_=src_rows_ap), reads=list(extra_reads), writes=[sgk])
            for half8 in range(2):
                ps, pk = nps()
                for cc in range(4):
                    c = half8 * 4 + cc
                    P.pe(lambda e, c=c, cc=cc, ps=ps: e.transpose(ps[:, cc * 128: cc * 128 + nrows], sg[0:nrows, c * 128:(c + 1) * 128], ident[0:nrows, 0:nrows]),
                         reads=[sgk, "ident"], writes=[pk])
                for cc in range(4):
                    c = half8 * 4 + cc
                    P.act(lambda e, c=c, cc=cc, ps=ps: e.activation(out=dst_fn(c), in_=ps[:, cc * 128: cc * 128 + nrows], func=AF.Copy),
                          reads=[pk], writes=[dst_keys[c]])

        def fm2tm(src_fn, src_keys, ncols, dst_ap, dst_key):
            sg, sgk = nstage()
            for half8 in range(2):
                ps, pk = nps()
                for cc in range(4):
                    c = half8 * 4 + cc
                    P.pe(lambda e, c=c, cc=cc, ps=ps: e.transpose(ps[0:ncols, cc * 128:(cc + 1) * 128], src_fn(c), ident[:, :]),
                         reads=[src_keys[c], "ident"], writes=[pk])
                P.act(lambda e, half8=half8, ps=ps: e.activation(out=sg[0:ncols, half8 * 512:(half8 + 1) * 512], in_=ps[0:ncols, :], func=AF.Copy),
                      reads=[pk], writes=[sgk])
            o = P.dma("sp", lambda e: e.dma_start(out=dst_ap, in_=sg[0:ncols, :]), reads=[sgk], writes=[dst_key])
            outs.append(o)

        P.dma("sp", lambda e: e.dma_start(out=ident[:], in_=IDN), writes=["ident"])
        P.dma("sp", lambda e: e.dma_start(out=vecs[:].rearrange("p l v c -> p (l v c)"), in_=VECS), writes=["vecs"])
        P.dma("sp", lambda e: e.dma_start(out=qkg[:], in_=QKG), writes=["qkg"])
        P.dma("sp", lambda e: e.dma_start(out=sinkexp[:], in_=bass.AP(SINK_t, 0, [[0, 128], [1, NL * 8]])), writes=["sinkexp"])
        P.dma("sp", lambda e: e.dma_start(out=t5sb[0:32, :], in_=T5), writes=["t5a"])
        P.dma("sp", lambda e: e.dma_start(out=ohsb, in_=OH), writes=[("t32", 0)])
        P.dve(lambda e: e.memset(t5sb[32:33, :], NEG), writes=["t5b"])
        P.dve(lambda e: e.memset(ones33, 1.0), writes=[("t32", 1)])
        P.dve(lambda e: e.memset(onesn[:], 1.0 / D), writes=["onesn"])
        P.dve(lambda e: e.memset(ones1[:], 1.0), writes=["ones1"])
        P.dve(lambda e: e.memset(blk1[:], 0.0), writes=["blk1"])
        P.dve(lambda e: e.memset(blk1[0:64, 0:64], 1.0 / 64), reads=["blk1"], writes=["blk1"])
        P.dve(lambda e: e.memset(blk1[64:128, 64:128], 1.0 / 64), reads=["blk1"], writes=["blk1"])
        P.dve(lambda e: e.memset(vz[:], 0.0), writes=[("vz", b) for b in range(8)])
        P.dve(lambda e: e.memset(Vs[:], 0.0), writes=["Vs"])
        P.dve(lambda e: e.memset(vprev[:], 0.0), writes=[("vprev", l) for l in range(NL)])
        P.dve(lambda e: e.tensor_copy(out=identb[:], in_=ident[:]), reads=["ident"], writes=["identb"])
        P.act(lambda e: e.activation(out=sinkexp[:], in_=sinkexp[:], func=AF.Exp), reads=["sinkexp"], writes=["sinkexp"])
        for l in range(nlayers):
            lam_ap = vecs[:, l, 10, :]
            P.act(lambda e, l=l, lam_ap=lam_ap: e.activation(out=cl[:, l, 0, :], in_=lam_ap, func=AF.Exp, scale=-1.0), reads=["vecs"], writes=[("cl", l)])
            P.act(lambda e, l=l: e.activation(out=cl[:, l, 0, :], in_=cl[:, l, 0, :], func=AF.Ln, bias=1.0), reads=[("cl", l)], writes=[("cl", l)])
            P.dve(lambda e, l=l: e.tensor_scalar(out=cl[:, l, 1, :], in0=cl[:, l, 0, :], scalar1=-16.0, scalar2=None, op0=ALU.mult), reads=[("cl", l)], writes=[("cl2", l)])
            P.dve(lambda e, l=l: e.tensor_scalar(out=cl[:, l, 0, :], in0=cl[:, l, 0, :], scalar1=-8.0, scalar2=None, op0=ALU.mult), reads=[("cl", l), ("cl2", l)], writes=[("cl", l)])
            P.dve(lambda e, l=l: e.tensor_scalar(out=hbias[:, l, 0, :], in0=vecs[:, l, 8, :], scalar1=-1.0, scalar2=None, op0=ALU.mult), reads=["vecs"], writes=[("hbias", l)])
            P.dve(lambda e, l=l: e.tensor_scalar(out=hbias[:, l, 1, :], in0=vecs[:, l, 9, :], scalar1=-1.0, scalar2=None, op0=ALU.mult), reads=["vecs", ("hbias", l)], writes=[("hbias", l)])
        for hd in range(8):
            P.dve(lambda e, hd=hd: e.tensor_scalar(out=L33, in0=ones33, scalar1=t5sb[:, hd:hd + 1], scalar2=None, op0=ALU.mult),
                  reads=["t5a", "t5b", ("t32", 1)], writes=[("t32", 2)])
            ps, pk = nps()
            P.pe(lambda e, ps=ps: e.matmul(ps[:, 0:384], lhsT=L33, rhs=ohsb, start=True, stop=True), reads=[("t32", 2), ("t32", 0)], writes=[pk])
            tt, tk = t32[3 + hd % 2], ("t32", 3 + hd % 2)
            P.dve(lambda e, ps=ps, tt=tt: e.tensor_copy(out=tt[:, 0:384], in_=ps[:, 0:384]), reads=[pk], writes=[tk])
            P.dma("sp", lambda e, hd=hd, tt=tt: e.dma_start(out=scr_t.ap()[hd], in_=tt[:, 0:384]), reads=[tk], writes=[("scr", hd)])
            P.dma("sp", lambda e, hd=hd: e.dma_start(out=bias[:, hd, :, :], in_=bass.AP(scr_t, hd * 128 * 384 + 128, [[383, 128], [128, 2], [1, 128]])),
                  reads=[("scr", hd)], writes=["bias"])
        P.dve(lambda e: e.tensor_copy(out=biasS[:, :], in_=bias[:, :, 1, 0]), reads=["bias"], writes=["biasS"])
        P.dve(lambda e: e.tensor_copy(out=biasS[0:1, :], in_=bias[0:1, :, 0, 0]), reads=["bias", "biasS"], writes=["biasS"])

        chk("setup")
        for hf in range(nhalves):
            t0 = hf * HALF
            cts = [(0, 512, 0), (512, 512, 1)]
            if hf == 0:
                cts.append((HALF, NSAMP, 2))
            has_s = hf == 0

            def xk(c, cti):
                return ("x", c, cti)

            def hk(c, cti):
                return ("h", c, cti)

            def sk(slot, cti):
                return ("S", slot, cti)

            def qk(slot, cti):
                if cti == 2:
                    return [("Sq", slot, "s", g) for g in range(2)]
                return [("Sq", slot, 4 * cti + b, g) for b in range(4) for g in range(2)]

            for b in range(8):
                cti = b // 4
                tm2fm(XP[t0 + b * 128: t0 + (b + 1) * 128, :], 128,
                      lambda c, b=b: xres[:, c, b * 128:(b + 1) * 128], [xk(c, cti) for c in range(8)])
            if has_s:
                tm2fm(XS[:, :], NSAMP, lambda c: xres[:, c, HALF:HALF + NSAMP], [xk(c, 2) for c in range(8)])

            for l in range(nlayers):
                def rmsnorm(vidx, l=l):
                    for (c0, n, cti) in cts:
                        ps, pk = nps()
                        for c in range(8):
                            sq, sqk = ntb()
                            P.act(lambda e, c=c, sq=sq, c0=c0, n=n: e.activation(out=sq[:, 0:n], in_=xres[:, c, c0:c0 + n], func=AF.Square),
                                  reads=[xk(c, cti)], writes=[sqk])
                            P.pe(lambda e, c=c, sq=sq, ps=ps, n=n: e.matmul(ps[:, 0:n], lhsT=onesn[:], rhs=sq[:, 0:n], start=(c == 0), stop=(c == 7)),
                                 reads=[sqk, "onesn"], writes=[pk])
                        rs, rk = nt32()
                        P.act(lambda e, ps=ps, rs=rs, n=n: e.activation(out=rs[:, 0:n], in_=ps[:, 0:n], func=AF.Ln, bias=EPS), reads=[pk], writes=[rk])
                        P.act(lambda e, rs=rs, ps=ps, n=n: e.activation(out=ps[:, 0:n], in_=rs[:, 0:n], func=AF.Exp, scale=-0.5), reads=[rk], writes=[pk])
                        for c in range(8):
                            P.dve(lambda e, c=c, ps=ps, c0=c0, n=n: e.scalar_tensor_tensor(
                                out=hb[:, c, c0:c0 + n], in0=xres[:, c, c0:c0 + n], scalar=vecs[:, l, vidx, c:c + 1],
                                in1=ps[:, 0:n], op0=ALU.mult, op1=ALU.mult),
                                reads=[xk(c, cti), pk, "vecs"], writes=[hk(c, cti)])

                def hreads(cti):
                    return [hk(c, cti) for c in range(8)]

                chk("xload")
                rmsnorm(0)
                chk("norm1")

                wq, wqk = wneed(hf, l, "q")
                for (c0, n, cti) in cts:
                    for qc in range(4):
                        ps, pk = nps()
                        mm(ps[:, 0:n], [(wq[:, kc * 512 + qc * 128: kc * 512 + (qc + 1) * 128], hb[:, kc, c0:c0 + n]) for kc in range(8)],
                           [wqk] + hreads(cti), pk)
                        sq, sqk = ntb()
                        q32, q32k = nt32()
                        P.act(lambda e, ps=ps, sq=sq, n=n: e.activation(out=sq[:, 0:n], in_=ps[:, 0:n], func=AF.Square), reads=[pk], writes=[sqk])
                        P.act(lambda e, ps=ps, q32=q32, n=n: e.activation(out=q32[:, 0:n], in_=ps[:, 0:n], func=AF.Copy), reads=[pk], writes=[q32k])
                        ps2, pk2 = nps()
                        P.pe(lambda e, ps2=ps2, sq=sq, n=n: e.matmul(ps2[:, 0:n], lhsT=blk1[:], rhs=sq[:, 0:n], start=True, stop=True),
                             reads=[sqk, "blk1"], writes=[pk2])
                        rs, rk = nt32()
                        P.act(lambda e, ps2=ps2, rs=rs, n=n: e.activation(out=rs[:, 0:n], in_=ps2[:, 0:n], func=AF.Ln, bias=EPS), reads=[pk2], writes=[rk])
                        P.act(lambda e, rs=rs, ps2=ps2, n=n: e.activation(out=ps2[:, 0:n], in_=rs[:, 0:n], func=AF.Exp, scale=-0.5), reads=[rk], writes=[pk2])
                        P.dve(lambda e, q32=q32, ps2=ps2, qc=qc, c0=c0, n=n, l=l: e.scalar_tensor_tensor(
                            out=S[:, 16 + qc, c0:c0 + n], in0=q32[:, 0:n], scalar=qkg[:, l:l + 1], in1=ps2[:, 0:n], op0=ALU.mult, op1=ALU.mult),
                            reads=[q32k, pk2, "qkg"], writes=qk(16 + qc, cti))
                wkv, wkvk = wneed(hf, l, "kv")
                for (c0, n, cti) in cts:
                    ps, pk = nps()
                    mm(ps[:, 0:n], [(wkv[:, kc * 256: kc * 256 + 128], hb[:, kc, c0:c0 + n]) for kc in range(8)], [wkvk] + hreads(cti), pk)
                    sq, sqk = ntb()
                    q32, q32k = nt32()
                    P.act(lambda e, ps=ps, sq=sq, n=n: e.activation(out=sq[:, 0:n], in_=ps[:, 0:n], func=AF.Square), reads=[pk], writes=[sqk])
                    P.act(lambda e, ps=ps, q32=q32, n=n: e.activation(out=q32[:, 0:n], in_=ps[:, 0:n], func=AF.Copy), reads=[pk], writes=[q32k])
                    ps2, pk2 = nps()
                    P.pe(lambda e, ps2=ps2, sq=sq, n=n: e.matmul(ps2[:, 0:n], lhsT=blk1[:], rhs=sq[:, 0:n], start=True, stop=True),
                         reads=[sqk, "blk1"], writes=[pk2])
                    rs, rk = nt32()
                    P.act(lambda e, ps2=ps2, rs=rs, n=n: e.activation(out=rs[:, 0:n], in_=ps2[:, 0:n], func=AF.Ln, bias=EPS), reads=[pk2], writes=[rk])
                    P.act(lambda e, rs=rs, ps2=ps2, n=n: e.activation(out=ps2[:, 0:n], in_=rs[:, 0:n], func=AF.Exp, scale=-0.5), reads=[rk], writes=[pk2])
                    kkeys = [("kT", "s")] if cti == 2 else [("kT", 4 * cti + b) for b in range(4)]
                    kn, knk = nt32()
                    P.dve(lambda e, q32=q32, ps2=ps2, kn=kn, n=n, l=l: e.scalar_tensor_tensor(
                        out=kn[:, 0:n], in0=q32[:, 0:n], scalar=qkg[:, NL + l:NL + l + 1], in1=ps2[:, 0:n], op0=ALU.mult, op1=ALU.mult),
                        reads=[q32k, pk2, "qkg"], writes=[knk])
                    P.act(lambda e, kn=kn, c0=c0, n=n: e.activation(out=kT[:, c0:c0 + n], in_=kn[:, 0:n], func=AF.Copy),
                          reads=[knk], writes=kkeys)
                    if hf == 0 and cti == 1 and nhalves > 1:
                        P.act(lambda e, kn=kn, l=l: e.activation(out=kprev[:, l, :], in_=kn[:, 384:512], func=AF.Copy), reads=[knk], writes=[("kprev", l)])
                    if hf == nhalves - 1 and cti == 1:
                        P.act(lambda e, kn=kn: e.activation(out=k32keep[:, 0:128], in_=kn[:, 384:512], func=AF.Copy), reads=[knk], writes=["k32l"])
                    if cti == 2:
                        P.act(lambda e, kn=kn: e.activation(out=k32keep[:, 128:128 + NSAMP], in_=kn[:, 0:NSAMP], func=AF.Copy), reads=[knk], writes=["k32s"])
                    if cti < 2:
                        ps, pk = nps()
                        for bi in range(4):
                            cb = c0 + bi * 128
                            mm(ps[:, bi * 128:(bi + 1) * 128], [(hb[:, kc, cb:cb + 128], wkv[:, kc * 256 + 128: kc * 256 + 256]) for kc in range(8)],
                               [wkvk] + hreads(cti), pk)
                        b0 = 4 * cti
                        P.dve(lambda e, ps=ps, b0=b0: e.tensor_copy(
                            out=vz[:, b0:b0 + 4, 0:64], in_=ps[:, :].rearrange("p (b g x) -> p b g x", b=4, g=2)[:, :, 0, :]),
                            reads=[pk], writes=[("vz", b0 + i) for i in range(4)])
                        P.dve(lambda e, ps=ps, b0=b0: e.tensor_copy(
                            out=vz[:, b0:b0 + 4, 128:192], in_=ps[:, :].rearrange("p (b g x) -> p b g x", b=4, g=2)[:, :, 1, :]),
                            reads=[pk] + [("vz", b0 + i) for i in range(4)], writes=[("vz", b0 + i) for i in range(4)])
                        if hf == nhalves - 1 and cti == 1:
                            P.dve(lambda e, ps=ps: e.tensor_copy(out=v32last[:], in_=ps[:, 384:512]), reads=[pk], writes=["v32last"])
                        if hf == 0 and cti == 1 and nhalves > 1:
                            P.act(lambda e, l=l: e.activation(out=vprev[:, l, :], in_=vz[:, 7, :], func=AF.Copy), reads=[("vz", 7)], writes=[("vprev", l)])
                    else:
                        ps, pk = nps()
                        mm(ps[0:NSAMP, 0:128], [(hb[:, kc, c0:c0 + n], wkv[:, kc * 256 + 128: kc * 256 + 256]) for kc in range(8)],
                           [wkvk] + hreads(cti), pk)
                        P.act(lambda e, ps=ps: e.activation(out=vnew32[:], in_=ps[0:NSAMP, 0:128], func=AF.Copy), reads=[pk], writes=["vnew32"])

                chk("qkv")
                def att_group(j, g, l=l):
                    gb = 8 * hf + j
                    qc0 = j * 128
                    rows = slice(g * 64, (g + 1) * 64)
                    kvsrc = []
                    if gb > 0:
                        if j > 0:
                            kvsrc.append((1, kT[rows, (j - 1) * 128: j * 128], ("kT", j - 1), vz[:, j - 1, g * 64: g * 64 + 128], ("vz", j - 1)))
                        else:
                            kvsrc.append((1, kprev[rows, l, :], ("kprev", l), vprev[:, l, g * 64: g * 64 + 128], ("vprev", l)))
                    kvsrc.append((0, kT[rows, j * 128:(j + 1) * 128], ("kT", j), vz[:, j, g * 64: g * 64 + 128], ("vz", j)))
                    qrhs = S[rows, 16:20, qc0:qc0 + 128]
                    qkeys = [("Sq", 16 + c, j, g) for c in range(4)]
                    st_ = {}

                    def stage_a():
                        pts = []
                        for (pc, kap, kkey, vap, vkey) in kvsrc:
                            ps, pk = nps()
                            P.pe(lambda e, ps=ps, kap=kap: e.matmul(ps[:, :].rearrange("p (c q) -> p c q", c=4), lhsT=kap, rhs=qrhs, start=True, stop=True),
                                 reads=[kkey] + qkeys, writes=[pk])
                            tt, tk = nt32()
                            P.dve(lambda e, ps=ps, tt=tt, pc=pc: e.scalar_tensor_tensor(
                                out=tt[:, :].rearrange("p (c q) -> p c q", c=4), in0=ps[:, :].rearrange("p (c q) -> p c q", c=4), scalar=0.125,
                                in1=bias[:, 4 * g:4 * g + 4, pc, :], op0=ALU.mult, op1=ALU.add),
                                reads=[pk, "bias"], writes=[tk])
                            pt, ptk = ntb()
                            P.act(lambda e, tt=tt, pt=pt: e.activation(out=pt[:, :], in_=tt[:, :], func=AF.Exp), reads=[tk], writes=[ptk])
                            pts.append((pt, ptk, vap, vkey))
                        st_["pts"] = pts

                    def stage_b1():
                        pts = st_["pts"]
                        npts = len(pts)
                        psn, pkn = nps()
                        psd, pkd = nps()
                        for i, (pt, ptk, vap, vkey) in enumerate(pts):
                            P.pe(lambda e, psn=psn, vap=vap, pt=pt, i=i: e.matmul(psn[:, :], lhsT=vap, rhs=pt[:, :], start=(i == 0), stop=(i == npts - 1)),
                                 reads=[ptk, vkey], writes=[pkn])
                        for i, (pt, ptk, vap, vkey) in enumerate(pts):
                            P.pe(lambda e, psd=psd, pt=pt, i=i: e.matmul(psd[:, :], lhsT=ones1[:], rhs=pt[:, :], start=(i == 0), stop=(i == npts - 1)),
                                 reads=[ptk, "ones1"], writes=[pkd])
                        dn, dnk = nt32()
                        P.dve(lambda e, psd=psd, dn=dn: e.tensor_tensor(
                            out=dn[rows, :].rearrange("p (c q) -> p c q", c=4), in0=psd[rows, :].rearrange("p (c q) -> p c q", c=4),
                            in1=sinkexp[rows, l * 8 + 4 * g: l * 8 + 4 * g + 4].unsqueeze(2).to_broadcast([64, 4, 128]), op=ALU.add),
                            reads=[pkd, "sinkexp"], writes=[dnk])
                        P.act(lambda e, dn=dn: e.activation(out=dn[rows, :], in_=dn[rows, :], func=AF.Ln), reads=[dnk], writes=[dnk])
                        P.act(lambda e, dn=dn: e.activation(out=dn[rows, :], in_=dn[rows, :], func=AF.Exp, scale=-1.0), reads=[dnk], writes=[dnk])
                        st_["b"] = (psn, pkn, dn, dnk)

                    def stage_b2():
                        psn, pkn, dn, dnk = st_["b"]
                        P.dve(lambda e, psn=psn, dn=dn: e.tensor_tensor(
                            out=S[rows, 16:20, qc0:qc0 + 128], in0=psn[rows, :].rearrange("p (c q) -> p c q", c=4),
                            in1=dn[rows, :].rearrange("p (c q) -> p c q", c=4), op=ALU.mult),
                            reads=[pkn, dnk], writes=qkeys)

                    return stage_a, stage_b1, stage_b2

                fillers = []
                for gi in range(2):
                    for (c0, n, cti) in cts:
                        for mi in range(4):
                            def fill(gi=gi, c0=c0, n=n, cti=cti, mi=mi, l=l):
                                w, wk = wneed(hf, l, "g%d" % gi, ahead=0)
                                mc = 4 * gi + mi
                                ps, pk = nps()
                                mm(ps[:, 0:n], [(w[:, kc * 512 + mi * 128: kc * 512 + (mi + 1) * 128], hb[:, kc, c0:c0 + n]) for kc in range(8)], [wk] + hreads(cti), pk)
                                if (mi + cti) % 2 == 0:
                                    P.dve(lambda e: e.tensor_copy(out=S[:, 8 + mc, c0:c0 + n], in_=ps[:, 0:n]), reads=[pk], writes=[sk(8 + mc, cti)])
                                else:
                                    P.act(lambda e: e.activation(out=S[:, 8 + mc, c0:c0 + n], in_=ps[:, 0:n], func=AF.Copy), reads=[pk], writes=[sk(8 + mc, cti)])
                            fillers.append(fill)
                if not has_s:
                    wneed(hf, l, "ga0", ahead=0)
                groups = [att_group(j, g) for j in range(8) for g in range(2)]
                ng = len(groups)
                nfill = [0]
                for i in range(ng + 2):
                    want = min(len(fillers), ((i + 1) * len(fillers) + ng - 1) // ng)
                    while nfill[0] < want:
                        fillers[nfill[0]]()
                        nfill[0] += 1
                    if i < ng:
                        groups[i][0]()
                    if 0 <= i - 1 < ng:
                        groups[i - 1][1]()
                    if 0 <= i - 2 < ng:
                        groups[i - 2][2]()
                chk("attn")
                if has_s:
                    sc0 = HALF
                    for s8 in range(2):
                        sg, sgk = nstage()
                        P.dma("sp", lambda e, sg=sg, s8=s8, l=l: e.dma_start(out=sg[:, :].rearrange("p (s x) -> p s x", s=8),
                                                                        in_=CK[l, s8 * 8:(s8 + 1) * 8].rearrange("s k x -> k s x")), writes=[sgk])
                        for (r0, r1) in ((1, 16), (16, 128)):
                            outs.append(P.dma("sp", lambda e, sg=sg, s8=s8, l=l, r0=r0, r1=r1: e.dma_start(
                                out=NKS[l, s8 * 8:(s8 + 1) * 8, r0 - 1:r1 - 1, :].rearrange("s k x -> k s x"),
                                in_=sg[r0:r1, :].rearrange("p (s x) -> p s x", s=8)), reads=[sgk], writes=[("nks", l, 0, s8, r0)]))
                        for si in range(8):
                            s = s8 * 8 + si
                            pst, pkt = nps()
                            P.pe(lambda e, pst=pst, sg=sg, si=si: e.transpose(pst[:, 0:128], sg[:, si * 128:(si + 1) * 128], ident[:, :]), reads=[sgk, "ident"], writes=[pkt])
                            P.act(lambda e, pst=pst, s=s: e.activation(out=KsT[:, s, 1:128], in_=pst[:, 1:128], func=AF.Copy), reads=[pkt], writes=[("KsT", s)])
                            P.act(lambda e, s=s: e.activation(out=KsT[:, s, 0:1], in_=kT[:, sc0 + s:sc0 + s + 1], func=AF.Copy), reads=[("kT", "s"), ("KsT", s)], writes=[("KsT", s)])
                    chk("sa1")
                    pssg = [nps(), nps()]
                    for s in range(NSAMP):
                        for g in range(2):
                            rows = slice(g * 64, (g + 1) * 64)
                            P.pe(lambda e, pq=pssg[g][0], s=s, g=g, rows=rows: e.matmul(pq[:, s * 4: s * 4 + 4], lhsT=KsT[rows, s, :],
                                                                                   rhs=S[rows, 16:20, sc0 + s], start=True, stop=True),
                                 reads=[("KsT", s)] + [("Sq", 16 + c, "s", g) for c in range(4)], writes=[pssg[g][1]])
                    chk("sa2")
                    for s8 in range(2):
                        sg, sgk = nstage()
                        P.dma("sp", lambda e, sg=sg, s8=s8, l=l: e.dma_start(out=sg[:, :].rearrange("p (s x) -> p s x", s=8),
                                                                        in_=CV[l, s8 * 8:(s8 + 1) * 8].rearrange("s k x -> k s x")), writes=[sgk])
                        for (r0, r1) in ((1, 16), (16, 128)):
                            outs.append(P.dma("sp", lambda e, sg=sg, s8=s8, l=l, r0=r0, r1=r1: e.dma_start(
                                out=NVS[l, s8 * 8:(s8 + 1) * 8, r0 - 1:r1 - 1, :].rearrange("s k x -> k s x"),
                                in_=sg[r0:r1, :].rearrange("p (s x) -> p s x", s=8)), reads=[sgk], writes=[("nvs", l, 0, s8, r0)]))
                        P.act(lambda e, sg=sg, s8=s8: e.activation(out=Vs[:, s8 * 8:(s8 + 1) * 8, 0:64], in_=sg[:, :].rearrange("p (s x) -> p s x", s=8)[:, :, 0:64], func=AF.Copy),
                              reads=[sgk, "Vs"], writes=["Vs"])
                        P.dve(lambda e, sg=sg, s8=s8: e.tensor_copy(out=Vs[:, s8 * 8:(s8 + 1) * 8, 128:192], in_=sg[:, :].rearrange("p (s x) -> p s x", s=8)[:, :, 64:128]),
                              reads=[sgk, "Vs"], writes=["Vs"])
                    outs.append(P.dma("sp", lambda e, l=l: e.dma_start(out=NVS[l, :, 127, :], in_=vnew32[:, :]), reads=["vnew32"], writes=[("nvs", l, 1)]))
                    for s4 in range(4):
                        psv, pkv = nps()
                        for si in range(4):
                            s = s4 * 4 + si
                            mm(psv[0:1, si * 128:(si + 1) * 128], [(hb[:, kc, sc0 + s:sc0 + s + 1], wkv[:, kc * 256 + 128: kc * 256 + 256]) for kc in range(8)],
                               [wkvk] + hreads(2), pkv)
                        P.act(lambda e, psv=psv, s4=s4: e.activation(out=Vs[0:1, s4 * 4:(s4 + 1) * 4, 0:64], in_=psv[0:1, :].rearrange("p (s x) -> p s x", s=4)[:, :, 0:64], func=AF.Copy),
                              reads=[pkv, "Vs"], writes=["Vs"])
                        P.act(lambda e, psv=psv, s4=s4: e.activation(out=Vs[0:1, s4 * 4:(s4 + 1) * 4, 128:192], in_=psv[0:1, :].rearrange("p (s x) -> p s x", s=4)[:, :, 64:128], func=AF.Copy),
                              reads=[pkv, "Vs"], writes=["Vs"])
                    wneed(hf, l, "ga0")
                    chk("sa3")
                    tt, tk = nt32()
                    for g in range(2):
                        P.dve(lambda e, pq=pssg[g][0], tt=tt, g=g: e.scalar_tensor_tensor(
                            out=tt[:, 0:128].rearrange("p (s h) -> p s h", s=NSAMP)[:, :, 4 * g:4 * g + 4], in0=pq[:, 0:64].rearrange("p (s c) -> p s c", s=NSAMP), scalar=0.125,
                            in1=biasS[:, 4 * g:4 * g + 4].unsqueeze(1).to_broadcast([128, NSAMP, 4]), op0=ALU.mult, op1=ALU.add),
                            reads=[pssg[g][1], "biasS"] + ([tk] if g else []), writes=[tk])
                    pt, ptk = ntb()
                    P.act(lambda e, tt=tt, pt=pt: e.activation(out=pt[:, 0:128], in_=tt[:, 0:128], func=AF.Exp), reads=[tk], writes=[ptk])
                    psd, pkd = nps()
                    P.pe(lambda e, psd=psd, pt=pt: e.matmul(psd[:, 0:128], lhsT=ones1[:], rhs=pt[:, 0:128], start=True, stop=True), reads=[ptk, "ones1"], writes=[pkd])
                    pso, pko = nps()
                    for s in range(NSAMP):
                        for g in range(2):
                            P.pe(lambda e, pso=pso, pt=pt, s=s, g=g: e.matmul(pso[:, s * 4:(s + 1) * 4], lhsT=Vs[:, s, g * 64: g * 64 + 128],
                                                                           rhs=pt[:, s * 8 + g * 4: s * 8 + g * 4 + 4], start=(g == 0), stop=(g == 1)),
                                 reads=[ptk, "Vs"], writes=[pko])
                    dn, dnk = nt32()
                    P.dve(lambda e, psd=psd, dn=dn, l=l: e.tensor_tensor(
                        out=dn[:, 0:128].rearrange("p (s h) -> p s h", s=NSAMP), in0=psd[:, 0:128].rearrange("p (s h) -> p s h", s=NSAMP),
                        in1=sinkexp[:, l * 8:(l + 1) * 8].unsqueeze(1).to_broadcast([128, NSAMP, 8]), op=ALU.add),
                        reads=[pkd, "sinkexp"], writes=[dnk])
                    P.act(lambda e, dn=dn: e.activation(out=dn[:, 0:128], in_=dn[:, 0:128], func=AF.Ln), reads=[dnk], writes=[dnk])
                    P.act(lambda e, dn=dn: e.activation(out=dn[:, 0:128], in_=dn[:, 0:128], func=AF.Exp, scale=-1.0), reads=[dnk], writes=[dnk])
                    for g in range(2):
                        rows = slice(g * 64, (g + 1) * 64)
                        P.dve(lambda e, pso=pso, dn=dn, rows=rows, g=g: e.tensor_tensor(
                            out=S[rows, 16:20, sc0:sc0 + NSAMP].rearrange("p c s -> p s c"),
                            in0=pso[rows, 0:64].rearrange("p (s c) -> p s c", c=4),
                            in1=dn[rows, 0:128].rearrange("p (s g c) -> p s g c", g=2, c=4)[:, :, g, :], op=ALU.mult),
                            reads=[pko, dnk], writes=[("Sq", 16 + c, "s", g) for c in range(4)])
                    chk("sa4")
                    pst, pkt = nps()
                    P.pe(lambda e, pst=pst: e.transpose(pst[0:NSAMP, 0:128], k32keep[:, 128:128 + NSAMP], ident[:, :]), reads=["k32s", "ident"], writes=[pkt])
                    tt, tk = nt32()
                    P.act(lambda e, pst=pst, tt=tt: e.activation(out=tt[0:NSAMP, 0:128], in_=pst[0:NSAMP, 0:128], func=AF.Copy), reads=[pkt], writes=[tk])
                    outs.append(P.dma("sp", lambda e, tt=tt, l=l: e.dma_start(out=NKS[l, :, 127, :], in_=tt[0:NSAMP, 0:128]), reads=[tk], writes=[("nks", l, 1)]))

                chk("sattn")
                if hf == nhalves - 1:
                    pst, pkt = nps()
                    P.pe(lambda e, pst=pst: e.transpose(pst[:, 0:128], k32keep[:, 0:128], ident[:, :]), reads=["k32l", "ident"], writes=[pkt])
                    tt, tk = nt32()
                    P.act(lambda e, pst=pst, tt=tt: e.activation(out=tt[:, 0:128], in_=pst[:, 0:128], func=AF.Copy), reads=[pkt], writes=[tk])
                    outs.append(P.dma("sp", lambda e, tt=tt, l=l: e.dma_start(out=NKP[l], in_=tt[:, 0:128]), reads=[tk], writes=[("nkp", l)]))
                    outs.append(P.dma("sp", lambda e, l=l: e.dma_start(out=NVP[l], in_=v32last[:, :]), reads=["v32last"], writes=[("nvp", l)]))

                chk("kvout")
                for i in range(2):
                    w, wk = wneed(hf, l, "ga%d" % i)
                    for (c0, n, cti) in cts:
                        for mi in range(4):
                            mc = 4 * i + mi
                            ps, pk = nps()
                            mm(ps[:, 0:n], [(w[:, kc * 512 + mi * 128: kc * 512 + (mi + 1) * 128], hb[:, kc, c0:c0 + n]) for kc in range(8)], [wk] + hreads(cti), pk)
                            P.act(lambda e, ps=ps, mc=mc, c0=c0, n=n: e.activation(out=S[:, mc, c0:c0 + n], in_=ps[:, 0:n], func=AF.Sigmoid),
                                  reads=[pk], writes=[sk(mc, cti)])
                w, wk = wneed(hf, l, "woa")
                for (c0, n, cti) in cts:
                    for mc in range(8):
                        ps, pk = nps()
                        rd = [wk]
                        for kc in range(4):
                            rd += qk(16 + kc, cti)
                        mm(ps[:, 0:n], [(w[:, kc * 1024 + mc * 128: kc * 1024 + (mc + 1) * 128], S[:, 16 + kc, c0:c0 + n]) for kc in range(4)], rd, pk)
                        P.dve(lambda e, ps=ps, mc=mc, c0=c0, n=n: e.tensor_tensor(out=S[:, mc, c0:c0 + n], in0=ps[:, 0:n], in1=S[:, mc, c0:c0 + n], op=ALU.mult),
                              reads=[pk, sk(mc, cti)], writes=[sk(mc, cti)])

                chk("oa")
                ntot = NCOL if has_s else HALF
                for mc in range(8):
                    P.act(lambda e, mc=mc, ntot=ntot: e.activation(out=S[:, 8 + mc, 0:ntot], in_=S[:, 8 + mc, 0:ntot], func=AF.Gelu_apprx_tanh),
                          reads=[sk(8 + mc, t_[2]) for t_ in cts], writes=[sk(8 + mc, t_[2]) for t_ in cts])
                chk("gelu")
                if has_s:
                    tm2fm(SCV[l].rearrange("s j d -> (s j) d"), 3 * NSAMP, lambda c: cbT[:, c, :], [("cbT", c) for c in range(8)])
                    tm2fm(SLH[l], NSAMP, lambda c: h0T[:, c, :], [("h0T", c) for c in range(8)])
                def lru_prep(c, l=l):
                    def run():
                        w, wk = wneed(hf, l, "r%d" % c, ahead=NBUF - 2)
                        for jt in range(4):
                            P.dve(lambda e, jt=jt: e.tensor_scalar(out=diag[:, c % 2, jt, :], in0=identb[:, :], scalar1=vecs[:, l, 3 + jt, c:c + 1], scalar2=None, op0=ALU.mult),
                                  reads=["identb", "vecs"], writes=[("diag", c % 2)])
                        xb = xrb[c % 2]
                        xbk = ("xrb", c % 2)
                        if hf == 0:
                            P.dve(lambda e: e.memset(xb[:, 0:3], 0.0), writes=[xbk])
                        else:
                            P.act(lambda e: e.activation(out=xb[:, 0:3], in_=xtail[:, l, c, :], func=AF.Copy), reads=[("xtail", l, c)], writes=[xbk])
                        for (c0, n, cti) in cts:
                            ps, pk = nps()
                            mm(ps[:, 0:n], [(w[:, kc * 128:(kc + 1) * 128], hb[:, kc, c0:c0 + n]) for kc in range(8)], [wk] + hreads(cti), pk)
                            P.dve(lambda e, ps=ps, c0=c0, n=n: e.tensor_copy(out=xb[:, 3 + c0:3 + c0 + n], in_=ps[:, 0:n]), reads=[pk, xbk], writes=[xbk])
                            if cti == 2:
                                P.dve(lambda e, ps=ps, n=n: e.tensor_copy(out=xr32s[:, c, :], in_=ps[:, 0:n]), reads=[pk], writes=[("xr32s", c)])
                            if hf == nhalves - 1 and cti == 1:
                                P.dve(lambda e, ps=ps: e.tensor_copy(out=xr32l[:, c, :], in_=ps[:, 509:512]), reads=[pk], writes=[("xr32l", c)])
                        if hf == 0 and nhalves > 1:
                            P.act(lambda e: e.activation(out=xtail[:, l, c, :], in_=xb[:, HALF:HALF + 3], func=AF.Copy), reads=[xbk], writes=[("xtail", l, c)])
                    return run

                def lru_step(i_, c, c0, n, cti, l=l):
                    w = wring[:, wpos[(hf, l, "r%d" % c)] % NBUF, :]
                    wk = ("w", wpos[(hf, l, "r%d" % c)] % NBUF)
                    xb = xrb[c % 2]
                    xbk = ("xrb", c % 2)
                    ra, rak = raR[i_ % 3], ("raR", i_ % 3)
                    ig, igk = igR[i_ % 3], ("igR", i_ % 3)
                    a2, a2k = a2R[i_ % 2], ("a2R", i_ % 2)
                    xcb, xcbk = xcbR[i_ % 3], ("xcbR", i_ % 3)

                    def a1():
                        ps, pk = nps()
                        if cti < 2:
                            pairs = [(diag[:, c % 2, jt, :], xb[:, c0 + jt:c0 + jt + n]) for jt in range(4)]
                            rd = [("diag", c % 2), xbk]
                        else:
                            pairs = [(diag[:, c % 2, jt, :], cbT[:, c, :].rearrange("p (s j) -> p s j", j=3)[:, :, jt]) for jt in range(3)] + [(diag[:, c % 2, 3, :], xb[:, 3 + c0:3 + c0 + n])]
                            rd = [("diag", c % 2), xbk, ("cbT", c)]
                        mm(ps[:, 0:n], pairs, rd, pk)
                        P.dve(lambda e: e.tensor_scalar(out=xcb[:, 0:n], in0=ps[:, 0:n], scalar1=vecs[:, l, 7, c:c + 1], scalar2=None, op0=ALU.add), reads=[pk, "vecs"], writes=[xcbk])

                    def a2_():
                        psr, pkr = nps()
                        psi, pki = nps()
                        P.pe(lambda e: e.matmul(psr[:, 0:n], lhsT=w[:, 1024:1152], rhs=xcb[:, 0:n], start=True, stop=True), reads=[wk, xcbk], writes=[pkr])
                        P.pe(lambda e: e.matmul(psi[:, 0:n], lhsT=w[:, 1152:1280], rhs=xcb[:, 0:n], start=True, stop=True), reads=[wk, xcbk], writes=[pki])
                        P.act(lambda e: e.activation(out=ra[:, 0:n], in_=psr[:, 0:n], func=AF.Exp, scale=-1.0, bias=hbias[:, l, 0, c:c + 1]), reads=[pkr, ("hbias", l)], writes=[rak])
                        P.act(lambda e: e.activation(out=ig[:, 0:n], in_=psi[:, 0:n], func=AF.Exp, scale=-1.0, bias=hbias[:, l, 1, c:c + 1]), reads=[pki, ("hbias", l)], writes=[igk])
                        P.act(lambda e: e.activation(out=ra[:, 0:n], in_=ra[:, 0:n], func=AF.Ln, bias=1.0), reads=[rak], writes=[rak])
                        P.act(lambda e: e.activation(out=ig[:, 0:n], in_=ig[:, 0:n], func=AF.Ln, bias=1.0), reads=[igk], writes=[igk])
                        P.act(lambda e: e.activation(out=ra[:, 0:n], in_=ra[:, 0:n], func=AF.Exp, scale=-1.0), reads=[rak], writes=[rak])
                        P.act(lambda e: e.activation(out=a2[:, 0:n], in_=ra[:, 0:n], func=AF.Exp, scale=cl[:, l, 1, c:c + 1]), reads=[rak, ("cl2", l)], writes=[a2k])
                        P.act(lambda e: e.activation(out=ra[:, 0:n], in_=ra[:, 0:n], func=AF.Exp, scale=cl[:, l, 0, c:c + 1]), reads=[rak, ("cl", l)], writes=[rak])
                        P.act(lambda e: e.activation(out=a2[:, 0:n], in_=a2[:, 0:n], func=AF.Ln, scale=-1.0, bias=1.0), reads=[a2k], writes=[a2k])

                    def b_():
                        P.dve(lambda e: e.scalar_tensor_tensor(out=ig[:, 0:n], in0=a2[:, 0:n], scalar=0.5, in1=ig[:, 0:n], op0=ALU.mult, op1=ALU.subtract),
                              reads=[igk, a2k], writes=[igk])
                        P.act(lambda e: e.activation(out=ig[:, 0:n], in_=ig[:, 0:n], func=AF.Exp), reads=[igk], writes=[igk])
                        P.dve(lambda e: e.tensor_tensor(out=ig[:, 0:n], in0=ig[:, 0:n], in1=xcb[:, 0:n], op=ALU.mult), reads=[igk, xcbk], writes=[igk])

                    def c_():
                        hq, hqk = nt32()
                        if cti < 2:
                            first = (hf == 0 and cti == 0)
                            P.dve(lambda e: e.tensor_tensor_scan(
                                out=hq[:, 0:n], data0=ra[:, 0:n], data1=ig[:, 0:n], initial=(0.0 if first else hlast[:, l, c:c + 1]), op0=ALU.mult, op1=ALU.add),
                                reads=[rak, igk] + ([] if first else [("hlast", l, c)]), writes=[hqk])
                            P.dve(lambda e: e.tensor_copy(out=hlast[:, l, c:c + 1], in_=hq[:, n - 1:n]), reads=[hqk], writes=[("hlast", l, c)])
                            src, srck = hq[:, 0:n], hqk
                        else:
                            P.dve(lambda e: e.tensor_tensor(out=hq[:, 0:n], in0=ra[:, 0:n], in1=h0T[:, c, :], op=ALU.mult), reads=[rak, ("h0T", c)], writes=[hqk])
                            P.dve(lambda e: e.tensor_tensor(out=hs32[:, c, :], in0=hq[:, 0:n], in1=ig[:, 0:n], op=ALU.add), reads=[hqk, igk], writes=[("hs32", c)])
                            src, srck = hs32[:, c, :], ("hs32", c)
                        P.pool(lambda e: e.tensor_tensor(out=S[:, 8 + c, c0:c0 + n], in0=src, in1=S[:, 8 + c, c0:c0 + n], op=ALU.mult),
                               reads=[srck, sk(8 + c, cti)], writes=[sk(8 + c, cti)])

                    return [a1, a2_, b_, c_]

                steps = []
                for c in range(8):
                    for k_, (c0, n, cti) in enumerate(cts):
                        stg = lru_step(len(steps), c, c0, n, cti)
                        steps.append([lru_prep(c) if k_ == 0 else None] + stg)
                nst = len(steps[0])
                for t in range(len(steps) + nst - 1):
                    for s_ in range(nst - 1, -1, -1):
                        i = t - s_
                        if 0 <= i < len(steps) and steps[i][s_] is not None:
                            steps[i][s_]()
                chk("lru")
                if has_s:
                    fm2tm(lambda c: hs32[:, c, :], [("hs32", c) for c in range(8)], NSAMP, NHS[l], ("nhs", l))
                    fm2tm(lambda c: xr32s[:, c, :], [("xr32s", c) for c in range(8)], NSAMP, NCS[l, :, 2, :], ("ncs", l, 1))
                    sg, sgk = nstage()
                    for j in range(2):
                        P.dma("sp", lambda e, sg=sg, l=l, j=j: e.dma_start(out=sg[j * NSAMP:(j + 1) * NSAMP, :], in_=SCV[l, :, 1 + j, :]), reads=[sgk] if j else [], writes=[sgk])
                    for j in range(2):
                        outs.append(P.dma("sp", lambda e, sg=sg, l=l, j=j: e.dma_start(out=NCS[l, :, j, :], in_=sg[j * NSAMP:(j + 1) * NSAMP, :]), reads=[sgk], writes=[("ncs", l, 0, j)]))
                if hf == nhalves - 1:
                    fm2tm(lambda c: xr32l[:, c, :], [("xr32l", c) for c in range(8)], 3, NCP[l], ("ncp", l))
                    pst, pkt = nps()
                    P.pe(lambda e, pst=pst, l=l: e.transpose(pst[0:8, 0:128], hlast[:, l, :], ident[:, :]), reads=[("hlast", l, c) for c in range(8)] + ["ident"], writes=[pkt])
                    tt, tk = nt32()
                    P.act(lambda e, pst=pst, tt=tt: e.activation(out=tt[0:8, 0:128], in_=pst[0:8, 0:128], func=AF.Copy), reads=[pkt], writes=[tk])
                    outs.append(P.dma("sp", lambda e, tt=tt, l=l: e.dma_start(out=NHP[l], in_=tt[0:8, 0:128]), reads=[tk], writes=[("nhp", l)]))

                chk("lruout")
                for i in range(4):
                    w, wk = wneed(hf, l, "ol%d" % i)
                    for (c0, n, cti) in cts:
                        for mi in range(2):
                            mc = 2 * i + mi
                            pso, pko = nps()
                            psg, pkg = nps()
                            mm(psg[:, 0:n], [(w[:, 2048 + kc * 256 + mi * 128: 2048 + kc * 256 + (mi + 1) * 128], hb[:, kc, c0:c0 + n]) for kc in range(8)], [wk] + hreads(cti), pkg)
                            mm(pso[:, 0:n], [(w[:, kc * 256 + mi * 128: kc * 256 + (mi + 1) * 128], S[:, 8 + kc, c0:c0 + n]) for kc in range(8)],
                               [wk] + [sk(8 + kc, cti) for kc in range(8)], pko)
                            sg_, sgk_ = nt32()
                            P.act(lambda e, psg=psg, sg_=sg_, n=n: e.activation(out=sg_[:, 0:n], in_=psg[:, 0:n], func=AF.Sigmoid), reads=[pkg], writes=[sgk_])
                            P.dve(lambda e, pso=pso, sg_=sg_, n=n: e.tensor_tensor(out=sg_[:, 0:n], in0=pso[:, 0:n], in1=sg_[:, 0:n], op=ALU.mult), reads=[pko, sgk_], writes=[sgk_])
                            P.pool(lambda e, sg_=sg_, mc=mc, c0=c0, n=n: e.tensor_tensor(out=S[:, mc, c0:c0 + n], in0=sg_[:, 0:n], in1=S[:, mc, c0:c0 + n], op=ALU.add),
                                   reads=[sgk_, sk(mc, cti)], writes=[sk(mc, cti)])
                chk("ol")
                for i in range(2):
                    w, wk = wneed(hf, l, "wo%d" % i)
                    for (c0, n, cti) in cts:
                        for mi in range(4):
                            mc = 4 * i + mi
                            ps, pk = nps()
                            mm(ps[:, 0:n], [(w[:, kc * 512 + mi * 128: kc * 512 + (mi + 1) * 128], S[:, kc, c0:c0 + n]) for kc in range(8)],
                               [wk] + [sk(kc, cti) for kc in range(8)], pk)
                            P.dve(lambda e, ps=ps, mc=mc, c0=c0, n=n: e.tensor_tensor(out=xres[:, mc, c0:c0 + n], in0=ps[:, 0:n], in1=xres[:, mc, c0:c0 + n], op=ALU.add),
                                  reads=[pk, xk(mc, cti)], writes=[xk(mc, cti)])
                chk("wo")
                rmsnorm(1)
                for i in range(11):
                    w, wk = wneed(hf, l, "ff%d" % i)
                    for (c0, n, cti) in cts:
                        for mi in range(2):
                            f = 2 * i + mi
                            psg, pkg = nps()
                            psu, pku = nps()
                            mm(psg[:, 0:n], [(w[:, kc * 256 + mi * 128: kc * 256 + (mi + 1) * 128], hb[:, kc, c0:c0 + n]) for kc in range(8)], [wk] + hreads(cti), pkg)
                            mm(psu[:, 0:n], [(w[:, 2048 + kc * 256 + mi * 128: 2048 + kc * 256 + (mi + 1) * 128], hb[:, kc, c0:c0 + n]) for kc in range(8)], [wk] + hreads(cti), pku)
                            sg_, sgk_ = nt32()
                            P.act(lambda e, psg=psg, sg_=sg_, n=n: e.activation(out=sg_[:, 0:n], in_=psg[:, 0:n], func=AF.Silu), reads=[pkg], writes=[sgk_])
                            wkeys = qk(f, cti) if 16 <= f < 20 else [sk(f, cti)]
                            P.dve(lambda e, psu=psu, sg_=sg_, f=f, c0=c0, n=n: e.tensor_tensor(out=S[:, f, c0:c0 + n], in0=psu[:, 0:n], in1=sg_[:, 0:n], op=ALU.mult),
                                  reads=[pku, sgk_], writes=wkeys)
                for m in range(8):
                    w, wk = wneed(hf, l, "dn%d" % m)
                    for (c0, n, cti) in cts:
                        ps, pk = nps()
                        rd = [wk]
                        for f in range(KFF):
                            rd += qk(f, cti) if 16 <= f < 20 else [sk(f, cti)]
                        mm(ps[:, 0:n], [(w[:, kc * 128:(kc + 1) * 128], S[:, kc, c0:c0 + n]) for kc in range(KFF)], rd, pk)
                        P.dve(lambda e, ps=ps, m=m, c0=c0, n=n: e.tensor_tensor(out=xres[:, m, c0:c0 + n], in0=ps[:, 0:n], in1=xres[:, m, c0:c0 + n], op=ALU.add),
                              reads=[pk, xk(m, cti)], writes=[xk(m, cti)])
                chk("ffn")
                rmsnorm(2)
                for b4 in range(2):
                    sg, sgk = nstage()
                    P.dma("sp", lambda e, sg=sg, b4=b4, l=l, t0=t0: e.dma_start(out=sg[:, :].rearrange("p (b x) -> p b x", b=4),
                                                                    in_=PP[l, t0 + b4 * 512: t0 + (b4 + 1) * 512, :].rearrange("(b p) x -> p b x", p=128)), writes=[sgk])
                    for kc in range(2):
                        ps, pk = nps()
                        for bi in range(4):
                            P.pe(lambda e, ps=ps, sg=sg, bi=bi, kc=kc: e.transpose(ps[:, bi * 128:(bi + 1) * 128], sg[:, bi * 256 + kc * 128: bi * 256 + (kc + 1) * 128], ident[:, :]),
                                 reads=[sgk, "ident"], writes=[pk])
                        P.act(lambda e, ps=ps, kc=kc, b4=b4: e.activation(out=pT[:, kc, b4 * 512:(b4 + 1) * 512], in_=ps[:, :], func=AF.Copy), reads=[pk], writes=[("pT", kc, b4), ("xrb", kc)])
                if has_s:
                    sg, sgk = nstage()
                    P.dma("sp", lambda e, sg=sg, l=l: e.dma_start(out=sg[0:NSAMP, 0:256], in_=PS_[l]), writes=[sgk])
                    ps, pk = nps()
                    for kc in range(2):
                        P.pe(lambda e, ps=ps, sg=sg, kc=kc: e.transpose(ps[:, kc * 128: kc * 128 + NSAMP], sg[0:NSAMP, kc * 128:(kc + 1) * 128], ident[0:NSAMP, 0:NSAMP]),
                             reads=[sgk, "ident"], writes=[pk])
                    for kc in range(2):
                        P.act(lambda e, ps=ps, kc=kc: e.activation(out=pT[:, kc, HALF:HALF + NSAMP], in_=ps[:, kc * 128: kc * 128 + NSAMP], func=AF.Copy), reads=[pk], writes=[("pT", kc, 2), ("xrb", kc)])
                for i in range(3):
                    w, wk = wneed(hf, l, "pl%d" % i)
                    c0_, c1_ = 3 * i, min(3 * i + 3, 8)
                    nch = c1_ - c0_
                    for (c0, n, cti) in cts:
                        for mi in range(nch):
                            mc = c0_ + mi
                            psg, pkg = nps()
                            psp, pkp = nps()
                            mm(psg[:, 0:n], [(w[:, kc * 128 * nch + mi * 128: kc * 128 * nch + (mi + 1) * 128], hb[:, kc, c0:c0 + n]) for kc in range(8)], [wk] + hreads(cti), pkg)
                            o2 = 8 * 128 * nch
                            mm(psp[:, 0:n], [(w[:, o2 + kc * 128 * nch + mi * 128: o2 + kc * 128 * nch + (mi + 1) * 128], pT[:, kc, c0:c0 + n]) for kc in range(2)],
                               [wk, ("pT", 0, cti), ("pT", 1, cti), ("xrb", 0), ("xrb", 1)], pkp)
                            sg_, sgk_ = nt32()
                            P.act(lambda e, psg=psg, sg_=sg_, n=n: e.activation(out=sg_[:, 0:n], in_=psg[:, 0:n], func=AF.Sigmoid), reads=[pkg], writes=[sgk_])
                            P.dve(lambda e, psp=psp, sg_=sg_, n=n: e.tensor_tensor(out=sg_[:, 0:n], in0=psp[:, 0:n], in1=sg_[:, 0:n], op=ALU.mult), reads=[pkp, sgk_], writes=[sgk_])
                            P.pool(lambda e, sg_=sg_, mc=mc, c0=c0, n=n: e.tensor_tensor(out=xres[:, mc, c0:c0 + n], in0=sg_[:, 0:n], in1=xres[:, mc, c0:c0 + n], op=ALU.add),
                                   reads=[sgk_, xk(mc, cti)], writes=[xk(mc, cti)])

            chk("ple")
            for b in range(8):
                cti = b // 4
                fm2tm(lambda c, b=b: xres[:, c, b * 128:(b + 1) * 128], [xk(c, cti) for c in range(8)], 128, YP[t0 + b * 128: t0 + (b + 1) * 128, :], ("yp", hf, b))
            if has_s:
                fm2tm(lambda c: xres[:, c, HALF:HALF + NSAMP], [xk(c, 2) for c in range(8)], NSAMP, YS[:, :], ("ys",))

        P.emit(final_wait_ops=[o for o in outs if o is not None])
    return nc


def _vec_layout(v):
    return np.ascontiguousarray(v.reshape(8, 128).T)


def make_in_maps(inp, nlayers=NL):
    f = lambda a: np.ascontiguousarray(np.asarray(a, dtype=np.float32))
    W = {k: f(inp[k]) for k in ["w_in", "w_o_attn", "w_a", "w_x", "w_o_lru", "w_out", "w_gate", "w_up", "w_down", "w_ple", "w_ple_gate"]}
    wall = np.zeros((NL, 128, TOTW), np.float32)
    for l in range(NL):
        for name, subs in _items(l, W):
            off, sz = ITEM_OFF[name]
            cat = np.concatenate(subs, axis=1)
            assert cat.shape == (128, sz), (name, cat.shape, sz)
            wall[l, :, off:off + sz] = cat
    vecs = np.zeros((128, NL, NV, 8), np.float32)
    cw = f(inp["conv_w"])
    srcs = {"ln1": f(inp["ln1"]), "ln2": f(inp["ln2"]), "ln3": f(inp["ln3"]), "cw0": cw[:, 0], "cw1": cw[:, 1], "cw2": cw[:, 2], "cw3": cw[:, 3],
            "conv_b": f(inp["conv_b"]), "b_a": f(inp["b_a"]), "b_x": f(inp["b_x"]), "lam": f(inp["lam"])}
    for vi, vn in enumerate(VEC_NAMES):
        for l in range(NL):
            vecs[:, l, vi, :] = _vec_layout(srcs[vn][l])
    vecs = vecs.reshape(128, NL * NV * 8)
    qg, kg = f(inp["q_gain"]), f(inp["k_gain"])
    qkg = np.zeros((128, 2 * NL), np.float32)
    for l in range(NL):
        qkg[:, l] = np.tile(qg[l], 2)
        qkg[:, NL + l] = np.tile(kg[l], 2)
    sinks = f(inp["sinks"]).reshape(1, NL * 8)
    t5 = f(inp["t5_table"])
    oh = _t5_onehot()
    ident = np.eye(128, dtype=np.float32)
    xp, xs = f(inp["x_prompt"]), f(inp["x_sample"])
    pp, psm = f(inp["p_prompt"]), f(inp["p_sample"])
    ck, cv = f(inp["cache_k_win"]), f(inp["cache_v_win"])
    slh, scv = f(inp["state_lru_h"]), f(inp["state_conv"])
    maps = []
    for c in range(NCORES):
        s0, s1 = c * NSAMP, (c + 1) * NSAMP
        maps.append(dict(
            xp=xp[c], xs=np.ascontiguousarray(xs[s0:s1, 0, :]), pp=np.ascontiguousarray(pp[:, c]), psm=np.ascontiguousarray(psm[:, s0:s1, 0, :]),
            ck=np.ascontiguousarray(ck[:, s0:s1].reshape(NL, NSAMP, 128, 128)), cv=np.ascontiguousarray(cv[:, s0:s1].reshape(NL, NSAMP, 128, 128)),
            slh=np.ascontiguousarray(slh[:, s0:s1]), scv=np.ascontiguousarray(scv[:, s0:s1]),
            wall=wall, vecs=vecs, qkg=qkg, sinks=sinks, t5=t5, oh=oh, ident=ident))
    return maps


def gather(results):
    r = results
    yp = np.stack([np.asarray(r[c]["yp"]) for c in range(NCORES)], 0)
    ys = np.concatenate([np.asarray(r[c]["ys"]) for c in range(NCORES)], 0)[:, None, :]
    nkp = np.stack([np.asarray(r[c]["nkp"]).reshape(NL, 128, 2, 64) for c in range(NCORES)], 1)
    nvp = np.stack([np.asarray(r[c]["nvp"]).reshape(NL, 128, 2, 64) for c in range(NCORES)], 1)
    nhp = np.stack([np.asarray(r[c]["nhp"]).reshape(NL, D) for c in range(NCORES)], 1)
    ncp = np.stack([np.asarray(r[c]["ncp"]) for c in range(NCORES)], 1)
    nks = np.concatenate([np.asarray(r[c]["nks"]).reshape(NL, NSAMP, 128, 2, 64) for c in range(NCORES)], 1)
    nvs = np.concatenate([np.asarray(r[c]["nvs"]).reshape(NL, NSAMP, 128, 2, 64) for c in range(NCORES)], 1)
    nhs = np.concatenate([np.asarray(r[c]["nhs"]) for c in range(NCORES)], 1)
    ncs = np.concatenate([np.asarray(r[c]["ncs"]) for c in range(NCORES)], 1)
    return tuple(np.ascontiguousarray(a, dtype=np.float32) for a in (yp, ys, nkp, nvp, nhp, ncp, nks, nvs, nhs, ncs))


def kernel(**inputs):
    nc = build()
    maps = make_in_maps(inputs)
    res = run_bass_kernel_spmd(nc, maps, core_ids=list(range(NCORES)))
    return gather(res.results)
```

```python
import math
from contextlib import ExitStack

import numpy as np
import concourse.bass as bass
import concourse.mybir as mybir
from concourse.bass_utils import run_bass_kernel_spmd

F32 = mybir.dt.float32
BF16 = mybir.dt.bfloat16
AF = mybir.ActivationFunctionType
ALU = mybir.AluOpType

D = 1024
NL = 4
SEQ = 2048
HALF = 1024
NSAMP = 16
NCOL = HALF + NSAMP
DFF = 2816
KFF = DFF // 128
EPS = 1e-6
NEG = -30000.0
NCORES = 8
NBUF = 3
SLOT = 4096


class _Op:
    __slots__ = ("eng", "fn", "deps", "dma", "sem", "sigval", "prevval", "needed")

    def __init__(self, eng, fn, deps, dma):
        self.eng = eng
        self.fn = fn
        self.deps = deps
        self.dma = dma
        self.sem = None
        self.sigval = None
        self.prevval = 0
        self.needed = False


class Prog:
    ENGS = ("pe", "act", "dve", "pool", "sp")
    NSLOT = {"sp": 8, "pool": 2}

    def __init__(self, nc):
        self.nc = nc
        self.ops = []
        self.last_writer = {}
        self.readers = {}
        self.stopped = False

    def op(self, eng, fn, reads=(), writes=(), dma=False):
        if self.stopped:
            return None
        idx = len(self.ops)
        deps = set()
        for k in reads:
            w = self.last_writer.get(k)
            if w is not None:
                deps.add(w)
        for k in writes:
            w = self.last_writer.get(k)
            if w is not None:
                deps.add(w)
            rs = self.readers.get(k)
            if rs:
                deps.update(rs)
        for k in reads:
            self.readers.setdefault(k, []).append(idx)
        for k in writes:
            self.last_writer[k] = idx
            self.readers[k] = []
        deps.discard(idx)
        self.ops.append(_Op(eng, fn, deps, dma))
        return idx

    def pe(self, fn, reads=(), writes=()):
        return self.op("pe", fn, reads, writes)

    def act(self, fn, reads=(), writes=()):
        return self.op("act", fn, reads, writes)

    def dve(self, fn, reads=(), writes=()):
        return self.op("dve", fn, reads, writes)

    def pool(self, fn, reads=(), writes=()):
        return self.op("pool", fn, reads, writes)

    def dma(self, q, fn, reads=(), writes=()):
        return self.op(q, fn, reads, writes, dma=True)

    def emit(self, final_wait_ops=()):
        nc = self.nc
        ops = self.ops
        for o in ops:
            for d in o.deps:
                ops[d].needed = True
        for i in final_wait_ops:
            ops[i].needed = True
        with ExitStack() as st:
            csem = {e: st.enter_context(nc.semaphore("c_" + e)) for e in ("pe", "act", "dve", "pool")}
            slots = {
                q: [st.enter_context(nc.semaphore("d_%s_%d" % (q, i))) for i in range(n)]
                for q, n in self.NSLOT.items()
            }
            cnt = {e: 0 for e in csem}
            nd = {q: 0 for q in slots}
            for o in ops:
                if o.dma:
                    j = nd[o.eng]
                    ns = len(slots[o.eng])
                    o.sem = slots[o.eng][j % ns]
                    o.sigval = 16 * (j // ns + 1)
                    o.prevval = 16 * (j // ns)
                    nd[o.eng] = j + 1
                elif o.needed:
                    cnt[o.eng] += 1
                    o.sem = csem[o.eng]
                    o.sigval = cnt[o.eng]
            per_eng = {e: [] for e in self.ENGS}
            for i, o in enumerate(ops):
                per_eng[o.eng].append(i)
            block = st.enter_context(nc.Block())
            final = list(final_wait_ops)

            def gen(ename, eng):
                waited = {}

                def do_wait(sem, val):
                    key = id(sem)
                    if waited.get(key, 0) >= val:
                        return
                    waited[key] = val
                    eng.wait_ge(sem, val)

                for i in per_eng[ename]:
                    o = ops[i]
                    w = {}
                    for d in o.deps:
                        do_ = ops[d]
                        if (not do_.dma) and do_.eng == ename and ename == "pe":
                            continue
                        key = id(do_.sem)
                        if key not in w or w[key][1] < do_.sigval:
                            w[key] = (do_.sem, do_.sigval)
                    if o.dma and o.prevval > 0:
                        key = id(o.sem)
                        if key not in w or w[key][1] < o.prevval:
                            w[key] = (o.sem, o.prevval)
                    for sem, val in w.values():
                        do_wait(sem, val)
                    ins = o.fn(eng)
                    if o.dma:
                        ins.then_inc(o.sem, 16)
                    elif o.needed:
                        ins.then_inc(o.sem, 1)
                if ename == "sp":
                    for i in final:
                        do_wait(ops[i].sem, ops[i].sigval)

            @block.tensor
            def _(e):
                gen("pe", e)

            @block.scalar
            def _(e):
                gen("act", e)

            @block.vector
            def _(e):
                gen("dve", e)

            @block.gpsimd
            def _(e):
                gen("pool", e)

            @block.sync
            def _(e):
                gen("sp", e)


QPERM = np.concatenate([np.concatenate([np.arange(c * 64, c * 64 + 64), np.arange((4 + c) * 64, (4 + c) * 64 + 64)]) for c in range(4)])

VEC_NAMES = ["ln1", "ln2", "ln3", "cw0", "cw1", "cw2", "cw3", "conv_b", "b_a", "b_x", "lam"]
NV = len(VEC_NAMES)


def _pack(w):
    k, n = w.shape
    kc = k // 128
    return np.ascontiguousarray(w.reshape(kc, 128, n).transpose(1, 0, 2).reshape(128, kc * n))


def _items(l, W):
    win = W["w_in"][l]
    its = []
    its.append(("q", [_pack(win[:, 0:512][:, QPERM])]))
    its.append(("kv", [_pack(win[:, 512:768])]))
    for i in range(2):
        its.append(("ga%d" % i, [_pack(win[:, 2816 + 512 * i: 2816 + 512 * (i + 1)])]))
    its.append(("woa", [_pack(W["w_o_attn"][l][QPERM, :])]))
    for i in range(2):
        its.append(("g%d" % i, [_pack(win[:, 1792 + 512 * i: 1792 + 512 * (i + 1)])]))
    for c in range(8):
        its.append(("r%d" % c, [_pack(win[:, 768 + 128 * c: 768 + 128 * (c + 1)]), W["w_a"][l][c], W["w_x"][l][c]]))
    for i in range(4):
        its.append(("ol%d" % i, [_pack(W["w_o_lru"][l][:, 256 * i: 256 * (i + 1)]), _pack(win[:, 3840 + 256 * i: 3840 + 256 * (i + 1)])]))
    for i in range(2):
        its.append(("wo%d" % i, [_pack(W["w_out"][l][:, 512 * i: 512 * (i + 1)])]))
    for i in range(11):
        its.append(("ff%d" % i, [_pack(W["w_gate"][l][:, 256 * i: 256 * (i + 1)]), _pack(W["w_up"][l][:, 256 * i: 256 * (i + 1)])]))
    for m in range(8):
        its.append(("dn%d" % m, [_pack(W["w_down"][l][:, 128 * m: 128 * (m + 1)])]))
    for i in range(3):
        c0, c1 = 3 * i, min(3 * i + 3, 8)
        its.append(("pl%d" % i, [_pack(W["w_ple_gate"][l][:, 128 * c0: 128 * c1]), _pack(W["w_ple"][l][:, 128 * c0: 128 * c1])]))
    return its


def _item_sizes():
    sz = [("q", 4096), ("kv", 2048), ("g0", 4096), ("g1", 4096), ("ga0", 4096), ("ga1", 4096), ("woa", 4096)]
    sz += [("r%d" % c, 1280) for c in range(8)]
    sz += [("ol%d" % i, 4096) for i in range(4)]
    sz += [("wo%d" % i, 4096) for i in range(2)]
    sz += [("ff%d" % i, 4096) for i in range(11)]
    sz += [("dn%d" % m, 2816) for m in range(8)]
    sz += [("pl%d" % i, 1280 * (min(3 * i + 3, 8) - 3 * i)) for i in range(3)]
    return sz


ITEM_SIZES = _item_sizes()
ITEM_OFF = {}
_o = 0
for _n, _s in ITEM_SIZES:
    ITEM_OFF[_n] = (_o, _s)
    _o += _s
TOTW = _o
ITEM_ORDER = [n for n, _ in ITEM_SIZES]


def _t5_onehot():
    oh = np.zeros((33, 384), np.float32)
    for j in range(384):
        d = j - 128
        if 0 <= d < 128:
            if d < 16:
                b = d
            else:
                nf = np.float32(max(d, 1))
                v = np.log(nf / np.float32(16)) / np.float32(math.log(128 / 16)) * np.float32(16)
                b = min(16 + int(np.float32(v)), 31)
            oh[b, j] = 1.0
        else:
            oh[32, j] = 1.0
    return oh


class _Stop(Exception):
    pass


def build(nlayers=NL, nhalves=2, stop=None):
    nc = bass.Bass("TRN2", target_bir_lowering=False)

    def din(name, shape):
        return nc.dram_tensor(name, list(shape), F32, kind="ExternalInput")

    def dout(name, shape):
        return nc.dram_tensor(name, list(shape), F32, kind="ExternalOutput")

    XP = din("xp", [SEQ, D]).ap()
    XS = din("xs", [NSAMP, D]).ap()
    PP = din("pp", [NL, SEQ, 256]).ap()
    PS_ = din("psm", [NL, NSAMP, 256]).ap()
    CK_t = din("ck", [NL, NSAMP, 128, 128])
    CV_t = din("cv", [NL, NSAMP, 128, 128])
    CK, CV = CK_t.ap(), CV_t.ap()
    SLH = din("slh", [NL, NSAMP, D]).ap()
    SCV_t = din("scv", [NL, NSAMP, 3, D])
    SCV = SCV_t.ap()
    WALL = din("wall", [NL, 128, TOTW]).ap()
    VECS = din("vecs", [128, NL * NV * 8]).ap()
    QKG = din("qkg", [128, 2 * NL]).ap()
    SINK_t = din("sinks", [1, NL * 8])
    T5 = din("t5", [32, 8]).ap()
    OH = din("oh", [33, 384]).ap()
    IDN = din("ident", [128, 128]).ap()
    scr_t = nc.dram_tensor("scr", [8, 128, 384], F32, kind="Internal")

    YP = dout("yp", [SEQ, D]).ap()
    YS = dout("ys", [NSAMP, D]).ap()
    NKP = dout("nkp", [NL, 128, 128]).ap()
    NVP = dout("nvp", [NL, 128, 128]).ap()
    NHP = dout("nhp", [NL, 8, 128]).ap()
    NCP = dout("ncp", [NL, 3, D]).ap()
    NKS_t = dout("nks", [NL, NSAMP, 128, 128])
    NVS_t = dout("nvs", [NL, NSAMP, 128, 128])
    NKS, NVS = NKS_t.ap(), NVS_t.ap()
    NHS = dout("nhs", [NL, NSAMP, D]).ap()
    NCS_t = dout("ncs", [NL, NSAMP, 3, D])
    NCS = NCS_t.ap()

    with ExitStack() as st:
        def sb(name, shape, dt):
            return st.enter_context(nc.sbuf_tensor(name, list(shape), dt))

        xres = sb("xres", [128, 8, NCOL], F32)
        hb = sb("hb", [128, 8, NCOL], BF16)
        S = sb("S", [128, 22, NCOL], BF16)
        kT = sb("kT", [128, NCOL], BF16)
        k32keep = sb("k32keep", [128, 128 + NSAMP], F32)
        vz = sb("vz", [128, 8, 192], BF16)
        kprev = sb("kprev", [128, NL, 128], BF16)
        vprev = sb("vprev", [128, NL, 192], BF16)
        v32last = sb("v32last", [128, 128], F32)
        vnew32 = sb("vnew32", [NSAMP, 128], F32)
        bias = sb("bias", [128, 8, 2, 128], F32)
        biasS = sb("biasS", [128, 8], F32)
        wring = sb("wring", [128, NBUF, SLOT], BF16)
        vecs = sb("vecs_sb", [128, NL, NV, 8], F32)
        hbias = sb("hbias", [128, NL, 2, 8], F32)
        cl = sb("cl", [128, NL, 2, 8], F32)
        qkg = sb("qkg_sb", [128, 2 * NL], F32)
        sinkexp = sb("sinkexp", [128, NL * 8], F32)
        ident = sb("ident_sb", [128, 128], F32)
        identb = sb("identb", [128, 128], BF16)
        onesn = sb("onesn", [128, 128], BF16)
        ones1 = sb("ones1", [128, 128], BF16)
        blk1 = sb("blk1", [128, 128], BF16)
        diag = sb("diag", [128, 2, 4, 128], BF16)
        xrb2 = sb("xrb2", [128, 2, 4 + NCOL], BF16)
        xrb = [xrb2[:, i, 0:3 + NCOL] for i in range(2)]
        pT = xrb2[:, :, 0:NCOL]
        xr32s = sb("xr32s", [128, 8, NSAMP], F32)
        xr32l = sb("xr32l", [128, 8, 3], F32)
        xtail = sb("xtail", [128, NL, 8, 3], BF16)
        hlast = sb("hlast", [128, NL, 8], F32)
        hs32 = sb("hs32", [128, 8, NSAMP], F32)
        h0T = sb("h0T", [128, 8, NSAMP], F32)
        cbT = sb("cbT", [128, 8, 3 * NSAMP], BF16)
        KsT = sb("KsT", [128, NSAMP, 128], BF16)
        Vs = sb("Vs", [128, NSAMP, 192], BF16)
        t5sb = sb("t5sb", [33, 8], F32)
        stage = [sb("stage%d" % i, [128, 1024], F32) for i in range(2)]
        t32 = [sb("t32_%d" % i, [128, 512], F32) for i in range(5)]
        raR = [sb("raR%d" % i, [128, 512], F32) for i in range(3)]
        igR = [sb("igR%d" % i, [128, 512], F32) for i in range(3)]
        a2R = [sb("a2R%d" % i, [128, 512], F32) for i in range(2)]
        xcbR = [sb("xcbR%d" % i, [128, 512], BF16) for i in range(3)]
        tb16 = [sb("tb16_%d" % i, [128, 512], BF16) for i in range(6)]
        psum = [st.enter_context(nc.psum_tensor("ps%d" % i, [128, 512], F32)) for i in range(8)]
        ohsb = t32[0][0:33, 0:384]
        ones33 = t32[1][0:33, 0:128]
        L33 = t32[2][0:33, 0:128]

        P = Prog(nc)
        ctr = {"ps": 0, "t32": 0, "tb": 0, "st": 0, "w": 0}

        def nps():
            b = ctr["ps"] % 8
            ctr["ps"] += 1
            return psum[b], ("ps", b)

        def nt32():
            i = ctr["t32"] % len(t32)
            ctr["t32"] += 1
            return t32[i], ("t32", i)

        def ntb():
            i = ctr["tb"] % len(tb16)
            ctr["tb"] += 1
            return tb16[i], ("tb", i)

        def nstage():
            i = ctr["st"] % len(stage)
            ctr["st"] += 1
            return stage[i], ("stage", i)

        outs = []

        def chk(name):
            if stop == name:
                P.stopped = True

        wseq = [(hf, l, n) for hf in range(nhalves) for l in range(nlayers) for n in ITEM_ORDER]
        wpos = {k: i for i, k in enumerate(wseq)}
        wissued = [0]

        def wneed(hf, l, name, ahead=NBUF - 1):
            i = wpos[(hf, l, name)]
            upto = min(len(wseq), i + 1 + ahead)
            while wissued[0] < upto:
                j = wissued[0]
                _, l2, n2 = wseq[j]
                off, sz = ITEM_OFF[n2]
                slot = j % NBUF
                P.dma("pool", lambda e, l2=l2, off=off, sz=sz, slot=slot: e.dma_start(out=wring[:, slot, 0:sz], in_=WALL[l2, :, off:off + sz]),
                      writes=[("w", slot)])
                wissued[0] += 1
            return wring[:, i % NBUF, :], ("w", i % NBUF)

        def mm(ps_ap, pairs, reads, pskey):
            n = len(pairs)
            for i, (l_, r_) in enumerate(pairs):
                P.pe(lambda e, l_=l_, r_=r_, i=i: e.matmul(ps_ap, lhsT=l_, rhs=r_, start=(i == 0), stop=(i == n - 1)),
                     reads=reads, writes=[pskey])

        def tm2fm(src_rows_ap, nrows, dst_fn, dst_keys, extra_reads=()):
            sg, sgk = nstage()
            P.dma("sp", lambda e: e.dma_start(out=sg[0:nrows, :], in_=src_rows_ap), reads=list(extra_reads), writes=[sgk])
            for half8 in range(2):
                ps, pk = nps()
                for cc in range(4):
                    c = half8 * 4 + cc
                    P.pe(lambda e, c=c, cc=cc, ps=ps: e.transpose(ps[:, cc * 128: cc * 128 + nrows], sg[0:nrows, c * 128:(c + 1) * 128], ident[0:nrows, 0:nrows]),
                         reads=[sgk, "ident"], writes=[pk])
                for cc in range(4):
                    c = half8 * 4 + cc
                    P.act(lambda e, c=c, cc=cc, ps=ps: e.activation(out=dst_fn(c), in_=ps[:, cc * 128: cc * 128 + nrows], func=AF.Copy),
                          reads=[pk], writes=[dst_keys[c]])

        def fm2tm(src_fn, src_keys, ncols, dst_ap, dst_key):
            sg, sgk = nstage()
            for half8 in range(2):
                ps, pk = nps()
                for cc in range(4):
                    c = half8 * 4 + cc
                    P.pe(lambda e, c=c, cc=cc, ps=ps: e.transpose(ps[0:ncols, cc * 128:(cc + 1) * 128], src_fn(c), ident[:, :]),
                         reads=[src_keys[c], "ident"], writes=[pk])
                P.act(lambda e, half8=half8, ps=ps: e.activation(out=sg[0:ncols, half8 * 512:(half8 + 1) * 512], in_=ps[0:ncols, :], func=AF.Copy),
                      reads=[pk], writes=[sgk])
            o = P.dma("sp", lambda e: e.dma_start(out=dst_ap, in_=sg[0:ncols, :]), reads=[sgk], writes=[dst_key])
            outs.append(o)

        P.dma("sp", lambda e: e.dma_start(out=ident[:], in_=IDN), writes=["ident"])
        P.dma("sp", lambda e: e.dma_start(out=vecs[:].rearrange("p l v c -> p (l v c)"), in_=VECS), writes=["vecs"])
        P.dma("sp", lambda e: e.dma_start(out=qkg[:], in_=QKG), writes=["qkg"])
        P.dma("sp", lambda e: e.dma_start(out=sinkexp[:], in_=bass.AP(SINK_t, 0, [[0, 128], [1, NL * 8]])), writes=["sinkexp"])
        P.dma("sp", lambda e: e.dma_start(out=t5sb[0:32, :], in_=T5), writes=["t5a"])
        P.dma("sp", lambda e: e.dma_start(out=ohsb, in_=OH), writes=[("t32", 0)])
        P.dve(lambda e: e.memset(t5sb[32:33, :], NEG), writes=["t5b"])
        P.dve(lambda e: e.memset(ones33, 1.0), writes=[("t32", 1)])
        P.dve(lambda e: e.memset(onesn[:], 1.0 / D), writes=["onesn"])
        P.dve(lambda e: e.memset(ones1[:], 1.0), writes=["ones1"])
        P.dve(lambda e: e.memset(blk1[:], 0.0), writes=["blk1"])
        P.dve(lambda e: e.memset(blk1[0:64, 0:64], 1.0 / 64), reads=["blk1"], writes=["blk1"])
        P.dve(lambda e: e.memset(blk1[64:128, 64:128], 1.0 / 64), reads=["blk1"], writes=["blk1"])
        P.dve(lambda e: e.memset(vz[:], 0.0), writes=[("vz", b) for b in range(8)])
        P.dve(lambda e: e.memset(Vs[:], 0.0), writes=["Vs"])
        P.dve(lambda e: e.memset(vprev[:], 0.0), writes=[("vprev", l) for l in range(NL)])
        P.dve(lambda e: e.tensor_copy(out=identb[:], in_=ident[:]), reads=["ident"], writes=["identb"])
        P.act(lambda e: e.activation(out=sinkexp[:], in_=sinkexp[:], func=AF.Exp), reads=["sinkexp"], writes=["sinkexp"])
        for l in range(nlayers):
            lam_ap = vecs[:, l, 10, :]
            P.act(lambda e, l=l, lam_ap=lam_ap: e.activation(out=cl[:, l, 0, :], in_=lam_ap, func=AF.Exp, scale=-1.0), reads=["vecs"], writes=[("cl", l)])
            P.act(lambda e, l=l: e.activation(out=cl[:, l, 0, :], in_=cl[:, l, 0, :], func=AF.Ln, bias=1.0), reads=[("cl", l)], writes=[("cl", l)])
            P.dve(lambda e, l=l: e.tensor_scalar(out=cl[:, l, 1, :], in0=cl[:, l, 0, :], scalar1=-16.0, scalar2=None, op0=ALU.mult), reads=[("cl", l)], writes=[("cl2", l)])
            P.dve(lambda e, l=l: e.tensor_scalar(out=cl[:, l, 0, :], in0=cl[:, l, 0, :], scalar1=-8.0, scalar2=None, op0=ALU.mult), reads=[("cl", l), ("cl2", l)], writes=[("cl", l)])
            P.dve(lambda e, l=l: e.tensor_scalar(out=hbias[:, l, 0, :], in0=vecs[:, l, 8, :], scalar1=-1.0, scalar2=None, op0=ALU.mult), reads=["vecs"], writes=[("hbias", l)])
            P.dve(lambda e, l=l: e.tensor_scalar(out=hbias[:, l, 1, :], in0=vecs[:, l, 9, :], scalar1=-1.0, scalar2=None, op0=ALU.mult), reads=["vecs", ("hbias", l)], writes=[("hbias", l)])
        for hd in range(8):
            P.dve(lambda e, hd=hd: e.tensor_scalar(out=L33, in0=ones33, scalar1=t5sb[:, hd:hd + 1], scalar2=None, op0=ALU.mult),
                  reads=["t5a", "t5b", ("t32", 1)], writes=[("t32", 2)])
            ps, pk = nps()
            P.pe(lambda e, ps=ps: e.matmul(ps[:, 0:384], lhsT=L33, rhs=ohsb, start=True, stop=True), reads=[("t32", 2), ("t32", 0)], writes=[pk])
            tt, tk = t32[3 + hd % 2], ("t32", 3 + hd % 2)
            P.dve(lambda e, ps=ps, tt=tt: e.tensor_copy(out=tt[:, 0:384], in_=ps[:, 0:384]), reads=[pk], writes=[tk])
            P.dma("sp", lambda e, hd=hd, tt=tt: e.dma_start(out=scr_t.ap()[hd], in_=tt[:, 0:384]), reads=[tk], writes=[("scr", hd)])
            P.dma("sp", lambda e, hd=hd: e.dma_start(out=bias[:, hd, :, :], in_=bass.AP(scr_t, hd * 128 * 384 + 128, [[383, 128], [128, 2], [1, 128]])),
                  reads=[("scr", hd)], writes=["bias"])
        P.dve(lambda e: e.tensor_copy(out=biasS[:, :], in_=bias[:, :, 1, 0]), reads=["bias"], writes=["biasS"])
        P.dve(lambda e: e.tensor_copy(out=biasS[0:1, :], in_=bias[0:1, :, 0, 0]), reads=["bias", "biasS"], writes=["biasS"])

        chk("setup")
        for hf in range(nhalves):
            t0 = hf * HALF
            cts = [(0, 512, 0), (512, 512, 1)]
            if hf == 0:
                cts.append((HALF, NSAMP, 2))
            has_s = hf == 0

            def xk(c, cti):
                return ("x", c, cti)

            def hk(c, cti):
                return ("h", c, cti)

            def sk(slot, cti):
                return ("S", slot, cti)

            def qk(slot, cti):
                if cti == 2:
                    return [("Sq", slot, "s", g) for g in range(2)]
                return [("Sq", slot, 4 * cti + b, g) for b in range(4) for g in range(2)]

            for b in range(8):
                cti = b // 4
                tm2fm(XP[t0 + b * 128: t0 + (b + 1) * 128, :], 128,
                      lambda c, b=b: xres[:, c, b * 128:(b + 1) * 128], [xk(c, cti) for c in range(8)])
            if has_s:
                tm2fm(XS[:, :], NSAMP, lambda c: xres[:, c, HALF:HALF + NSAMP], [xk(c, 2) for c in range(8)])

            for l in range(nlayers):
                def rmsnorm(vidx, l=l):
                    for (c0, n, cti) in cts:
                        ps, pk = nps()
                        for c in range(8):
                            sq, sqk = ntb()
                            P.act(lambda e, c=c, sq=sq, c0=c0, n=n: e.activation(out=sq[:, 0:n], in_=xres[:, c, c0:c0 + n], func=AF.Square),
                                  reads=[xk(c, cti)], writes=[sqk])
                            P.pe(lambda e, c=c, sq=sq, ps=ps, n=n: e.matmul(ps[:, 0:n], lhsT=onesn[:], rhs=sq[:, 0:n], start=(c == 0), stop=(c == 7)),
                                 reads=[sqk, "onesn"], writes=[pk])
                        rs, rk = nt32()
                        P.act(lambda e, ps=ps, rs=rs, n=n: e.activation(out=rs[:, 0:n], in_=ps[:, 0:n], func=AF.Ln, bias=EPS), reads=[pk], writes=[rk])
                        P.act(lambda e, rs=rs, ps=ps, n=n: e.activation(out=ps[:, 0:n], in_=rs[:, 0:n], func=AF.Exp, scale=-0.5), reads=[rk], writes=[pk])
                        for c in range(8):
                            P.dve(lambda e, c=c, ps=ps, c0=c0, n=n: e.scalar_tensor_tensor(
                                out=hb[:, c, c0:c0 + n], in0=xres[:, c, c0:c0 + n], scalar=vecs[:, l, vidx, c:c + 1],
                                in1=ps[:, 0:n], op0=ALU.mult, op1=ALU.mult),
                                reads=[xk(c, cti), pk, "vecs"], writes=[hk(c, cti)])

                def hreads(cti):
                    return [hk(c, cti) for c in range(8)]

                chk("xload")
                rmsnorm(0)
                chk("norm1")

                wq, wqk = wneed(hf, l, "q")
                for (c0, n, cti) in cts:
                    for qc in range(4):
                        ps, pk = nps()
                        mm(ps[:, 0:n], [(wq[:, kc * 512 + qc * 128: kc * 512 + (qc + 1) * 128], hb[:, kc, c0:c0 + n]) for kc in range(8)],
                           [wqk] + hreads(cti), pk)
                        sq, sqk = ntb()
                        q32, q32k = nt32()
                        P.act(lambda e, ps=ps, sq=sq, n=n: e.activation(out=sq[:, 0:n], in_=ps[:, 0:n], func=AF.Square), reads=[pk], writes=[sqk])
                        P.act(lambda e, ps=ps, q32=q32, n=n: e.activation(out=q32[:, 0:n], in_=ps[:, 0:n], func=AF.Copy), reads=[pk], writes=[q32k])
                        ps2, pk2 = nps()
                        P.pe(lambda e, ps2=ps2, sq=sq, n=n: e.matmul(ps2[:, 0:n], lhsT=blk1[:], rhs=sq[:, 0:n], start=True, stop=True),
                             reads=[sqk, "blk1"], writes=[pk2])
                        rs, rk = nt32()
                        P.act(lambda e, ps2=ps2, rs=rs, n=n: e.activation(out=rs[:, 0:n], in_=ps2[:, 0:n], func=AF.Ln, bias=EPS), reads=[pk2], writes=[rk])
                        P.act(lambda e, rs=rs, ps2=ps2, n=n: e.activation(out=ps2[:, 0:n], in_=rs[:, 0:n], func=AF.Exp, scale=-0.5), reads=[rk], writes=[pk2])
                        P.dve(lambda e, q32=q32, ps2=ps2, qc=qc, c0=c0, n=n, l=l: e.scalar_tensor_tensor(
                            out=S[:, 16 + qc, c0:c0 + n], in0=q32[:, 0:n], scalar=qkg[:, l:l + 1], in1=ps2[:, 0:n], op0=ALU.mult, op1=ALU.mult),
                            reads=[q32k, pk2, "qkg"], writes=qk(16 + qc, cti))
                wkv, wkvk = wneed(hf, l, "kv")
                for (c0, n, cti) in cts:
                    ps, pk = nps()
                    mm(ps[:, 0:n], [(wkv[:, kc * 256: kc * 256 + 128], hb[:, kc, c0:c0 + n]) for kc in range(8)], [wkvk] + hreads(cti), pk)
                    sq, sqk = ntb()
                    q32, q32k = nt32()
                    P.act(lambda e, ps=ps, sq=sq, n=n: e.activation(out=sq[:, 0:n], in_=ps[:, 0:n], func=AF.Square), reads=[pk], writes=[sqk])
                    P.act(lambda e, ps=ps, q32=q32, n=n: e.activation(out=q32[:, 0:n], in_=ps[:, 0:n], func=AF.Copy), reads=[pk], writes=[q32k])
                    ps2, pk2 = nps()
                    P.pe(lambda e, ps2=ps2, sq=sq, n=n: e.matmul(ps2[:, 0:n], lhsT=blk1[:], rhs=sq[:, 0:n], start=True, stop=True),
                         reads=[sqk, "blk1"], writes=[pk2])
                    rs, rk = nt32()
                    P.act(lambda e, ps2=ps2, rs=rs, n=n: e.activation(out=rs[:, 0:n], in_=ps2[:, 0:n], func=AF.Ln, bias=EPS), reads=[pk2], writes=[rk])
                    P.act(lambda e, rs=rs, ps2=ps2, n=n: e.activation(out=ps2[:, 0:n], in_=rs[:, 0:n], func=AF.Exp, scale=-0.5), reads=[rk], writes=[pk2])
                    kkeys = [("kT", "s")] if cti == 2 else [("kT", 4 * cti + b) for b in range(4)]
                    kn, knk = nt32()
                    P.dve(lambda e, q32=q32, ps2=ps2, kn=kn, n=n, l=l: e.scalar_tensor_tensor(
                        out=kn[:, 0:n], in0=q32[:, 0:n], scalar=qkg[:, NL + l:NL + l + 1], in1=ps2[:, 0:n], op0=ALU.mult, op1=ALU.mult),
                        reads=[q32k, pk2, "qkg"], writes=[knk])
                    P.act(lambda e, kn=kn, c0=c0, n=n: e.activation(out=kT[:, c0:c0 + n], in_=kn[:, 0:n], func=AF.Copy),
                          reads=[knk], writes=kkeys)
                    if hf == 0 and cti == 1 and nhalves > 1:
                        P.act(lambda e, kn=kn, l=l: e.activation(out=kprev[:, l, :], in_=kn[:, 384:512], func=AF.Copy), reads=[knk], writes=[("kprev", l)])
                    if hf == nhalves - 1 and cti == 1:
                        P.act(lambda e, kn=kn: e.activation(out=k32keep[:, 0:128], in_=kn[:, 384:512], func=AF.Copy), reads=[knk], writes=["k32l"])
                    if cti == 2:
                        P.act(lambda e, kn=kn: e.activation(out=k32keep[:, 128:128 + NSAMP], in_=kn[:, 0:NSAMP], func=AF.Copy), reads=[knk], writes=["k32s"])
                    if cti < 2:
                        ps, pk = nps()
                        for bi in range(4):
                            cb = c0 + bi * 128
                            mm(ps[:, bi * 128:(bi + 1) * 128], [(hb[:, kc, cb:cb + 128], wkv[:, kc * 256 + 128: kc * 256 + 256]) for kc in range(8)],
                               [wkvk] + hreads(cti), pk)
                        b0 = 4 * cti
                        P.dve(lambda e, ps=ps, b0=b0: e.tensor_copy(
                            out=vz[:, b0:b0 + 4, 0:64], in_=ps[:, :].rearrange("p (b g x) -> p b g x", b=4, g=2)[:, :, 0, :]),
                            reads=[pk], writes=[("vz", b0 + i) for i in range(4)])
                        P.dve(lambda e, ps=ps, b0=b0: e.tensor_copy(
                            out=vz[:, b0:b0 + 4, 128:192], in_=ps[:, :].rearrange("p (b g x) -> p b g x", b=4, g=2)[:, :, 1, :]),
                            reads=[pk] + [("vz", b0 + i) for i in range(4)], writes=[("vz", b0 + i) for i in range(4)])
                        if hf == nhalves - 1 and cti == 1:
                            P.dve(lambda e, ps=ps: e.tensor_copy(out=v32last[:], in_=ps[:, 384:512]), reads=[pk], writes=["v32last"])
                        if hf == 0 and cti == 1 and nhalves > 1:
                            P.act(lambda e, l=l: e.activation(out=vprev[:, l, :], in_=vz[:, 7, :], func=AF.Copy), reads=[("vz", 7)], writes=[("vprev", l)])
                    else:
                        ps, pk = nps()
                        mm(ps[0:NSAMP, 0:128], [(hb[:, kc, c0:c0 + n], wkv[:, kc * 256 + 128: kc * 256 + 256]) for kc in range(8)],
                           [wkvk] + hreads(cti), pk)
                        P.act(lambda e, ps=ps: e.activation(out=vnew32[:], in_=ps[0:NSAMP, 0:128], func=AF.Copy), reads=[pk], writes=["vnew32"])

                chk("qkv")
                def att_group(j, g, l=l):
                    gb = 8 * hf + j
                    qc0 = j * 128
                    rows = slice(g * 64, (g + 1) * 64)
                    kvsrc = []
                    if gb > 0:
                        if j > 0:
                            kvsrc.append((1, kT[rows, (j - 1) * 128: j * 128], ("kT", j - 1), vz[:, j - 1, g * 64: g * 64 + 128], ("vz", j - 1)))
                        else:
                            kvsrc.append((1, kprev[rows, l, :], ("kprev", l), vprev[:, l, g * 64: g * 64 + 128], ("vprev", l)))
                    kvsrc.append((0, kT[rows, j * 128:(j + 1) * 128], ("kT", j), vz[:, j, g * 64: g * 64 + 128], ("vz", j)))
                    qrhs = S[rows, 16:20, qc0:qc0 + 128]
                    qkeys = [("Sq", 16 + c, j, g) for c in range(4)]
                    st_ = {}

                    def stage_a():
                        pts = []
                        for (pc, kap, kkey, vap, vkey) in kvsrc:
                            ps, pk = nps()
                            P.pe(lambda e, ps=ps, kap=kap: e.matmul(ps[:, :].rearrange("p (c q) -> p c q", c=4), lhsT=kap, rhs=qrhs, start=True, stop=True),
                                 reads=[kkey] + qkeys, writes=[pk])
                            tt, tk = nt32()
                            P.dve(lambda e, ps=ps, tt=tt, pc=pc: e.scalar_tensor_tensor(
                                out=tt[:, :].rearrange("p (c q) -> p c q", c=4), in0=ps[:, :].rearrange("p (c q) -> p c q", c=4), scalar=0.125,
                                in1=bias[:, 4 * g:4 * g + 4, pc, :], op0=ALU.mult, op1=ALU.add),
                                reads=[pk, "bias"], writes=[tk])
                            pt, ptk = ntb()
                            P.act(lambda e, tt=tt, pt=pt: e.activation(out=pt[:, :], in_=tt[:, :], func=AF.Exp), reads=[tk], writes=[ptk])
                            pts.append((pt, ptk, vap, vkey))
                        st_["pts"] = pts

                    def stage_b1():
                        pts = st_["pts"]
                        npts = len(pts)
                        psn, pkn = nps()
                        psd, pkd = nps()
                        for i, (pt, ptk, vap, vkey) in enumerate(pts):
                            P.pe(lambda e, psn=psn, vap=vap, pt=pt, i=i: e.matmul(psn[:, :], lhsT=vap, rhs=pt[:, :], start=(i == 0), stop=(i == npts - 1)),
                                 reads=[ptk, vkey], writes=[pkn])
                        for i, (pt, ptk, vap, vkey) in enumerate(pts):
                            P.pe(lambda e, psd=psd, pt=pt, i=i: e.matmul(psd[:, :], lhsT=ones1[:], rhs=pt[:, :], start=(i == 0), stop=(i == npts - 1)),
                                 reads=[ptk, "ones1"], writes=[pkd])
                        dn, dnk = nt32()
                        P.dve(lambda e, psd=psd, dn=dn: e.tensor_tensor(
                            out=dn[rows, :].rearrange("p (c q) -> p c q", c=4), in0=psd[rows, :].rearrange("p (c q) -> p c q", c=4),
                            in1=sinkexp[rows, l * 8 + 4 * g: l * 8 + 4 * g + 4].unsqueeze(2).to_broadcast([64, 4, 128]), op=ALU.add),
                            reads=[pkd, "sinkexp"], writes=[dnk])
                        P.act(lambda e, dn=dn: e.activation(out=dn[rows, :], in_=dn[rows, :], func=AF.Ln), reads=[dnk], writes=[dnk])
                        P.act(lambda e, dn=dn: e.activation(out=dn[rows, :], in_=dn[rows, :], func=AF.Exp, scale=-1.0), reads=[dnk], writes=[dnk])
                        st_["b"] = (psn, pkn, dn, dnk)

                    def stage_b2():
                        psn, pkn, dn, dnk = st_["b"]
                        P.dve(lambda e, psn=psn, dn=dn: e.tensor_tensor(
                            out=S[rows, 16:20, qc0:qc0 + 128], in0=psn[rows, :].rearrange("p (c q) -> p c q", c=4),
                            in1=dn[rows, :].rearrange("p (c q) -> p c q", c=4), op=ALU.mult),
                            reads=[pkn, dnk], writes=qkeys)

                    return stage_a, stage_b1, stage_b2

                fillers = []
                for gi in range(2):
                    for (c0, n, cti) in cts:
                        for mi in range(4):
                            def fill(gi=gi, c0=c0, n=n, cti=cti, mi=mi, l=l):
                                w, wk = wneed(hf, l, "g%d" % gi, ahead=0)
                                mc = 4 * gi + mi
                                ps, pk = nps()
                                mm(ps[:, 0:n], [(w[:, kc * 512 + mi * 128: kc * 512 + (mi + 1) * 128], hb[:, kc, c0:c0 + n]) for kc in range(8)], [wk] + hreads(cti), pk)
                                if (mi + cti) % 2 == 0:
                                    P.dve(lambda e: e.tensor_copy(out=S[:, 8 + mc, c0:c0 + n], in_=ps[:, 0:n]), reads=[pk], writes=[sk(8 + mc, cti)])
                                else:
                                    P.act(lambda e: e.activation(out=S[:, 8 + mc, c0:c0 + n], in_=ps[:, 0:n], func=AF.Copy), reads=[pk], writes=[sk(8 + mc, cti)])
                            fillers.append(fill)
                if not has_s:
                    wneed(hf, l, "ga0", ahead=0)
                groups = [att_group(j, g) for j in range(8) for g in range(2)]
                ng = len(groups)
                nfill = [0]
                for i in range(ng + 2):
                    want = min(len(fillers), ((i + 1) * len(fillers) + ng - 1) // ng)
                    while nfill[0] < want:
                        fillers[nfill[0]]()
                        nfill[0] += 1
                    if i < ng:
                        groups[i][0]()
                    if 0 <= i - 1 < ng:
                        groups[i - 1][1]()
                    if 0 <= i - 2 < ng:
                        groups[i - 2][2]()
                chk("attn")
                if has_s:
                    sc0 = HALF
                    for s8 in range(2):
                        sg, sgk = nstage()
                        P.dma("sp", lambda e, sg=sg, s8=s8, l=l: e.dma_start(out=sg[:, :].rearrange("p (s x) -> p s x", s=8),
                                                                        in_=CK[l, s8 * 8:(s8 + 1) * 8].rearrange("s k x -> k s x")), writes=[sgk])
                        for (r0, r1) in ((1, 16), (16, 128)):
                            outs.append(P.dma("sp", lambda e, sg=sg, s8=s8, l=l, r0=r0, r1=r1: e.dma_start(
                                out=NKS[l, s8 * 8:(s8 + 1) * 8, r0 - 1:r1 - 1, :].rearrange("s k x -> k s x"),
                                in_=sg[r0:r1, :].rearrange("p (s x) -> p s x", s=8)), reads=[sgk], writes=[("nks", l, 0, s8, r0)]))
                        for si in range(8):
                            s = s8 * 8 + si
                            pst, pkt = nps()
                            P.pe(lambda e, pst=pst, sg=sg, si=si: e.transpose(pst[:, 0:128], sg[:, si * 128:(si + 1) * 128], ident[:, :]), reads=[sgk, "ident"], writes=[pkt])
                            P.act(lambda e, pst=pst, s=s: e.activation(out=KsT[:, s, 1:128], in_=pst[:, 1:128], func=AF.Copy), reads=[pkt], writes=[("KsT", s)])
                            P.act(lambda e, s=s: e.activation(out=KsT[:, s, 0:1], in_=kT[:, sc0 + s:sc0 + s + 1], func=AF.Copy), reads=[("kT", "s"), ("KsT", s)], writes=[("KsT", s)])
                    chk("sa1")
                    pssg = [nps(), nps()]
                    for s in range(NSAMP):
                        for g in range(2):
                            rows = slice(g * 64, (g + 1) * 64)
                            P.pe(lambda e, pq=pssg[g][0], s=s, g=g, rows=rows: e.matmul(pq[:, s * 4: s * 4 + 4], lhsT=KsT[rows, s, :],
                                                                                   rhs=S[rows, 16:20, sc0 + s], start=True, stop=True),
                                 reads=[("KsT", s)] + [("Sq", 16 + c, "s", g) for c in range(4)], writes=[pssg[g][1]])
                    chk("sa2")
                    for s8 in range(2):
                        sg, sgk = nstage()
                        P.dma("sp", lambda e, sg=sg, s8=s8, l=l: e.dma_start(out=sg[:, :].rearrange("p (s x) -> p s x", s=8),
                                                                        in_=CV[l, s8 * 8:(s8 + 1) * 8].rearrange("s k x -> k s x")), writes=[sgk])
                        for (r0, r1) in ((1, 16), (16, 128)):
                            outs.append(P.dma("sp", lambda e, sg=sg, s8=s8, l=l, r0=r0, r1=r1: e.dma_start(
                                out=NVS[l, s8 * 8:(s8 + 1) * 8, r0 - 1:r1 - 1, :].rearrange("s k x -> k s x"),
                                in_=sg[r0:r1, :].rearrange("p (s x) -> p s x", s=8)), reads=[sgk], writes=[("nvs", l, 0, s8, r0)]))
                        P.act(lambda e, sg=sg, s8=s8: e.activation(out=Vs[:, s8 * 8:(s8 + 1) * 8, 0:64], in_=sg[:, :].rearrange("p (s x) -> p s x", s=8)[:, :, 0:64], func=AF.Copy),
                              reads=[sgk, "Vs"], writes=["Vs"])
                        P.dve(lambda e, sg=sg, s8=s8: e.tensor_copy(out=Vs[:, s8 * 8:(s8 + 1) * 8, 128:192], in_=sg[:, :].rearrange("p (s x) -> p s x", s=8)[:, :, 64:128]),
                              reads=[sgk, "Vs"], writes=["Vs"])
                    outs.append(P.dma("sp", lambda e, l=l: e.dma_start(out=NVS[l, :, 127, :], in_=vnew32[:, :]), reads=["vnew32"], writes=[("nvs", l, 1)]))
                    for s4 in range(4):
                        psv, pkv = nps()
                        for si in range(4):
                            s = s4 * 4 + si
                            mm(psv[0:1, si * 128:(si + 1) * 128], [(hb[:, kc, sc0 + s:sc0 + s + 1], wkv[:, kc * 256 + 128: kc * 256 + 256]) for kc in range(8)],
                               [wkvk] + hreads(2), pkv)
                        P.act(lambda e, psv=psv, s4=s4: e.activation(out=Vs[0:1, s4 * 4:(s4 + 1) * 4, 0:64], in_=psv[0:1, :].rearrange("p (s x) -> p s x", s=4)[:, :, 0:64], func=AF.Copy),
                              reads=[pkv, "Vs"], writes=["Vs"])
                        P.act(lambda e, psv=psv, s4=s4: e.activation(out=Vs[0:1, s4 * 4:(s4 + 1) * 4, 128:192], in_=psv[0:1, :].rearrange("p (s x) -> p s x", s=4)[:, :, 64:128], func=AF.Copy),
                              reads=[pkv, "Vs"], writes=["Vs"])
                    wneed(hf, l, "ga0")
                    chk("sa3")
                    tt, tk = nt32()
                    for g in range(2):
                        P.dve(lambda e, pq=pssg[g][0], tt=tt, g=g: e.scalar_tensor_tensor(
                            out=tt[:, 0:128].rearrange("p (s h) -> p s h", s=NSAMP)[:, :, 4 * g:4 * g + 4], in0=pq[:, 0:64].rearrange("p (s c) -> p s c", s=NSAMP), scalar=0.125,
                            in1=biasS[:, 4 * g:4 * g + 4].unsqueeze(1).to_broadcast([128, NSAMP, 4]), op0=ALU.mult, op1=ALU.add),
                            reads=[pssg[g][1], "biasS"] + ([tk] if g else []), writes=[tk])
                    pt, ptk = ntb()
                    P.act(lambda e, tt=tt, pt=pt: e.activation(out=pt[:, 0:128], in_=tt[:, 0:128], func=AF.Exp), reads=[tk], writes=[ptk])
                    psd, pkd = nps()
                    P.pe(lambda e, psd=psd, pt=pt: e.matmul(psd[:, 0:128], lhsT=ones1[:], rhs=pt[:, 0:128], start=True, stop=True), reads=[ptk, "ones1"], writes=[pkd])
                    pso, pko = nps()
                    for s in range(NSAMP):
                        for g in range(2):
                            P.pe(lambda e, pso=pso, pt=pt, s=s, g=g: e.matmul(pso[:, s * 4:(s + 1) * 4], lhsT=Vs[:, s, g * 64: g * 64 + 128],
                                                                           rhs=pt[:, s * 8 + g * 4: s * 8 + g * 4 + 4], start=(g == 0), stop=(g == 1)),
                                 reads=[ptk, "Vs"], writes=[pko])
                    dn, dnk = nt32()
                    P.dve(lambda e, psd=psd, dn=dn, l=l: e.tensor_tensor(
                        out=dn[:, 0:128].rearrange("p (s h) -> p s h", s=NSAMP), in0=psd[:, 0:128].rearrange("p (s h) -> p s h", s=NSAMP),
                        in1=sinkexp[:, l * 8:(l + 1) * 8].unsqueeze(1).to_broadcast([128, NSAMP, 8]), op=ALU.add),
                        reads=[pkd, "sinkexp"], writes=[dnk])
                    P.act(lambda e, dn=dn: e.activation(out=dn[:, 0:128], in_=dn[:, 0:128], func=AF.Ln), reads=[dnk], writes=[dnk])
                    P.act(lambda e, dn=dn: e.activation(out=dn[:, 0:128], in_=dn[:, 0:128], func=AF.Exp, scale=-1.0), reads=[dnk], writes=[dnk])
                    for g in range(2):
                        rows = slice(g * 64, (g + 1) * 64)
                        P.dve(lambda e, pso=pso, dn=dn, rows=rows, g=g: e.tensor_tensor(
                            out=S[rows, 16:20, sc0:sc0 + NSAMP].rearrange("p c s -> p s c"),
                            in0=pso[rows, 0:64].rearrange("p (s c) -> p s c", c=4),
                            in1=dn[rows, 0:128].rearrange("p (s g c) -> p s g c", g=2, c=4)[:, :, g, :], op=ALU.mult),
                            reads=[pko, dnk], writes=[("Sq", 16 + c, "s", g) for c in range(4)])
                    chk("sa4")
                    pst, pkt = nps()
                    P.pe(lambda e, pst=pst: e.transpose(pst[0:NSAMP, 0:128], k32keep[:, 128:128 + NSAMP], ident[:, :]), reads=["k32s", "ident"], writes=[pkt])
                    tt, tk = nt32()
                    P.act(lambda e, pst=pst, tt=tt: e.activation(out=tt[0:NSAMP, 0:128], in_=pst[0:NSAMP, 0:128], func=AF.Copy), reads=[pkt], writes=[tk])
                    outs.append(P.dma("sp", lambda e, tt=tt, l=l: e.dma_start(out=NKS[l, :, 127, :], in_=tt[0:NSAMP, 0:128]), reads=[tk], writes=[("nks", l, 1)]))

                chk("sattn")
                if hf == nhalves - 1:
                    pst, pkt = nps()
                    P.pe(lambda e, pst=pst: e.transpose(pst[:, 0:128], k32keep[:, 0:128], ident[:, :]), reads=["k32l", "ident"], writes=[pkt])
                    tt, tk = nt32()
                    P.act(lambda e, pst=pst, tt=tt: e.activation(out=tt[:, 0:128], in_=pst[:, 0:128], func=AF.Copy), reads=[pkt], writes=[tk])
                    outs.append(P.dma("sp", lambda e, tt=tt, l=l: e.dma_start(out=NKP[l], in_=tt[:, 0:128]), reads=[tk], writes=[("nkp", l)]))
                    outs.append(P.dma("sp", lambda e, l=l: e.dma_start(out=NVP[l], in_=v32last[:, :]), reads=["v32last"], writes=[("nvp", l)]))

                chk("kvout")
                for i in range(2):
                    w, wk = wneed(hf, l, "ga%d" % i)
                    for (c0, n, cti) in cts:
                        for mi in range(4):
                            mc = 4 * i + mi
                            ps, pk = nps()
                            mm(ps[:, 0:n], [(w[:, kc * 512 + mi * 128: kc * 512 + (mi + 1) * 128], hb[:, kc, c0:c0 + n]) for kc in range(8)], [wk] + hreads(cti), pk)
                            P.act(lambda e, ps=ps, mc=mc, c0=c0, n=n: e.activation(out=S[:, mc, c0:c0 + n], in_=ps[:, 0:n], func=AF.Sigmoid),
                                  reads=[pk], writes=[sk(mc, cti)])
                w, wk = wneed(hf, l, "woa")
                for (c0, n, cti) in cts:
                    for mc in range(8):
                        ps, pk = nps()
                        rd = [wk]
                        for kc in range(4):
                            rd += qk(16 + kc, cti)
                        mm(ps[:, 0:n], [(w[:, kc * 1024 + mc * 128: kc * 1024 + (mc + 1) * 128], S[:, 16 + kc, c0:c0 + n]) for kc in range(4)], rd, pk)
                        P.dve(lambda e, ps=ps, mc=mc, c0=c0, n=n: e.tensor_tensor(out=S[:, mc, c0:c0 + n], in0=ps[:, 0:n], in1=S[:, mc, c0:c0 + n], op=ALU.mult),
                              reads=[pk, sk(mc, cti)], writes=[sk(mc, cti)])

                chk("oa")
                ntot = NCOL if has_s else HALF
                for mc in range(8):
                    P.act(lambda e, mc=mc, ntot=ntot: e.activation(out=S[:, 8 + mc, 0:ntot], in_=S[:, 8 + mc, 0:ntot], func=AF.Gelu_apprx_tanh),
                          reads=[sk(8 + mc, t_[2]) for t_ in cts], writes=[sk(8 + mc, t_[2]) for t_ in cts])
                chk("gelu")
                if has_s:
                    tm2fm(SCV[l].rearrange("s j d -> (s j) d"), 3 * NSAMP, lambda c: cbT[:, c, :], [("cbT", c) for c in range(8)])
                    tm2fm(SLH[l], NSAMP, lambda c: h0T[:, c, :], [("h0T", c) for c in range(8)])
                def lru_prep(c, l=l):
                    def run():
                        w, wk = wneed(hf, l, "r%d" % c, ahead=NBUF - 2)
                        for jt in range(4):
                            P.dve(lambda e, jt=jt: e.tensor_scalar(out=diag[:, c % 2, jt, :], in0=identb[:, :], scalar1=vecs[:, l, 3 + jt, c:c + 1], scalar2=None, op0=ALU.mult),
                                  reads=["identb", "vecs"], writes=[("diag", c % 2)])
                        xb = xrb[c % 2]
                        xbk = ("xrb", c % 2)
                        if hf == 0:
                            P.dve(lambda e: e.memset(xb[:, 0:3], 0.0), writes=[xbk])
                        else:
                            P.act(lambda e: e.activation(out=xb[:, 0:3], in_=xtail[:, l, c, :], func=AF.Copy), reads=[("xtail", l, c)], writes=[xbk])
                        for (c0, n, cti) in cts:
                            ps, pk = nps()
                            mm(ps[:, 0:n], [(w[:, kc * 128:(kc + 1) * 128], hb[:, kc, c0:c0 + n]) for kc in range(8)], [wk] + hreads(cti), pk)
                            P.dve(lambda e, ps=ps, c0=c0, n=n: e.tensor_copy(out=xb[:, 3 + c0:3 + c0 + n], in_=ps[:, 0:n]), reads=[pk, xbk], writes=[xbk])
                            if cti == 2:
                                P.dve(lambda e, ps=ps, n=n: e.tensor_copy(out=xr32s[:, c, :], in_=ps[:, 0:n]), reads=[pk], writes=[("xr32s", c)])
                            if hf == nhalves - 1 and cti == 1:
                                P.dve(lambda e, ps=ps: e.tensor_copy(out=xr32l[:, c, :], in_=ps[:, 509:512]), reads=[pk], writes=[("xr32l", c)])
                        if hf == 0 and nhalves > 1:
                            P.act(lambda e: e.activation(out=xtail[:, l, c, :], in_=xb[:, HALF:HALF + 3], func=AF.Copy), reads=[xbk], writes=[("xtail", l, c)])
                    return run

                def lru_step(i_, c, c0, n, cti, l=l):
                    w = wring[:, wpos[(hf, l, "r%d" % c)] % NBUF, :]
                    wk = ("w", wpos[(hf, l, "r%d" % c)] % NBUF)
                    xb = xrb[c % 2]
                    xbk = ("xrb", c % 2)
                    ra, rak = raR[i_ % 3], ("raR", i_ % 3)
                    ig, igk = igR[i_ % 3], ("igR", i_ % 3)
                    a2, a2k = a2R[i_ % 2], ("a2R", i_ % 2)
                    xcb, xcbk = xcbR[i_ % 3], ("xcbR", i_ % 3)

                    def a1():
                        ps, pk = nps()
                        if cti < 2:
                            pairs = [(diag[:, c % 2, jt, :], xb[:, c0 + jt:c0 + jt + n]) for jt in range(4)]
                            rd = [("diag", c % 2), xbk]
                        else:
                            pairs = [(diag[:, c % 2, jt, :], cbT[:, c, :].rearrange("p (s j) -> p s j", j=3)[:, :, jt]) for jt in range(3)] + [(diag[:, c % 2, 3, :], xb[:, 3 + c0:3 + c0 + n])]
                            rd = [("diag", c % 2), xbk, ("cbT", c)]
                        mm(ps[:, 0:n], pairs, rd, pk)
                        P.dve(lambda e: e.tensor_scalar(out=xcb[:, 0:n], in0=ps[:, 0:n], scalar1=vecs[:, l, 7, c:c + 1], scalar2=None, op0=ALU.add), reads=[pk, "vecs"], writes=[xcbk])

                    def a2_():
                        psr, pkr = nps()
                        psi, pki = nps()
                        P.pe(lambda e: e.matmul(psr[:, 0:n], lhsT=w[:, 1024:1152], rhs=xcb[:, 0:n], start=True, stop=True), reads=[wk, xcbk], writes=[pkr])
                        P.pe(lambda e: e.matmul(psi[:, 0:n], lhsT=w[:, 1152:1280], rhs=xcb[:, 0:n], start=True, stop=True), reads=[wk, xcbk], writes=[pki])
                        P.act(lambda e: e.activation(out=ra[:, 0:n], in_=psr[:, 0:n], func=AF.Exp, scale=-1.0, bias=hbias[:, l, 0, c:c + 1]), reads=[pkr, ("hbias", l)], writes=[rak])
                        P.act(lambda e: e.activation(out=ig[:, 0:n], in_=psi[:, 0:n], func=AF.Exp, scale=-1.0, bias=hbias[:, l, 1, c:c + 1]), reads=[pki, ("hbias", l)], writes=[igk])
                        P.act(lambda e: e.activation(out=ra[:, 0:n], in_=ra[:, 0:n], func=AF.Ln, bias=1.0), reads=[rak], writes=[rak])
                        P.act(lambda e: e.activation(out=ig[:, 0:n], in_=ig[:, 0:n], func=AF.Ln, bias=1.0), reads=[igk], writes=[igk])
                        P.act(lambda e: e.activation(out=ra[:, 0:n], in_=ra[:, 0:n], func=AF.Exp, scale=-1.0), reads=[rak], writes=[rak])
                        P.act(lambda e: e.activation(out=a2[:, 0:n], in_=ra[:, 0:n], func=AF.Exp, scale=cl[:, l, 1, c:c + 1]), reads=[rak, ("cl2", l)], writes=[a2k])
                        P.act(lambda e: e.activation(out=ra[:, 0:n], in_=ra[:, 0:n], func=AF.Exp, scale=cl[:, l, 0, c:c + 1]), reads=[rak, ("cl", l)], writes=[rak])
                        P.act(lambda e: e.activation(out=a2[:, 0:n], in_=a2[:, 0:n], func=AF.Ln, scale=-1.0, bias=1.0), reads=[a2k], writes=[a2k])

                    def b_():
                        P.dve(lambda e: e.scalar_tensor_tensor(out=ig[:, 0:n], in0=a2[:, 0:n], scalar=0.5, in1=ig[:, 0:n], op0=ALU.mult, op1=ALU.subtract),
                              reads=[igk, a2k], writes=[igk])
                        P.act(lambda e: e.activation(out=ig[:, 0:n], in_=ig[:, 0:n], func=AF.Exp), reads=[igk], writes=[igk])

                    def b_post():
                        P.dve(lambda e: e.tensor_tensor(out=ig[:, 0:n], in0=ig[:, 0:n], in1=xcb[:, 0:n], op=ALU.mult), reads=[igk, xcbk], writes=[igk])

                    def c_():
                        hq, hqk = nt32()
                        if cti < 2:
                            first = (hf == 0 and cti == 0)
                            P.dve(lambda e: e.tensor_tensor_scan(
                                out=hq[:, 0:n], data0=ra[:, 0:n], data1=ig[:, 0:n], initial=(0.0 if first else hlast[:, l, c:c + 1]), op0=ALU.mult, op1=ALU.add),
                                reads=[rak, igk] + ([] if first else [("hlast", l, c)]), writes=[hqk])
                            P.dve(lambda e: e.tensor_copy(out=hlast[:, l, c:c + 1], in_=hq[:, n - 1:n]), reads=[hqk], writes=[("hlast", l, c)])
                            src, srck = hq[:, 0:n], hqk
                        else:
                            P.dve(lambda e: e.tensor_tensor(out=hq[:, 0:n], in0=ra[:, 0:n], in1=h0T[:, c, :], op=ALU.mult), reads=[rak, ("h0T", c)], writes=[hqk])
                            P.dve(lambda e: e.tensor_tensor(out=hs32[:, c, :], in0=hq[:, 0:n], in1=ig[:, 0:n], op=ALU.add), reads=[hqk, igk], writes=[("hs32", c)])
                            src, srck = hs32[:, c, :], ("hs32", c)
                        P.pool(lambda e: e.tensor_tensor(out=S[:, 8 + c, c0:c0 + n], in0=src, in1=S[:, 8 + c, c0:c0 + n], op=ALU.mult),
                               reads=[srck, sk(8 + c, cti)], writes=[sk(8 + c, cti)])

                    return [a1, a2_, b_, c_, b_post]

                steps = []
                for c in range(8):
                    for k_, (c0, n, cti) in enumerate(cts):
                        stg = lru_step(len(steps), c, c0, n, cti)
                        steps.append([lru_prep(c) if k_ == 0 else None] + stg)
                nst = 5
                for t in range(len(steps) + nst - 1):
                    for s_ in (3, 4, 2, 1, 0):
                        i = t - s_
                        if 0 <= i < len(steps) and steps[i][s_] is not None:
                            steps[i][s_]()
                    i = t - 3
                    if 0 <= i < len(steps):
                        steps[i][5]()
                chk("lru")
                if has_s:
                    fm2tm(lambda c: hs32[:, c, :], [("hs32", c) for c in range(8)], NSAMP, NHS[l], ("nhs", l))
                    fm2tm(lambda c: xr32s[:, c, :], [("xr32s", c) for c in range(8)], NSAMP, NCS[l, :, 2, :], ("ncs", l, 1))
                    sg, sgk = nstage()
                    for j in range(2):
                        P.dma("sp", lambda e, sg=sg, l=l, j=j: e.dma_start(out=sg[j * NSAMP:(j + 1) * NSAMP, :], in_=SCV[l, :, 1 + j, :]), reads=[sgk] if j else [], writes=[sgk])
                    for j in range(2):
                        outs.append(P.dma("sp", lambda e, sg=sg, l=l, j=j: e.dma_start(out=NCS[l, :, j, :], in_=sg[j * NSAMP:(j + 1) * NSAMP, :]), reads=[sgk], writes=[("ncs", l, 0, j)]))
                if hf == nhalves - 1:
                    fm2tm(lambda c: xr32l[:, c, :], [("xr32l", c) for c in range(8)], 3, NCP[l], ("ncp", l))
                    pst, pkt = nps()
                    P.pe(lambda e, pst=pst, l=l: e.transpose(pst[0:8, 0:128], hlast[:, l, :], ident[:, :]), reads=[("hlast", l, c) for c in range(8)] + ["ident"], writes=[pkt])
                    tt, tk = nt32()
                    P.act(lambda e, pst=pst, tt=tt: e.activation(out=tt[0:8, 0:128], in_=pst[0:8, 0:128], func=AF.Copy), reads=[pkt], writes=[tk])
                    outs.append(P.dma("sp", lambda e, tt=tt, l=l: e.dma_start(out=NHP[l], in_=tt[0:8, 0:128]), reads=[tk], writes=[("nhp", l)]))

                chk("lruout")
                for i in range(4):
                    w, wk = wneed(hf, l, "ol%d" % i)
                    for (c0, n, cti) in cts:
                        for mi in range(2):
                            mc = 2 * i + mi
                            pso, pko = nps()
                            psg, pkg = nps()
                            mm(psg[:, 0:n], [(w[:, 2048 + kc * 256 + mi * 128: 2048 + kc * 256 + (mi + 1) * 128], hb[:, kc, c0:c0 + n]) for kc in range(8)], [wk] + hreads(cti), pkg)
                            mm(pso[:, 0:n], [(w[:, kc * 256 + mi * 128: kc * 256 + (mi + 1) * 128], S[:, 8 + kc, c0:c0 + n]) for kc in range(8)],
                               [wk] + [sk(8 + kc, cti) for kc in range(8)], pko)
                            sg_, sgk_ = nt32()
                            P.act(lambda e, psg=psg, sg_=sg_, n=n: e.activation(out=sg_[:, 0:n], in_=psg[:, 0:n], func=AF.Sigmoid), reads=[pkg], writes=[sgk_])
                            P.dve(lambda e, pso=pso, sg_=sg_, n=n: e.tensor_tensor(out=sg_[:, 0:n], in0=pso[:, 0:n], in1=sg_[:, 0:n], op=ALU.mult), reads=[pko, sgk_], writes=[sgk_])
                            P.pool(lambda e, sg_=sg_, mc=mc, c0=c0, n=n: e.tensor_tensor(out=S[:, mc, c0:c0 + n], in0=sg_[:, 0:n], in1=S[:, mc, c0:c0 + n], op=ALU.add),
                                   reads=[sgk_, sk(mc, cti)], writes=[sk(mc, cti)])
                chk("ol")
                for i in range(2):
                    w, wk = wneed(hf, l, "wo%d" % i)
                    for (c0, n, cti) in cts:
                        for mi in range(4):
                            mc = 4 * i + mi
                            ps, pk = nps()
                            mm(ps[:, 0:n], [(w[:, kc * 512 + mi * 128: kc * 512 + (mi + 1) * 128], S[:, kc, c0:c0 + n]) for kc in range(8)],
                               [wk] + [sk(kc, cti) for kc in range(8)], pk)
                            P.dve(lambda e, ps=ps, mc=mc, c0=c0, n=n: e.tensor_tensor(out=xres[:, mc, c0:c0 + n], in0=ps[:, 0:n], in1=xres[:, mc, c0:c0 + n], op=ALU.add),
                                  reads=[pk, xk(mc, cti)], writes=[xk(mc, cti)])
                chk("wo")
                rmsnorm(1)
                for i in range(11):
                    w, wk = wneed(hf, l, "ff%d" % i)
                    for (c0, n, cti) in cts:
                        for mi in range(2):
                            f = 2 * i + mi
                            psg, pkg = nps()
                            psu, pku = nps()
                            mm(psg[:, 0:n], [(w[:, kc * 256 + mi * 128: kc * 256 + (mi + 1) * 128], hb[:, kc, c0:c0 + n]) for kc in range(8)], [wk] + hreads(cti), pkg)
                            mm(psu[:, 0:n], [(w[:, 2048 + kc * 256 + mi * 128: 2048 + kc * 256 + (mi + 1) * 128], hb[:, kc, c0:c0 + n]) for kc in range(8)], [wk] + hreads(cti), pku)
                            sg_, sgk_ = nt32()
                            P.act(lambda e, psg=psg, sg_=sg_, n=n: e.activation(out=sg_[:, 0:n], in_=psg[:, 0:n], func=AF.Silu), reads=[pkg], writes=[sgk_])
                            wkeys = qk(f, cti) if 16 <= f < 20 else [sk(f, cti)]
                            P.dve(lambda e, psu=psu, sg_=sg_, f=f, c0=c0, n=n: e.tensor_tensor(out=S[:, f, c0:c0 + n], in0=psu[:, 0:n], in1=sg_[:, 0:n], op=ALU.mult),
                                  reads=[pku, sgk_], writes=wkeys)
                for m in range(8):
                    w, wk = wneed(hf, l, "dn%d" % m)
                    for (c0, n, cti) in cts:
                        ps, pk = nps()
                        rd = [wk]
                        for f in range(KFF):
                            rd += qk(f, cti) if 16 <= f < 20 else [sk(f, cti)]
                        mm(ps[:, 0:n], [(w[:, kc * 128:(kc + 1) * 128], S[:, kc, c0:c0 + n]) for kc in range(KFF)], rd, pk)
                        P.dve(lambda e, ps=ps, m=m, c0=c0, n=n: e.tensor_tensor(out=xres[:, m, c0:c0 + n], in0=ps[:, 0:n], in1=xres[:, m, c0:c0 + n], op=ALU.add),
                              reads=[pk, xk(m, cti)], writes=[xk(m, cti)])
                chk("ffn")
                rmsnorm(2)
                for b4 in range(2):
                    sg, sgk = nstage()
                    P.dma("sp", lambda e, sg=sg, b4=b4, l=l, t0=t0: e.dma_start(out=sg[:, :].rearrange("p (b x) -> p b x", b=4),
                                                                    in_=PP[l, t0 + b4 * 512: t0 + (b4 + 1) * 512, :].rearrange("(b p) x -> p b x", p=128)), writes=[sgk])
                    for kc in range(2):
                        ps, pk = nps()
                        for bi in range(4):
                            P.pe(lambda e, ps=ps, sg=sg, bi=bi, kc=kc: e.transpose(ps[:, bi * 128:(bi + 1) * 128], sg[:, bi * 256 + kc * 128: bi * 256 + (kc + 1) * 128], ident[:, :]),
                                 reads=[sgk, "ident"], writes=[pk])
                        P.act(lambda e, ps=ps, kc=kc, b4=b4: e.activation(out=pT[:, kc, b4 * 512:(b4 + 1) * 512], in_=ps[:, :], func=AF.Copy), reads=[pk], writes=[("pT", kc, b4), ("xrb", kc)])
                if has_s:
                    sg, sgk = nstage()
                    P.dma("sp", lambda e, sg=sg, l=l: e.dma_start(out=sg[0:NSAMP, 0:256], in_=PS_[l]), writes=[sgk])
                    ps, pk = nps()
                    for kc in range(2):
                        P.pe(lambda e, ps=ps, sg=sg, kc=kc: e.transpose(ps[:, kc * 128: kc * 128 + NSAMP], sg[0:NSAMP, kc * 128:(kc + 1) * 128], ident[0:NSAMP, 0:NSAMP]),
                             reads=[sgk, "ident"], writes=[pk])
                    for kc in range(2):
                        P.act(lambda e, ps=ps, kc=kc: e.activation(out=pT[:, kc, HALF:HALF + NSAMP], in_=ps[:, kc * 128: kc * 128 + NSAMP], func=AF.Copy), reads=[pk], writes=[("pT", kc, 2), ("xrb", kc)])
                for i in range(3):
                    w, wk = wneed(hf, l, "pl%d" % i)
                    c0_, c1_ = 3 * i, min(3 * i + 3, 8)
                    nch = c1_ - c0_
                    for (c0, n, cti) in cts:
                        for mi in range(nch):
                            mc = c0_ + mi
                            psg, pkg = nps()
                            psp, pkp = nps()
                            mm(psg[:, 0:n], [(w[:, kc * 128 * nch + mi * 128: kc * 128 * nch + (mi + 1) * 128], hb[:, kc, c0:c0 + n]) for kc in range(8)], [wk] + hreads(cti), pkg)
                            o2 = 8 * 128 * nch
                            mm(psp[:, 0:n], [(w[:, o2 + kc * 128 * nch + mi * 128: o2 + kc * 128 * nch + (mi + 1) * 128], pT[:, kc, c0:c0 + n]) for kc in range(2)],
                               [wk, ("pT", 0, cti), ("pT", 1, cti), ("xrb", 0), ("xrb", 1)], pkp)
                            sg_, sgk_ = nt32()
                            P.act(lambda e, psg=psg, sg_=sg_, n=n: e.activation(out=sg_[:, 0:n], in_=psg[:, 0:n], func=AF.Sigmoid), reads=[pkg], writes=[sgk_])
                            P.dve(lambda e, psp=psp, sg_=sg_, n=n: e.tensor_tensor(out=sg_[:, 0:n], in0=psp[:, 0:n], in1=sg_[:, 0:n], op=ALU.mult), reads=[pkp, sgk_], writes=[sgk_])
                            P.pool(lambda e, sg_=sg_, mc=mc, c0=c0, n=n: e.tensor_tensor(out=xres[:, mc, c0:c0 + n], in0=sg_[:, 0:n], in1=xres[:, mc, c0:c0 + n], op=ALU.add),
                                   reads=[sgk_, xk(mc, cti)], writes=[xk(mc, cti)])

            chk("ple")
            for b in range(8):
                cti = b // 4
                fm2tm(lambda c, b=b: xres[:, c, b * 128:(b + 1) * 128], [xk(c, cti) for c in range(8)], 128, YP[t0 + b * 128: t0 + (b + 1) * 128, :], ("yp", hf, b))
            if has_s:
                fm2tm(lambda c: xres[:, c, HALF:HALF + NSAMP], [xk(c, 2) for c in range(8)], NSAMP, YS[:, :], ("ys",))

        P.emit(final_wait_ops=[o for o in outs if o is not None])
    return nc


def _vec_layout(v):
    return np.ascontiguousarray(v.reshape(8, 128).T)


def make_in_maps(inp, nlayers=NL):
    f = lambda a: np.ascontiguousarray(np.asarray(a, dtype=np.float32))
    W = {k: f(inp[k]) for k in ["w_in", "w_o_attn", "w_a", "w_x", "w_o_lru", "w_out", "w_gate", "w_up", "w_down", "w_ple", "w_ple_gate"]}
    wall = np.zeros((NL, 128, TOTW), np.float32)
    for l in range(NL):
        for name, subs in _items(l, W):
            off, sz = ITEM_OFF[name]
            cat = np.concatenate(subs, axis=1)
            assert cat.shape == (128, sz), (name, cat.shape, sz)
            wall[l, :, off:off + sz] = cat
    vecs = np.zeros((128, NL, NV, 8), np.float32)
    cw = f(inp["conv_w"])
    srcs = {"ln1": f(inp["ln1"]), "ln2": f(inp["ln2"]), "ln3": f(inp["ln3"]), "cw0": cw[:, 0], "cw1": cw[:, 1], "cw2": cw[:, 2], "cw3": cw[:, 3],
            "conv_b": f(inp["conv_b"]), "b_a": f(inp["b_a"]), "b_x": f(inp["b_x"]), "lam": f(inp["lam"])}
    for vi, vn in enumerate(VEC_NAMES):
        for l in range(NL):
            vecs[:, l, vi, :] = _vec_layout(srcs[vn][l])
    vecs = vecs.reshape(128, NL * NV * 8)
    qg, kg = f(inp["q_gain"]), f(inp["k_gain"])
    qkg = np.zeros((128, 2 * NL), np.float32)
    for l in range(NL):
        qkg[:, l] = np.tile(qg[l], 2)
        qkg[:, NL + l] = np.tile(kg[l], 2)
    sinks = f(inp["sinks"]).reshape(1, NL * 8)
    t5 = f(inp["t5_table"])
    oh = _t5_onehot()
    ident = np.eye(128, dtype=np.float32)
    xp, xs = f(inp["x_prompt"]), f(inp["x_sample"])
    pp, psm = f(inp["p_prompt"]), f(inp["p_sample"])
    ck, cv = f(inp["cache_k_win"]), f(inp["cache_v_win"])
    slh, scv = f(inp["state_lru_h"]), f(inp["state_conv"])
    maps = []
    for c in range(NCORES):
        s0, s1 = c * NSAMP, (c + 1) * NSAMP
        maps.append(dict(
            xp=xp[c], xs=np.ascontiguousarray(xs[s0:s1, 0, :]), pp=np.ascontiguousarray(pp[:, c]), psm=np.ascontiguousarray(psm[:, s0:s1, 0, :]),
            ck=np.ascontiguousarray(ck[:, s0:s1].reshape(NL, NSAMP, 128, 128)), cv=np.ascontiguousarray(cv[:, s0:s1].reshape(NL, NSAMP, 128, 128)),
            slh=np.ascontiguousarray(slh[:, s0:s1]), scv=np.ascontiguousarray(scv[:, s0:s1]),
            wall=wall, vecs=vecs, qkg=qkg, sinks=sinks, t5=t5, oh=oh, ident=ident))
    return maps


def gather(results):
    r = results
    yp = np.stack([np.asarray(r[c]["yp"]) for c in range(NCORES)], 0)
    ys = np.concatenate([np.asarray(r[c]["ys"]) for c in range(NCORES)], 0)[:, None, :]
    nkp = np.stack([np.asarray(r[c]["nkp"]).reshape(NL, 128, 2, 64) for c in range(NCORES)], 1)
    nvp = np.stack([np.asarray(r[c]["nvp"]).reshape(NL, 128, 2, 64) for c in range(NCORES)], 1)
    nhp = np.stack([np.asarray(r[c]["nhp"]).reshape(NL, D) for c in range(NCORES)], 1)
    ncp = np.stack([np.asarray(r[c]["ncp"]) for c in range(NCORES)], 1)
    nks = np.concatenate([np.asarray(r[c]["nks"]).reshape(NL, NSAMP, 128, 2, 64) for c in range(NCORES)], 1)
    nvs = np.concatenate([np.asarray(r[c]["nvs"]).reshape(NL, NSAMP, 128, 2, 64) for c in range(NCORES)], 1)
    nhs = np.concatenate([np.asarray(r[c]["nhs"]) for c in range(NCORES)], 1)
    ncs = np.concatenate([np.asarray(r[c]["ncs"]) for c in range(NCORES)], 1)
    return tuple(np.ascontiguousarray(a, dtype=np.float32) for a in (yp, ys, nkp, nvp, nhp, ncp, nks, nvs, nhs, ncs))


def kernel(**inputs):
    nc = build()
    maps = make_in_maps(inputs)
    res = run_bass_kernel_spmd(nc, maps, core_ids=list(range(NCORES)))
    return gather(res.results)
```

```python
import math
from contextlib import ExitStack

import numpy as np
import concourse.bass as bass
import concourse.mybir as mybir
from concourse.bass_utils import run_bass_kernel_spmd

F32 = mybir.dt.float32
BF16 = mybir.dt.bfloat16
AF = mybir.ActivationFunctionType
ALU = mybir.AluOpType

D = 1024
NL = 4
SEQ = 2048
HALF = 1024
NSAMP = 16
NCOL = HALF + NSAMP
DFF = 2816
KFF = DFF // 128
EPS = 1e-6
NEG = -30000.0
NCORES = 8
NBUF = 3
SLOT = 4096


class _Op:
    __slots__ = ("eng", "fn", "deps", "dma", "sem", "sigval", "prevval", "needed")

    def __init__(self, eng, fn, deps, dma):
        self.eng = eng
        self.fn = fn
        self.deps = deps
        self.dma = dma
        self.sem = None
        self.sigval = None
        self.prevval = 0
        self.needed = False


class Prog:
    ENGS = ("pe", "act", "dve", "pool", "sp")
    NSLOT = {"sp": 8, "pool": 2}

    def __init__(self, nc):
        self.nc = nc
        self.ops = []
        self.last_writer = {}
        self.readers = {}
        self.stopped = False

    def op(self, eng, fn, reads=(), writes=(), dma=False):
        if self.stopped:
            return None
        idx = len(self.ops)
        deps = set()
        for k in reads:
            w = self.last_writer.get(k)
            if w is not None:
                deps.add(w)
        for k in writes:
            w = self.last_writer.get(k)
            if w is not None:
                deps.add(w)
            rs = self.readers.get(k)
            if rs:
                deps.update(rs)
        for k in reads:
            self.readers.setdefault(k, []).append(idx)
        for k in writes:
            self.last_writer[k] = idx
            self.readers[k] = []
        deps.discard(idx)
        self.ops.append(_Op(eng, fn, deps, dma))
        return idx

    def pe(self, fn, reads=(), writes=()):
        return self.op("pe", fn, reads, writes)

    def act(self, fn, reads=(), writes=()):
        return self.op("act", fn, reads, writes)

    def dve(self, fn, reads=(), writes=()):
        return self.op("dve", fn, reads, writes)

    def pool(self, fn, reads=(), writes=()):
        return self.op("pool", fn, reads, writes)

    def dma(self, q, fn, reads=(), writes=()):
        return self.op(q, fn, reads, writes, dma=True)

    def emit(self, final_wait_ops=()):
        nc = self.nc
        ops = self.ops
        for o in ops:
            for d in o.deps:
                ops[d].needed = True
        for i in final_wait_ops:
            ops[i].needed = True
        with ExitStack() as st:
            csem = {e: st.enter_context(nc.semaphore("c_" + e)) for e in ("pe", "act", "dve", "pool")}
            slots = {
                q: [st.enter_context(nc.semaphore("d_%s_%d" % (q, i))) for i in range(n)]
                for q, n in self.NSLOT.items()
            }
            cnt = {e: 0 for e in csem}
            nd = {q: 0 for q in slots}
            for o in ops:
                if o.dma:
                    j = nd[o.eng]
                    ns = len(slots[o.eng])
                    o.sem = slots[o.eng][j % ns]
                    o.sigval = 16 * (j // ns + 1)
                    o.prevval = 16 * (j // ns)
                    nd[o.eng] = j + 1
                elif o.needed:
                    cnt[o.eng] += 1
                    o.sem = csem[o.eng]
                    o.sigval = cnt[o.eng]
            per_eng = {e: [] for e in self.ENGS}
            for i, o in enumerate(ops):
                per_eng[o.eng].append(i)
            block = st.enter_context(nc.Block())
            final = list(final_wait_ops)

            def gen(ename, eng):
                waited = {}

                def do_wait(sem, val):
                    key = id(sem)
                    if waited.get(key, 0) >= val:
                        return
                    waited[key] = val
                    eng.wait_ge(sem, val)

                for i in per_eng[ename]:
                    o = ops[i]
                    w = {}
                    for d in o.deps:
                        do_ = ops[d]
                        if (not do_.dma) and do_.eng == ename and ename == "pe":
                            continue
                        key = id(do_.sem)
                        if key not in w or w[key][1] < do_.sigval:
                            w[key] = (do_.sem, do_.sigval)
                    if o.dma and o.prevval > 0:
                        key = id(o.sem)
                        if key not in w or w[key][1] < o.prevval:
                            w[key] = (o.sem, o.prevval)
                    for sem, val in w.values():
                        do_wait(sem, val)
                    ins = o.fn(eng)
                    if o.dma:
                        ins.then_inc(o.sem, 16)
                    elif o.needed:
                        ins.then_inc(o.sem, 1)
                if ename == "sp":
                    for i in final:
                        do_wait(ops[i].sem, ops[i].sigval)

            @block.tensor
            def _(e):
                gen("pe", e)

            @block.scalar
            def _(e):
                gen("act", e)

            @block.vector
            def _(e):
                gen("dve", e)

            @block.gpsimd
            def _(e):
                gen("pool", e)

            @block.sync
            def _(e):
                gen("sp", e)


QPERM = np.concatenate([np.concatenate([np.arange(c * 64, c * 64 + 64), np.arange((4 + c) * 64, (4 + c) * 64 + 64)]) for c in range(4)])

VEC_NAMES = ["ln1", "ln2", "ln3", "cw0", "cw1", "cw2", "cw3", "conv_b", "b_a", "b_x", "lam"]
NV = len(VEC_NAMES)


def _pack(w):
    k, n = w.shape
    kc = k // 128
    return np.ascontiguousarray(w.reshape(kc, 128, n).transpose(1, 0, 2).reshape(128, kc * n))


def _items(l, W):
    win = W["w_in"][l]
    its = []
    its.append(("q", [_pack(win[:, 0:512][:, QPERM])]))
    its.append(("kv", [_pack(win[:, 512:768])]))
    for i in range(2):
        its.append(("ga%d" % i, [_pack(win[:, 2816 + 512 * i: 2816 + 512 * (i + 1)])]))
    its.append(("woa", [_pack(W["w_o_attn"][l][QPERM, :])]))
    for i in range(2):
        its.append(("g%d" % i, [_pack(win[:, 1792 + 512 * i: 1792 + 512 * (i + 1)])]))
    for c in range(8):
        its.append(("r%d" % c, [_pack(win[:, 768 + 128 * c: 768 + 128 * (c + 1)]), W["w_a"][l][c], W["w_x"][l][c]]))
    for i in range(4):
        its.append(("ol%d" % i, [_pack(W["w_o_lru"][l][:, 256 * i: 256 * (i + 1)]), _pack(win[:, 3840 + 256 * i: 3840 + 256 * (i + 1)])]))
    for i in range(2):
        its.append(("wo%d" % i, [_pack(W["w_out"][l][:, 512 * i: 512 * (i + 1)])]))
    for i in range(11):
        its.append(("ff%d" % i, [_pack(W["w_gate"][l][:, 256 * i: 256 * (i + 1)]), _pack(W["w_up"][l][:, 256 * i: 256 * (i + 1)])]))
    for m in range(8):
        its.append(("dn%d" % m, [_pack(W["w_down"][l][:, 128 * m: 128 * (m + 1)])]))
    for i in range(3):
        c0, c1 = 3 * i, min(3 * i + 3, 8)
        its.append(("pl%d" % i, [_pack(W["w_ple_gate"][l][:, 128 * c0: 128 * c1]), _pack(W["w_ple"][l][:, 128 * c0: 128 * c1])]))
    return its


def _item_sizes():
    sz = [("q", 4096), ("kv", 2048), ("g0", 4096), ("g1", 4096), ("ga0", 4096), ("ga1", 4096), ("woa", 4096)]
    sz += [("r%d" % c, 1280) for c in range(8)]
    sz += [("ol%d" % i, 4096) for i in range(4)]
    sz += [("wo%d" % i, 4096) for i in range(2)]
    sz += [("ff%d" % i, 4096) for i in range(11)]
    sz += [("dn%d" % m, 2816) for m in range(8)]
    sz += [("pl%d" % i, 1280 * (min(3 * i + 3, 8) - 3 * i)) for i in range(3)]
    return sz


ITEM_SIZES = _item_sizes()
ITEM_OFF = {}
_o = 0
for _n, _s in ITEM_SIZES:
    ITEM_OFF[_n] = (_o, _s)
    _o += _s
TOTW = _o
ITEM_ORDER = [n for n, _ in ITEM_SIZES]


def _t5_onehot():
    oh = np.zeros((33, 384), np.float32)
    for j in range(384):
        d = j - 128
        if 0 <= d < 128:
            if d < 16:
                b = d
            else:
                nf = np.float32(max(d, 1))
                v = np.log(nf / np.float32(16)) / np.float32(math.log(128 / 16)) * np.float32(16)
                b = min(16 + int(np.float32(v)), 31)
            oh[b, j] = 1.0
        else:
            oh[32, j] = 1.0
    return oh


class _Stop(Exception):
    pass


def build(nlayers=NL, nhalves=2, stop=None):
    nc = bass.Bass("TRN2", target_bir_lowering=False)

    def din(name, shape):
        return nc.dram_tensor(name, list(shape), F32, kind="ExternalInput")

    def dout(name, shape):
        return nc.dram_tensor(name, list(shape), F32, kind="ExternalOutput")

    XP = din("xp", [SEQ, D]).ap()
    XS = din("xs", [NSAMP, D]).ap()
    PP = din("pp", [NL, SEQ, 256]).ap()
    PS_ = din("psm", [NL, NSAMP, 256]).ap()
    CK_t = din("ck", [NL, NSAMP, 128, 128])
    CV_t = din("cv", [NL, NSAMP, 128, 128])
    CK, CV = CK_t.ap(), CV_t.ap()
    SLH = din("slh", [NL, NSAMP, D]).ap()
    SCV_t = din("scv", [NL, NSAMP, 3, D])
    SCV = SCV_t.ap()
    WALL = din("wall", [NL, 128, TOTW]).ap()
    VECS = din("vecs", [128, NL * NV * 8]).ap()
    QKG = din("qkg", [128, 2 * NL]).ap()
    SINK_t = din("sinks", [1, NL * 8])
    T5 = din("t5", [32, 8]).ap()
    OH = din("oh", [33, 384]).ap()
    IDN = din("ident", [128, 128]).ap()
    scr_t = nc.dram_tensor("scr", [8, 128, 384], F32, kind="Internal")

    YP = dout("yp", [SEQ, D]).ap()
    YS = dout("ys", [NSAMP, D]).ap()
    NKP = dout("nkp", [NL, 128, 128]).ap()
    NVP = dout("nvp", [NL, 128, 128]).ap()
    NHP = dout("nhp", [NL, 8, 128]).ap()
    NCP = dout("ncp", [NL, 3, D]).ap()
    NKS_t = dout("nks", [NL, NSAMP, 128, 128])
    NVS_t = dout("nvs", [NL, NSAMP, 128, 128])
    NKS, NVS = NKS_t.ap(), NVS_t.ap()
    NHS = dout("nhs", [NL, NSAMP, D]).ap()
    NCS_t = dout("ncs", [NL, NSAMP, 3, D])
    NCS = NCS_t.ap()

    with ExitStack() as st:
        def sb(name, shape, dt):
            return st.enter_context(nc.sbuf_tensor(name, list(shape), dt))

        xres = sb("xres", [128, 8, NCOL], F32)
        hb = sb("hb", [128, 8, NCOL], BF16)
        S = sb("S", [128, 22, NCOL], BF16)
        kT = sb("kT", [128, NCOL], BF16)
        k32keep = sb("k32keep", [128, 128 + NSAMP], F32)
        vz = sb("vz", [128, 8, 192], BF16)
        kprev = sb("kprev", [128, NL, 128], BF16)
        vprev = sb("vprev", [128, NL, 192], BF16)
        v32last = sb("v32last", [128, 128], F32)
        vnew32 = sb("vnew32", [NSAMP, 128], F32)
        bias = sb("bias", [128, 8, 2, 128], F32)
        biasS = sb("biasS", [128, 8], F32)
        wring = sb("wring", [128, NBUF, SLOT], BF16)
        vecs = sb("vecs_sb", [128, NL, NV, 8], F32)
        hbias = sb("hbias", [128, NL, 2, 8], F32)
        cl = sb("cl", [128, NL, 2, 8], F32)
        qkg = sb("qkg_sb", [128, 2 * NL], F32)
        sinkexp = sb("sinkexp", [128, NL * 8], F32)
        ident = sb("ident_sb", [128, 128], F32)
        identb = sb("identb", [128, 128], BF16)
        onesn = sb("onesn", [128, 128], BF16)
        ones1 = sb("ones1", [128, 128], BF16)
        blk1 = sb("blk1", [128, 128], BF16)
        diag = sb("diag", [128, 2, 4, 128], BF16)
        xrb2 = sb("xrb2", [128, 2, 4 + NCOL], BF16)
        xrb = [xrb2[:, i, 0:3 + NCOL] for i in range(2)]
        pT = xrb2[:, :, 0:NCOL]
        xr32s = sb("xr32s", [128, 8, NSAMP], F32)
        xr32l = sb("xr32l", [128, 8, 3], F32)
        xtail = sb("xtail", [128, NL, 8, 3], BF16)
        hlast = sb("hlast", [128, NL, 8], F32)
        hs32 = sb("hs32", [128, 8, NSAMP], F32)
        h0T = sb("h0T", [128, 8, NSAMP], F32)
        cbT = sb("cbT", [128, 8, 3 * NSAMP], BF16)
        KsT = sb("KsT", [128, NSAMP, 128], BF16)
        Vs = sb("Vs", [128, NSAMP, 192], BF16)
        t5sb = sb("t5sb", [33, 8], F32)
        stage = [sb("stage%d" % i, [128, 1024], F32) for i in range(2)]
        t32 = [sb("t32_%d" % i, [128, 512], F32) for i in range(5)]
        raR = [sb("raR%d" % i, [128, 512], F32) for i in range(3)]
        igR = [sb("igR%d" % i, [128, 512], F32) for i in range(3)]
        a2R = [sb("a2R%d" % i, [128, 512], F32) for i in range(2)]
        xcbR = [sb("xcbR%d" % i, [128, 512], BF16) for i in range(3)]
        tb16 = [sb("tb16_%d" % i, [128, 512], BF16) for i in range(6)]
        psum = [st.enter_context(nc.psum_tensor("ps%d" % i, [128, 512], F32)) for i in range(8)]
        ohsb = t32[0][0:33, 0:384]
        ones33 = t32[1][0:33, 0:128]
        L33 = t32[2][0:33, 0:128]

        P = Prog(nc)
        ctr = {"ps": 0, "t32": 0, "tb": 0, "st": 0, "w": 0}

        def nps():
            b = ctr["ps"] % 8
            ctr["ps"] += 1
            return psum[b], ("ps", b)

        def nt32():
            i = ctr["t32"] % len(t32)
            ctr["t32"] += 1
            return t32[i], ("t32", i)

        def ntb():
            i = ctr["tb"] % len(tb16)
            ctr["tb"] += 1
            return tb16[i], ("tb", i)

        def nstage():
            i = ctr["st"] % len(stage)
            ctr["st"] += 1
            return stage[i], ("stage", i)

        outs = []

        def chk(name):
            if stop == name:
                P.stopped = True

        wseq = [(hf, l, n) for hf in range(nhalves) for l in range(nlayers) for n in ITEM_ORDER]
        wpos = {k: i for i, k in enumerate(wseq)}
        wissued = [0]

        def wneed(hf, l, name, ahead=NBUF - 1):
            i = wpos[(hf, l, name)]
            upto = min(len(wseq), i + 1 + ahead)
            while wissued[0] < upto:
                j = wissued[0]
                _, l2, n2 = wseq[j]
                off, sz = ITEM_OFF[n2]
                slot = j % NBUF
                P.dma("pool", lambda e, l2=l2, off=off, sz=sz, slot=slot: e.dma_start(out=wring[:, slot, 0:sz], in_=WALL[l2, :, off:off + sz]),
                      writes=[("w", slot)])
                wissued[0] += 1
            return wring[:, i % NBUF, :], ("w", i % NBUF)

        def mm(ps_ap, pairs, reads, pskey):
            n = len(pairs)
            for i, (l_, r_) in enumerate(pairs):
                P.pe(lambda e, l_=l_, r_=r_, i=i: e.matmul(ps_ap, lhsT=l_, rhs=r_, start=(i == 0), stop=(i == n - 1)),
                     reads=reads, writes=[pskey])

        def tm2fm(src_rows_ap, nrows, dst_fn, dst_keys, extra_reads=()):
            sg, sgk = nstage()
            P.dma("sp", lambda e: e.dma_start(out=sg[0:nrows, :], in_=src_rows_ap), reads=list(extra_reads), writes=[sgk])
            for half8 in range(2):
                ps, pk = nps()
                for cc in range(4):
                    c = half8 * 4 + cc
                    P.pe(lambda e, c=c, cc=cc, ps=ps: e.transpose(ps[:, cc * 128: cc * 128 + nrows], sg[0:nrows, c * 128:(c + 1) * 128], ident[0:nrows, 0:nrows]),
                         reads=[sgk, "ident"], writes=[pk])
                for cc in range(4):
                    c = half8 * 4 + cc
                    P.act(lambda e, c=c, cc=cc, ps=ps: e.activation(out=dst_fn(c), in_=ps[:, cc * 128: cc * 128 + nrows], func=AF.Copy),
                          reads=[pk], writes=[dst_keys[c]])

        def fm2tm(src_fn, src_keys, ncols, dst_ap, dst_key):
            sg, sgk = nstage()
            for half8 in range(2):
                ps, pk = nps()
                for cc in range(4):
                    c = half8 * 4 + cc
                    P.pe(lambda e, c=c, cc=cc, ps=ps: e.transpose(ps[0:ncols, cc * 128:(cc + 1) * 128], src_fn(c), ident[:, :]),
                         reads=[src_keys[c], "ident"], writes=[pk])
                P.act(lambda e, half8=half8, ps=ps: e.activation(out=sg[0:ncols, half8 * 512:(half8 + 1) * 512], in_=ps[0:ncols, :], func=AF.Copy),
                      reads=[pk], writes=[sgk])
            o = P.dma("sp", lambda e: e.dma_start(out=dst_ap, in_=sg[0:ncols, :]), reads=[sgk], writes=[dst_key])
            outs.append(o)

        P.dma("sp", lambda e: e.dma_start(out=ident[:], in_=IDN), writes=["ident"])
        P.dma("sp", lambda e: e.dma_start(out=vecs[:].rearrange("p l v c -> p (l v c)"), in_=VECS), writes=["vecs"])
        P.dma("sp", lambda e: e.dma_start(out=qkg[:], in_=QKG), writes=["qkg"])
        P.dma("sp", lambda e: e.dma_start(out=sinkexp[:], in_=bass.AP(SINK_t, 0, [[0, 128], [1, NL * 8]])), writes=["sinkexp"])
        P.dma("sp", lambda e: e.dma_start(out=t5sb[0:32, :], in_=T5), writes=["t5a"])
        P.dma("sp", lambda e: e.dma_start(out=ohsb, in_=OH), writes=[("t32", 0)])
        P.dve(lambda e: e.memset(t5sb[32:33, :], NEG), writes=["t5b"])
        P.dve(lambda e: e.memset(ones33, 1.0), writes=[("t32", 1)])
        P.dve(lambda e: e.memset(onesn[:], 1.0 / D), writes=["onesn"])
        P.dve(lambda e: e.memset(ones1[:], 1.0), writes=["ones1"])
        P.dve(lambda e: e.memset(blk1[:], 0.0), writes=["blk1"])
        P.dve(lambda e: e.memset(blk1[0:64, 0:64], 1.0 / 64), reads=["blk1"], writes=["blk1"])
        P.dve(lambda e: e.memset(blk1[64:128, 64:128], 1.0 / 64), reads=["blk1"], writes=["blk1"])
        P.dve(lambda e: e.memset(vz[:], 0.0), writes=[("vz", b) for b in range(8)])
        P.dve(lambda e: e.memset(Vs[:], 0.0), writes=["Vs"])
        P.dve(lambda e: e.memset(vprev[:], 0.0), writes=[("vprev", l) for l in range(NL)])
        P.dve(lambda e: e.tensor_copy(out=identb[:], in_=ident[:]), reads=["ident"], writes=["identb"])
        P.act(lambda e: e.activation(out=sinkexp[:], in_=sinkexp[:], func=AF.Exp), reads=["sinkexp"], writes=["sinkexp"])
        for l in range(nlayers):
            lam_ap = vecs[:, l, 10, :]
            P.act(lambda e, l=l, lam_ap=lam_ap: e.activation(out=cl[:, l, 0, :], in_=lam_ap, func=AF.Exp, scale=-1.0), reads=["vecs"], writes=[("cl", l)])
            P.act(lambda e, l=l: e.activation(out=cl[:, l, 0, :], in_=cl[:, l, 0, :], func=AF.Ln, bias=1.0), reads=[("cl", l)], writes=[("cl", l)])
            P.dve(lambda e, l=l: e.tensor_scalar(out=cl[:, l, 1, :], in0=cl[:, l, 0, :], scalar1=-16.0, scalar2=None, op0=ALU.mult), reads=[("cl", l)], writes=[("cl2", l)])
            P.dve(lambda e, l=l: e.tensor_scalar(out=cl[:, l, 0, :], in0=cl[:, l, 0, :], scalar1=-8.0, scalar2=None, op0=ALU.mult), reads=[("cl", l), ("cl2", l)], writes=[("cl", l)])
            P.dve(lambda e, l=l: e.tensor_scalar(out=hbias[:, l, 0, :], in0=vecs[:, l, 8, :], scalar1=-1.0, scalar2=None, op0=ALU.mult), reads=["vecs"], writes=[("hbias", l)])
            P.dve(lambda e, l=l: e.tensor_scalar(out=hbias[:, l, 1, :], in0=vecs[:, l, 9, :], scalar1=-1.0, scalar2=None, op0=ALU.mult), reads=["vecs", ("hbias", l)], writes=[("hbias", l)])
        def t5_setup():
            for hd in range(8):
                P.dve(lambda e, hd=hd: e.tensor_scalar(out=L33, in0=ones33, scalar1=t5sb[:, hd:hd + 1], scalar2=None, op0=ALU.mult),
                      reads=["t5a", "t5b", ("t32", 1)], writes=[("t32", 2)])
                ps, pk = nps()
                P.pe(lambda e, ps=ps: e.matmul(ps[:, 0:384], lhsT=L33, rhs=ohsb, start=True, stop=True), reads=[("t32", 2), ("t32", 0)], writes=[pk])
                tt, tk = t32[3 + hd % 2], ("t32", 3 + hd % 2)
                P.dve(lambda e, ps=ps, tt=tt: e.tensor_copy(out=tt[:, 0:384], in_=ps[:, 0:384]), reads=[pk], writes=[tk])
                P.dma("sp", lambda e, hd=hd, tt=tt: e.dma_start(out=scr_t.ap()[hd], in_=tt[:, 0:384]), reads=[tk], writes=[("scr", hd)])
                P.dma("sp", lambda e, hd=hd: e.dma_start(out=bias[:, hd, :, :], in_=bass.AP(scr_t, hd * 128 * 384 + 128, [[383, 128], [128, 2], [1, 128]])),
                      reads=[("scr", hd)], writes=["bias"])
            P.dve(lambda e: e.tensor_copy(out=biasS[:, :], in_=bias[:, :, 1, 0]), reads=["bias"], writes=["biasS"])
            P.dve(lambda e: e.tensor_copy(out=biasS[0:1, :], in_=bias[0:1, :, 0, 0]), reads=["bias", "biasS"], writes=["biasS"])


        chk("setup")
        for hf in range(nhalves):
            t0 = hf * HALF
            cts = [(0, 512, 0), (512, 512, 1)]
            if hf == 0:
                cts.append((HALF, NSAMP, 2))
            has_s = hf == 0

            def xk(c, cti):
                return ("x", c, cti)

            def hk(c, cti):
                return ("h", c, cti)

            def sk(slot, cti):
                return ("S", slot, cti)

            def qk(slot, cti):
                if cti == 2:
                    return [("Sq", slot, "s", g) for g in range(2)]
                return [("Sq", slot, 4 * cti + b, g) for b in range(4) for g in range(2)]

            for b in range(8):
                cti = b // 4
                tm2fm(XP[t0 + b * 128: t0 + (b + 1) * 128, :], 128,
                      lambda c, b=b: xres[:, c, b * 128:(b + 1) * 128], [xk(c, cti) for c in range(8)])
            if has_s:
                tm2fm(XS[:, :], NSAMP, lambda c: xres[:, c, HALF:HALF + NSAMP], [xk(c, 2) for c in range(8)])
            if hf == 0:
                t5_setup()

            for l in range(nlayers):
                def rmsnorm(vidx, l=l):
                    for (c0, n, cti) in cts:
                        ps, pk = nps()
                        for c in range(8):
                            sq, sqk = ntb()
                            P.act(lambda e, c=c, sq=sq, c0=c0, n=n: e.activation(out=sq[:, 0:n], in_=xres[:, c, c0:c0 + n], func=AF.Square),
                                  reads=[xk(c, cti)], writes=[sqk])
                            P.pe(lambda e, c=c, sq=sq, ps=ps, n=n: e.matmul(ps[:, 0:n], lhsT=onesn[:], rhs=sq[:, 0:n], start=(c == 0), stop=(c == 7)),
                                 reads=[sqk, "onesn"], writes=[pk])
                        rs, rk = nt32()
                        P.act(lambda e, ps=ps, rs=rs, n=n: e.activation(out=rs[:, 0:n], in_=ps[:, 0:n], func=AF.Ln, bias=EPS), reads=[pk], writes=[rk])
                        P.act(lambda e, rs=rs, ps=ps, n=n: e.activation(out=ps[:, 0:n], in_=rs[:, 0:n], func=AF.Exp, scale=-0.5), reads=[rk], writes=[pk])
                        for c in range(8):
                            P.dve(lambda e, c=c, ps=ps, c0=c0, n=n: e.scalar_tensor_tensor(
                                out=hb[:, c, c0:c0 + n], in0=xres[:, c, c0:c0 + n], scalar=vecs[:, l, vidx, c:c + 1],
                                in1=ps[:, 0:n], op0=ALU.mult, op1=ALU.mult),
                                reads=[xk(c, cti), pk, "vecs"], writes=[hk(c, cti)])

                def hreads(cti):
                    return [hk(c, cti) for c in range(8)]

                chk("xload")
                rmsnorm(0)
                chk("norm1")

                wq, wqk = wneed(hf, l, "q")
                for (c0, n, cti) in cts:
                    for qc in range(4):
                        ps, pk = nps()
                        mm(ps[:, 0:n], [(wq[:, kc * 512 + qc * 128: kc * 512 + (qc + 1) * 128], hb[:, kc, c0:c0 + n]) for kc in range(8)],
                           [wqk] + hreads(cti), pk)
                        sq, sqk = ntb()
                        q32, q32k = nt32()
                        P.act(lambda e, ps=ps, sq=sq, n=n: e.activation(out=sq[:, 0:n], in_=ps[:, 0:n], func=AF.Square), reads=[pk], writes=[sqk])
                        P.act(lambda e, ps=ps, q32=q32, n=n: e.activation(out=q32[:, 0:n], in_=ps[:, 0:n], func=AF.Copy), reads=[pk], writes=[q32k])
                        ps2, pk2 = nps()
                        P.pe(lambda e, ps2=ps2, sq=sq, n=n: e.matmul(ps2[:, 0:n], lhsT=blk1[:], rhs=sq[:, 0:n], start=True, stop=True),
                             reads=[sqk, "blk1"], writes=[pk2])
                        rs, rk = nt32()
                        P.act(lambda e, ps2=ps2, rs=rs, n=n: e.activation(out=rs[:, 0:n], in_=ps2[:, 0:n], func=AF.Ln, bias=EPS), reads=[pk2], writes=[rk])
                        P.act(lambda e, rs=rs, ps2=ps2, n=n: e.activation(out=ps2[:, 0:n], in_=rs[:, 0:n], func=AF.Exp, scale=-0.5), reads=[rk], writes=[pk2])
                        P.dve(lambda e, q32=q32, ps2=ps2, qc=qc, c0=c0, n=n, l=l: e.scalar_tensor_tensor(
                            out=S[:, 16 + qc, c0:c0 + n], in0=q32[:, 0:n], scalar=qkg[:, l:l + 1], in1=ps2[:, 0:n], op0=ALU.mult, op1=ALU.mult),
                            reads=[q32k, pk2, "qkg"], writes=qk(16 + qc, cti))
                wkv, wkvk = wneed(hf, l, "kv")
                for (c0, n, cti) in cts:
                    ps, pk = nps()
                    mm(ps[:, 0:n], [(wkv[:, kc * 256: kc * 256 + 128], hb[:, kc, c0:c0 + n]) for kc in range(8)], [wkvk] + hreads(cti), pk)
                    sq, sqk = ntb()
                    q32, q32k = nt32()
                    P.act(lambda e, ps=ps, sq=sq, n=n: e.activation(out=sq[:, 0:n], in_=ps[:, 0:n], func=AF.Square), reads=[pk], writes=[sqk])
                    P.act(lambda e, ps=ps, q32=q32, n=n: e.activation(out=q32[:, 0:n], in_=ps[:, 0:n], func=AF.Copy), reads=[pk], writes=[q32k])
                    ps2, pk2 = nps()
                    P.pe(lambda e, ps2=ps2, sq=sq, n=n: e.matmul(ps2[:, 0:n], lhsT=blk1[:], rhs=sq[:, 0:n], start=True, stop=True),
                         reads=[sqk, "blk1"], writes=[pk2])
                    rs, rk = nt32()
                    P.act(lambda e, ps2=ps2, rs=rs, n=n: e.activation(out=rs[:, 0:n], in_=ps2[:, 0:n], func=AF.Ln, bias=EPS), reads=[pk2], writes=[rk])
                    P.act(lambda e, rs=rs, ps2=ps2, n=n: e.activation(out=ps2[:, 0:n], in_=rs[:, 0:n], func=AF.Exp, scale=-0.5), reads=[rk], writes=[pk2])
                    kkeys = [("kT", "s")] if cti == 2 else [("kT", 4 * cti + b) for b in range(4)]
                    kn, knk = nt32()
                    P.dve(lambda e, q32=q32, ps2=ps2, kn=kn, n=n, l=l: e.scalar_tensor_tensor(
                        out=kn[:, 0:n], in0=q32[:, 0:n], scalar=qkg[:, NL + l:NL + l + 1], in1=ps2[:, 0:n], op0=ALU.mult, op1=ALU.mult),
                        reads=[q32k, pk2, "qkg"], writes=[knk])
                    P.act(lambda e, kn=kn, c0=c0, n=n: e.activation(out=kT[:, c0:c0 + n], in_=kn[:, 0:n], func=AF.Copy),
                          reads=[knk], writes=kkeys)
                    if hf == 0 and cti == 1 and nhalves > 1:
                        P.act(lambda e, kn=kn, l=l: e.activation(out=kprev[:, l, :], in_=kn[:, 384:512], func=AF.Copy), reads=[knk], writes=[("kprev", l)])
                    if hf == nhalves - 1 and cti == 1:
                        P.act(lambda e, kn=kn: e.activation(out=k32keep[:, 0:128], in_=kn[:, 384:512], func=AF.Copy), reads=[knk], writes=["k32l"])
                    if cti == 2:
                        P.act(lambda e, kn=kn: e.activation(out=k32keep[:, 128:128 + NSAMP], in_=kn[:, 0:NSAMP], func=AF.Copy), reads=[knk], writes=["k32s"])
                    if cti < 2:
                        ps, pk = nps()
                        for bi in range(4):
                            cb = c0 + bi * 128
                            mm(ps[:, bi * 128:(bi + 1) * 128], [(hb[:, kc, cb:cb + 128], wkv[:, kc * 256 + 128: kc * 256 + 256]) for kc in range(8)],
                               [wkvk] + hreads(cti), pk)
                        b0 = 4 * cti
                        P.dve(lambda e, ps=ps, b0=b0: e.tensor_copy(
                            out=vz[:, b0:b0 + 4, 0:64], in_=ps[:, :].rearrange("p (b g x) -> p b g x", b=4, g=2)[:, :, 0, :]),
                            reads=[pk], writes=[("vz", b0 + i) for i in range(4)])
                        P.dve(lambda e, ps=ps, b0=b0: e.tensor_copy(
                            out=vz[:, b0:b0 + 4, 128:192], in_=ps[:, :].rearrange("p (b g x) -> p b g x", b=4, g=2)[:, :, 1, :]),
                            reads=[pk] + [("vz", b0 + i) for i in range(4)], writes=[("vz", b0 + i) for i in range(4)])
                        if hf == nhalves - 1 and cti == 1:
                            P.dve(lambda e, ps=ps: e.tensor_copy(out=v32last[:], in_=ps[:, 384:512]), reads=[pk], writes=["v32last"])
                        if hf == 0 and cti == 1 and nhalves > 1:
                            P.act(lambda e, l=l: e.activation(out=vprev[:, l, :], in_=vz[:, 7, :], func=AF.Copy), reads=[("vz", 7)], writes=[("vprev", l)])
                    else:
                        ps, pk = nps()
                        mm(ps[0:NSAMP, 0:128], [(hb[:, kc, c0:c0 + n], wkv[:, kc * 256 + 128: kc * 256 + 256]) for kc in range(8)],
                           [wkvk] + hreads(cti), pk)
                        P.act(lambda e, ps=ps: e.activation(out=vnew32[:], in_=ps[0:NSAMP, 0:128], func=AF.Copy), reads=[pk], writes=["vnew32"])

                chk("qkv")
                def att_group(j, g, l=l):
                    gb = 8 * hf + j
                    qc0 = j * 128
                    rows = slice(g * 64, (g + 1) * 64)
                    kvsrc = []
                    if gb > 0:
                        if j > 0:
                            kvsrc.append((1, kT[rows, (j - 1) * 128: j * 128], ("kT", j - 1), vz[:, j - 1, g * 64: g * 64 + 128], ("vz", j - 1)))
                        else:
                            kvsrc.append((1, kprev[rows, l, :], ("kprev", l), vprev[:, l, g * 64: g * 64 + 128], ("vprev", l)))
                    kvsrc.append((0, kT[rows, j * 128:(j + 1) * 128], ("kT", j), vz[:, j, g * 64: g * 64 + 128], ("vz", j)))
                    qrhs = S[rows, 16:20, qc0:qc0 + 128]
                    qkeys = [("Sq", 16 + c, j, g) for c in range(4)]
                    st_ = {}

                    def stage_a():
                        pts = []
                        for (pc, kap, kkey, vap, vkey) in kvsrc:
                            ps, pk = nps()
                            P.pe(lambda e, ps=ps, kap=kap: e.matmul(ps[:, :].rearrange("p (c q) -> p c q", c=4), lhsT=kap, rhs=qrhs, start=True, stop=True),
                                 reads=[kkey] + qkeys, writes=[pk])
                            tt, tk = nt32()
                            P.dve(lambda e, ps=ps, tt=tt, pc=pc: e.scalar_tensor_tensor(
                                out=tt[:, :].rearrange("p (c q) -> p c q", c=4), in0=ps[:, :].rearrange("p (c q) -> p c q", c=4), scalar=0.125,
                                in1=bias[:, 4 * g:4 * g + 4, pc, :], op0=ALU.mult, op1=ALU.add),
                                reads=[pk, "bias"], writes=[tk])
                            pt, ptk = ntb()
                            P.act(lambda e, tt=tt, pt=pt: e.activation(out=pt[:, :], in_=tt[:, :], func=AF.Exp), reads=[tk], writes=[ptk])
                            pts.append((pt, ptk, vap, vkey))
                        st_["pts"] = pts

                    def stage_b1():
                        pts = st_["pts"]
                        npts = len(pts)
                        psn, pkn = nps()
                        psd, pkd = nps()
                        for i, (pt, ptk, vap, vkey) in enumerate(pts):
                            P.pe(lambda e, psn=psn, vap=vap, pt=pt, i=i: e.matmul(psn[:, :], lhsT=vap, rhs=pt[:, :], start=(i == 0), stop=(i == npts - 1)),
                                 reads=[ptk, vkey], writes=[pkn])
                        for i, (pt, ptk, vap, vkey) in enumerate(pts):
                            P.pe(lambda e, psd=psd, pt=pt, i=i: e.matmul(psd[:, :], lhsT=ones1[:], rhs=pt[:, :], start=(i == 0), stop=(i == npts - 1)),
                                 reads=[ptk, "ones1"], writes=[pkd])
                        dn, dnk = nt32()
                        P.dve(lambda e, psd=psd, dn=dn: e.tensor_tensor(
                            out=dn[rows, :].rearrange("p (c q) -> p c q", c=4), in0=psd[rows, :].rearrange("p (c q) -> p c q", c=4),
                            in1=sinkexp[rows, l * 8 + 4 * g: l * 8 + 4 * g + 4].unsqueeze(2).to_broadcast([64, 4, 128]), op=ALU.add),
                            reads=[pkd, "sinkexp"], writes=[dnk])
                        P.act(lambda e, dn=dn: e.activation(out=dn[rows, :], in_=dn[rows, :], func=AF.Ln), reads=[dnk], writes=[dnk])
                        P.act(lambda e, dn=dn: e.activation(out=dn[rows, :], in_=dn[rows, :], func=AF.Exp, scale=-1.0), reads=[dnk], writes=[dnk])
                        st_["b"] = (psn, pkn, dn, dnk)

                    def stage_b2():
                        psn, pkn, dn, dnk = st_["b"]
                        P.dve(lambda e, psn=psn, dn=dn: e.tensor_tensor(
                            out=S[rows, 16:20, qc0:qc0 + 128], in0=psn[rows, :].rearrange("p (c q) -> p c q", c=4),
                            in1=dn[rows, :].rearrange("p (c q) -> p c q", c=4), op=ALU.mult),
                            reads=[pkn, dnk], writes=qkeys)

                    return stage_a, stage_b1, stage_b2

                fillers = []
                for gi in range(2):
                    for (c0, n, cti) in cts:
                        for mi in range(4):
                            def fill(gi=gi, c0=c0, n=n, cti=cti, mi=mi, l=l):
                                w, wk = wneed(hf, l, "g%d" % gi, ahead=0)
                                mc = 4 * gi + mi
                                ps, pk = nps()
                                mm(ps[:, 0:n], [(w[:, kc * 512 + mi * 128: kc * 512 + (mi + 1) * 128], hb[:, kc, c0:c0 + n]) for kc in range(8)], [wk] + hreads(cti), pk)
                                if (mi + cti) % 2 == 0:
                                    P.dve(lambda e: e.tensor_copy(out=S[:, 8 + mc, c0:c0 + n], in_=ps[:, 0:n]), reads=[pk], writes=[sk(8 + mc, cti)])
                                else:
                                    P.act(lambda e: e.activation(out=S[:, 8 + mc, c0:c0 + n], in_=ps[:, 0:n], func=AF.Copy), reads=[pk], writes=[sk(8 + mc, cti)])
                            fillers.append(fill)
                if not has_s:
                    wneed(hf, l, "ga0", ahead=0)
                groups = [att_group(j, g) for j in range(8) for g in range(2)]
                ng = len(groups)
                nfill = [0]
                for i in range(ng + 2):
                    want = min(len(fillers), ((i + 1) * len(fillers) + ng - 1) // ng)
                    while nfill[0] < want:
                        fillers[nfill[0]]()
                        nfill[0] += 1
                    if i < ng:
                        groups[i][0]()
                    if 0 <= i - 1 < ng:
                        groups[i - 1][1]()
                    if 0 <= i - 2 < ng:
                        groups[i - 2][2]()
                chk("attn")
                if has_s:
                    sc0 = HALF
                    for s8 in range(2):
                        sg, sgk = nstage()
                        P.dma("sp", lambda e, sg=sg, s8=s8, l=l: e.dma_start(out=sg[:, :].rearrange("p (s x) -> p s x", s=8),
                                                                        in_=CK[l, s8 * 8:(s8 + 1) * 8].rearrange("s k x -> k s x")), writes=[sgk])
                        for (r0, r1) in ((1, 16), (16, 128)):
                            outs.append(P.dma("sp", lambda e, sg=sg, s8=s8, l=l, r0=r0, r1=r1: e.dma_start(
                                out=NKS[l, s8 * 8:(s8 + 1) * 8, r0 - 1:r1 - 1, :].rearrange("s k x -> k s x"),
                                in_=sg[r0:r1, :].rearrange("p (s x) -> p s x", s=8)), reads=[sgk], writes=[("nks", l, 0, s8, r0)]))
                        for si in range(8):
                            s = s8 * 8 + si
                            pst, pkt = nps()
                            P.pe(lambda e, pst=pst, sg=sg, si=si: e.transpose(pst[:, 0:128], sg[:, si * 128:(si + 1) * 128], ident[:, :]), reads=[sgk, "ident"], writes=[pkt])
                            P.act(lambda e, pst=pst, s=s: e.activation(out=KsT[:, s, 1:128], in_=pst[:, 1:128], func=AF.Copy), reads=[pkt], writes=[("KsT", s)])
                            P.act(lambda e, s=s: e.activation(out=KsT[:, s, 0:1], in_=kT[:, sc0 + s:sc0 + s + 1], func=AF.Copy), reads=[("kT", "s"), ("KsT", s)], writes=[("KsT", s)])
                    chk("sa1")
                    pssg = [nps(), nps()]
                    for s in range(NSAMP):
                        for g in range(2):
                            rows = slice(g * 64, (g + 1) * 64)
                            P.pe(lambda e, pq=pssg[g][0], s=s, g=g, rows=rows: e.matmul(pq[:, s * 4: s * 4 + 4], lhsT=KsT[rows, s, :],
                                                                                   rhs=S[rows, 16:20, sc0 + s], start=True, stop=True),
                                 reads=[("KsT", s)] + [("Sq", 16 + c, "s", g) for c in range(4)], writes=[pssg[g][1]])
                    chk("sa2")
                    for s8 in range(2):
                        sg, sgk = nstage()
                        P.dma("sp", lambda e, sg=sg, s8=s8, l=l: e.dma_start(out=sg[:, :].rearrange("p (s x) -> p s x", s=8),
                                                                        in_=CV[l, s8 * 8:(s8 + 1) * 8].rearrange("s k x -> k s x")), writes=[sgk])
                        for (r0, r1) in ((1, 16), (16, 128)):
                            outs.append(P.dma("sp", lambda e, sg=sg, s8=s8, l=l, r0=r0, r1=r1: e.dma_start(
                                out=NVS[l, s8 * 8:(s8 + 1) * 8, r0 - 1:r1 - 1, :].rearrange("s k x -> k s x"),
                                in_=sg[r0:r1, :].rearrange("p (s x) -> p s x", s=8)), reads=[sgk], writes=[("nvs", l, 0, s8, r0)]))
                        P.act(lambda e, sg=sg, s8=s8: e.activation(out=Vs[:, s8 * 8:(s8 + 1) * 8, 0:64], in_=sg[:, :].rearrange("p (s x) -> p s x", s=8)[:, :, 0:64], func=AF.Copy),
                              reads=[sgk, "Vs"], writes=["Vs"])
                        P.dve(lambda e, sg=sg, s8=s8: e.tensor_copy(out=Vs[:, s8 * 8:(s8 + 1) * 8, 128:192], in_=sg[:, :].rearrange("p (s x) -> p s x", s=8)[:, :, 64:128]),
                              reads=[sgk, "Vs"], writes=["Vs"])
                    outs.append(P.dma("sp", lambda e, l=l: e.dma_start(out=NVS[l, :, 127, :], in_=vnew32[:, :]), reads=["vnew32"], writes=[("nvs", l, 1)]))
                    for s4 in range(4):
                        psv, pkv = nps()
                        for si in range(4):
                            s = s4 * 4 + si
                            mm(psv[0:1, si * 128:(si + 1) * 128], [(hb[:, kc, sc0 + s:sc0 + s + 1], wkv[:, kc * 256 + 128: kc * 256 + 256]) for kc in range(8)],
                               [wkvk] + hreads(2), pkv)
                        P.act(lambda e, psv=psv, s4=s4: e.activation(out=Vs[0:1, s4 * 4:(s4 + 1) * 4, 0:64], in_=psv[0:1, :].rearrange("p (s x) -> p s x", s=4)[:, :, 0:64], func=AF.Copy),
                              reads=[pkv, "Vs"], writes=["Vs"])
                        P.act(lambda e, psv=psv, s4=s4: e.activation(out=Vs[0:1, s4 * 4:(s4 + 1) * 4, 128:192], in_=psv[0:1, :].rearrange("p (s x) -> p s x", s=4)[:, :, 64:128], func=AF.Copy),
                              reads=[pkv, "Vs"], writes=["Vs"])
                    wneed(hf, l, "ga0")
                    chk("sa3")
                    tt, tk = nt32()
                    for g in range(2):
                        P.dve(lambda e, pq=pssg[g][0], tt=tt, g=g: e.scalar_tensor_tensor(
                            out=tt[:, 0:128].rearrange("p (s h) -> p s h", s=NSAMP)[:, :, 4 * g:4 * g + 4], in0=pq[:, 0:64].rearrange("p (s c) -> p s c", s=NSAMP), scalar=0.125,
                            in1=biasS[:, 4 * g:4 * g + 4].unsqueeze(1).to_broadcast([128, NSAMP, 4]), op0=ALU.mult, op1=ALU.add),
                            reads=[pssg[g][1], "biasS"] + ([tk] if g else []), writes=[tk])
                    pt, ptk = ntb()
                    P.act(lambda e, tt=tt, pt=pt: e.activation(out=pt[:, 0:128], in_=tt[:, 0:128], func=AF.Exp), reads=[tk], writes=[ptk])
                    psd, pkd = nps()
                    P.pe(lambda e, psd=psd, pt=pt: e.matmul(psd[:, 0:128], lhsT=ones1[:], rhs=pt[:, 0:128], start=True, stop=True), reads=[ptk, "ones1"], writes=[pkd])
                    pso, pko = nps()
                    for s in range(NSAMP):
                        for g in range(2):
                            P.pe(lambda e, pso=pso, pt=pt, s=s, g=g: e.matmul(pso[:, s * 4:(s + 1) * 4], lhsT=Vs[:, s, g * 64: g * 64 + 128],
                                                                           rhs=pt[:, s * 8 + g * 4: s * 8 + g * 4 + 4], start=(g == 0), stop=(g == 1)),
                                 reads=[ptk, "Vs"], writes=[pko])
                    dn, dnk = nt32()
                    P.dve(lambda e, psd=psd, dn=dn, l=l: e.tensor_tensor(
                        out=dn[:, 0:128].rearrange("p (s h) -> p s h", s=NSAMP), in0=psd[:, 0:128].rearrange("p (s h) -> p s h", s=NSAMP),
                        in1=sinkexp[:, l * 8:(l + 1) * 8].unsqueeze(1).to_broadcast([128, NSAMP, 8]), op=ALU.add),
                        reads=[pkd, "sinkexp"], writes=[dnk])
                    P.act(lambda e, dn=dn: e.activation(out=dn[:, 0:128], in_=dn[:, 0:128], func=AF.Ln), reads=[dnk], writes=[dnk])
                    P.act(lambda e, dn=dn: e.activation(out=dn[:, 0:128], in_=dn[:, 0:128], func=AF.Exp, scale=-1.0), reads=[dnk], writes=[dnk])
                    for g in range(2):
                        rows = slice(g * 64, (g + 1) * 64)
                        P.dve(lambda e, pso=pso, dn=dn, rows=rows, g=g: e.tensor_tensor(
                            out=S[rows, 16:20, sc0:sc0 + NSAMP].rearrange("p c s -> p s c"),
                            in0=pso[rows, 0:64].rearrange("p (s c) -> p s c", c=4),
                            in1=dn[rows, 0:128].rearrange("p (s g c) -> p s g c", g=2, c=4)[:, :, g, :], op=ALU.mult),
                            reads=[pko, dnk], writes=[("Sq", 16 + c, "s", g) for c in range(4)])
                    chk("sa4")
                    pst, pkt = nps()
                    P.pe(lambda e, pst=pst: e.transpose(pst[0:NSAMP, 0:128], k32keep[:, 128:128 + NSAMP], ident[:, :]), reads=["k32s", "ident"], writes=[pkt])
                    tt, tk = nt32()
                    P.act(lambda e, pst=pst, tt=tt: e.activation(out=tt[0:NSAMP, 0:128], in_=pst[0:NSAMP, 0:128], func=AF.Copy), reads=[pkt], writes=[tk])
                    outs.append(P.dma("sp", lambda e, tt=tt, l=l: e.dma_start(out=NKS[l, :, 127, :], in_=tt[0:NSAMP, 0:128]), reads=[tk], writes=[("nks", l, 1)]))

                chk("sattn")
                if hf == nhalves - 1:
                    pst, pkt = nps()
                    P.pe(lambda e, pst=pst: e.transpose(pst[:, 0:128], k32keep[:, 0:128], ident[:, :]), reads=["k32l", "ident"], writes=[pkt])
                    tt, tk = nt32()
                    P.act(lambda e, pst=pst, tt=tt: e.activation(out=tt[:, 0:128], in_=pst[:, 0:128], func=AF.Copy), reads=[pkt], writes=[tk])
                    outs.append(P.dma("sp", lambda e, tt=tt, l=l: e.dma_start(out=NKP[l], in_=tt[:, 0:128]), reads=[tk], writes=[("nkp", l)]))
                    outs.append(P.dma("sp", lambda e, l=l: e.dma_start(out=NVP[l], in_=v32last[:, :]), reads=["v32last"], writes=[("nvp", l)]))

                chk("kvout")
                for i in range(2):
                    w, wk = wneed(hf, l, "ga%d" % i)
                    for (c0, n, cti) in cts:
                        for mi in range(4):
                            mc = 4 * i + mi
                            ps, pk = nps()
                            mm(ps[:, 0:n], [(w[:, kc * 512 + mi * 128: kc * 512 + (mi + 1) * 128], hb[:, kc, c0:c0 + n]) for kc in range(8)], [wk] + hreads(cti), pk)
                            P.act(lambda e, ps=ps, mc=mc, c0=c0, n=n: e.activation(out=S[:, mc, c0:c0 + n], in_=ps[:, 0:n], func=AF.Sigmoid),
                                  reads=[pk], writes=[sk(mc, cti)])
                w, wk = wneed(hf, l, "woa")
                for (c0, n, cti) in cts:
                    for mc in range(8):
                        ps, pk = nps()
                        rd = [wk]
                        for kc in range(4):
                            rd += qk(16 + kc, cti)
                        mm(ps[:, 0:n], [(w[:, kc * 1024 + mc * 128: kc * 1024 + (mc + 1) * 128], S[:, 16 + kc, c0:c0 + n]) for kc in range(4)], rd, pk)
                        P.dve(lambda e, ps=ps, mc=mc, c0=c0, n=n: e.tensor_tensor(out=S[:, mc, c0:c0 + n], in0=ps[:, 0:n], in1=S[:, mc, c0:c0 + n], op=ALU.mult),
                              reads=[pk, sk(mc, cti)], writes=[sk(mc, cti)])

                chk("oa")
                ntot = NCOL if has_s else HALF
                for mc in range(8):
                    P.act(lambda e, mc=mc, ntot=ntot: e.activation(out=S[:, 8 + mc, 0:ntot], in_=S[:, 8 + mc, 0:ntot], func=AF.Gelu_apprx_tanh),
                          reads=[sk(8 + mc, t_[2]) for t_ in cts], writes=[sk(8 + mc, t_[2]) for t_ in cts])
                chk("gelu")
                if has_s:
                    tm2fm(SCV[l].rearrange("s j d -> (s j) d"), 3 * NSAMP, lambda c: cbT[:, c, :], [("cbT", c) for c in range(8)])
                    tm2fm(SLH[l], NSAMP, lambda c: h0T[:, c, :], [("h0T", c) for c in range(8)])
                def lru_prep(c, l=l):
                    def run():
                        w, wk = wneed(hf, l, "r%d" % c, ahead=NBUF - 2)
                        for jt in range(4):
                            P.dve(lambda e, jt=jt: e.tensor_scalar(out=diag[:, c % 2, jt, :], in0=identb[:, :], scalar1=vecs[:, l, 3 + jt, c:c + 1], scalar2=None, op0=ALU.mult),
                                  reads=["identb", "vecs"], writes=[("diag", c % 2)])
                        xb = xrb[c % 2]
                        xbk = ("xrb", c % 2)
                        if hf == 0:
                            P.dve(lambda e: e.memset(xb[:, 0:3], 0.0), writes=[xbk])
                        else:
                            P.act(lambda e: e.activation(out=xb[:, 0:3], in_=xtail[:, l, c, :], func=AF.Copy), reads=[("xtail", l, c)], writes=[xbk])
                        for (c0, n, cti) in cts:
                            ps, pk = nps()
                            mm(ps[:, 0:n], [(w[:, kc * 128:(kc + 1) * 128], hb[:, kc, c0:c0 + n]) for kc in range(8)], [wk] + hreads(cti), pk)
                            P.dve(lambda e, ps=ps, c0=c0, n=n: e.tensor_copy(out=xb[:, 3 + c0:3 + c0 + n], in_=ps[:, 0:n]), reads=[pk, xbk], writes=[xbk])
                            if cti == 2:
                                P.dve(lambda e, ps=ps, n=n: e.tensor_copy(out=xr32s[:, c, :], in_=ps[:, 0:n]), reads=[pk], writes=[("xr32s", c)])
                            if hf == nhalves - 1 and cti == 1:
                                P.dve(lambda e, ps=ps: e.tensor_copy(out=xr32l[:, c, :], in_=ps[:, 509:512]), reads=[pk], writes=[("xr32l", c)])
                        if hf == 0 and nhalves > 1:
                            P.act(lambda e: e.activation(out=xtail[:, l, c, :], in_=xb[:, HALF:HALF + 3], func=AF.Copy), reads=[xbk], writes=[("xtail", l, c)])
                    return run

                def lru_step(i_, c, c0, n, cti, l=l):
                    w = wring[:, wpos[(hf, l, "r%d" % c)] % NBUF, :]
                    wk = ("w", wpos[(hf, l, "r%d" % c)] % NBUF)
                    xb = xrb[c % 2]
                    xbk = ("xrb", c % 2)
                    ra, rak = raR[i_ % 3], ("raR", i_ % 3)
                    ig, igk = igR[i_ % 3], ("igR", i_ % 3)
                    a2, a2k = a2R[i_ % 2], ("a2R", i_ % 2)
                    xcb, xcbk = xcbR[i_ % 3], ("xcbR", i_ % 3)

                    def a1():
                        ps, pk = nps()
                        if cti < 2:
                            pairs = [(diag[:, c % 2, jt, :], xb[:, c0 + jt:c0 + jt + n]) for jt in range(4)]
                            rd = [("diag", c % 2), xbk]
                        else:
                            pairs = [(diag[:, c % 2, jt, :], cbT[:, c, :].rearrange("p (s j) -> p s j", j=3)[:, :, jt]) for jt in range(3)] + [(diag[:, c % 2, 3, :], xb[:, 3 + c0:3 + c0 + n])]
                            rd = [("diag", c % 2), xbk, ("cbT", c)]
                        mm(ps[:, 0:n], pairs, rd, pk)
                        P.dve(lambda e: e.tensor_scalar(out=xcb[:, 0:n], in0=ps[:, 0:n], scalar1=vecs[:, l, 7, c:c + 1], scalar2=None, op0=ALU.add), reads=[pk, "vecs"], writes=[xcbk])

                    def a2_():
                        psr, pkr = nps()
                        psi, pki = nps()
                        P.pe(lambda e: e.matmul(psr[:, 0:n], lhsT=w[:, 1024:1152], rhs=xcb[:, 0:n], start=True, stop=True), reads=[wk, xcbk], writes=[pkr])
                        P.pe(lambda e: e.matmul(psi[:, 0:n], lhsT=w[:, 1152:1280], rhs=xcb[:, 0:n], start=True, stop=True), reads=[wk, xcbk], writes=[pki])
                        P.act(lambda e: e.activation(out=ra[:, 0:n], in_=psr[:, 0:n], func=AF.Exp, scale=-1.0, bias=hbias[:, l, 0, c:c + 1]), reads=[pkr, ("hbias", l)], writes=[rak])
                        P.act(lambda e: e.activation(out=ig[:, 0:n], in_=psi[:, 0:n], func=AF.Exp, scale=-1.0, bias=hbias[:, l, 1, c:c + 1]), reads=[pki, ("hbias", l)], writes=[igk])
                        P.act(lambda e: e.activation(out=ra[:, 0:n], in_=ra[:, 0:n], func=AF.Ln, bias=1.0), reads=[rak], writes=[rak])
                        P.act(lambda e: e.activation(out=ig[:, 0:n], in_=ig[:, 0:n], func=AF.Ln, bias=1.0), reads=[igk], writes=[igk])
                        P.act(lambda e: e.activation(out=ra[:, 0:n], in_=ra[:, 0:n], func=AF.Exp, scale=-1.0), reads=[rak], writes=[rak])
                        P.act(lambda e: e.activation(out=a2[:, 0:n], in_=ra[:, 0:n], func=AF.Exp, scale=cl[:, l, 1, c:c + 1]), reads=[rak, ("cl2", l)], writes=[a2k])
                        P.act(lambda e: e.activation(out=ra[:, 0:n], in_=ra[:, 0:n], func=AF.Exp, scale=cl[:, l, 0, c:c + 1]), reads=[rak, ("cl", l)], writes=[rak])
                        P.act(lambda e: e.activation(out=a2[:, 0:n], in_=a2[:, 0:n], func=AF.Ln, scale=-1.0, bias=1.0), reads=[a2k], writes=[a2k])

                    def b_():
                        P.dve(lambda e: e.scalar_tensor_tensor(out=ig[:, 0:n], in0=a2[:, 0:n], scalar=0.5, in1=ig[:, 0:n], op0=ALU.mult, op1=ALU.subtract),
                              reads=[igk, a2k], writes=[igk])
                        P.act(lambda e: e.activation(out=ig[:, 0:n], in_=ig[:, 0:n], func=AF.Exp), reads=[igk], writes=[igk])

                    def b_post():
                        P.dve(lambda e: e.tensor_tensor(out=ig[:, 0:n], in0=ig[:, 0:n], in1=xcb[:, 0:n], op=ALU.mult), reads=[igk, xcbk], writes=[igk])

                    def c_():
                        hq, hqk = nt32()
                        if cti < 2:
                            first = (hf == 0 and cti == 0)
                            P.dve(lambda e: e.tensor_tensor_scan(
                                out=hq[:, 0:n], data0=ra[:, 0:n], data1=ig[:, 0:n], initial=(0.0 if first else hlast[:, l, c:c + 1]), op0=ALU.mult, op1=ALU.add),
                                reads=[rak, igk] + ([] if first else [("hlast", l, c)]), writes=[hqk])
                            P.dve(lambda e: e.tensor_copy(out=hlast[:, l, c:c + 1], in_=hq[:, n - 1:n]), reads=[hqk], writes=[("hlast", l, c)])
                            src, srck = hq[:, 0:n], hqk
                        else:
                            P.dve(lambda e: e.tensor_tensor(out=hq[:, 0:n], in0=ra[:, 0:n], in1=h0T[:, c, :], op=ALU.mult), reads=[rak, ("h0T", c)], writes=[hqk])
                            P.dve(lambda e: e.tensor_tensor(out=hs32[:, c, :], in0=hq[:, 0:n], in1=ig[:, 0:n], op=ALU.add), reads=[hqk, igk], writes=[("hs32", c)])
                            src, srck = hs32[:, c, :], ("hs32", c)
                        P.pool(lambda e: e.tensor_tensor(out=S[:, 8 + c, c0:c0 + n], in0=src, in1=S[:, 8 + c, c0:c0 + n], op=ALU.mult),
                               reads=[srck, sk(8 + c, cti)], writes=[sk(8 + c, cti)])

                    return [a1, a2_, b_, c_, b_post]

                steps = []
                for c in range(8):
                    for k_, (c0, n, cti) in enumerate(cts):
                        stg = lru_step(len(steps), c, c0, n, cti)
                        steps.append([lru_prep(c) if k_ == 0 else None] + stg)
                nst = 5
                for t in range(len(steps) + nst - 1):
                    for s_ in (3, 4, 2, 1, 0):
                        i = t - s_
                        if 0 <= i < len(steps) and steps[i][s_] is not None:
                            steps[i][s_]()
                    i = t - 3
                    if 0 <= i < len(steps):
                        steps[i][5]()
                chk("lru")
                if has_s:
                    fm2tm(lambda c: hs32[:, c, :], [("hs32", c) for c in range(8)], NSAMP, NHS[l], ("nhs", l))
                    fm2tm(lambda c: xr32s[:, c, :], [("xr32s", c) for c in range(8)], NSAMP, NCS[l, :, 2, :], ("ncs", l, 1))
                    sg, sgk = nstage()
                    for j in range(2):
                        P.dma("sp", lambda e, sg=sg, l=l, j=j: e.dma_start(out=sg[j * NSAMP:(j + 1) * NSAMP, :], in_=SCV[l, :, 1 + j, :]), reads=[sgk] if j else [], writes=[sgk])
                    for j in range(2):
                        outs.append(P.dma("sp", lambda e, sg=sg, l=l, j=j: e.dma_start(out=NCS[l, :, j, :], in_=sg[j * NSAMP:(j + 1) * NSAMP, :]), reads=[sgk], writes=[("ncs", l, 0, j)]))
                if hf == nhalves - 1:
                    fm2tm(lambda c: xr32l[:, c, :], [("xr32l", c) for c in range(8)], 3, NCP[l], ("ncp", l))
                    pst, pkt = nps()
                    P.pe(lambda e, pst=pst, l=l: e.transpose(pst[0:8, 0:128], hlast[:, l, :], ident[:, :]), reads=[("hlast", l, c) for c in range(8)] + ["ident"], writes=[pkt])
                    tt, tk = nt32()
                    P.act(lambda e, pst=pst, tt=tt: e.activation(out=tt[0:8, 0:128], in_=pst[0:8, 0:128], func=AF.Copy), reads=[pkt], writes=[tk])
                    outs.append(P.dma("sp", lambda e, tt=tt, l=l: e.dma_start(out=NHP[l], in_=tt[0:8, 0:128]), reads=[tk], writes=[("nhp", l)]))

                chk("lruout")
                for i in range(4):
                    w, wk = wneed(hf, l, "ol%d" % i)
                    for (c0, n, cti) in cts:
                        for mi in range(2):
                            mc = 2 * i + mi
                            pso, pko = nps()
                            psg, pkg = nps()
                            mm(psg[:, 0:n], [(w[:, 2048 + kc * 256 + mi * 128: 2048 + kc * 256 + (mi + 1) * 128], hb[:, kc, c0:c0 + n]) for kc in range(8)], [wk] + hreads(cti), pkg)
                            mm(pso[:, 0:n], [(w[:, kc * 256 + mi * 128: kc * 256 + (mi + 1) * 128], S[:, 8 + kc, c0:c0 + n]) for kc in range(8)],
                               [wk] + [sk(8 + kc, cti) for kc in range(8)], pko)
                            sg_, sgk_ = nt32()
                            P.act(lambda e, psg=psg, sg_=sg_, n=n: e.activation(out=sg_[:, 0:n], in_=psg[:, 0:n], func=AF.Sigmoid), reads=[pkg], writes=[sgk_])
                            P.dve(lambda e, pso=pso, sg_=sg_, n=n: e.tensor_tensor(out=sg_[:, 0:n], in0=pso[:, 0:n], in1=sg_[:, 0:n], op=ALU.mult), reads=[pko, sgk_], writes=[sgk_])
                            P.pool(lambda e, sg_=sg_, mc=mc, c0=c0, n=n: e.tensor_tensor(out=S[:, mc, c0:c0 + n], in0=sg_[:, 0:n], in1=S[:, mc, c0:c0 + n], op=ALU.add),
                                   reads=[sgk_, sk(mc, cti)], writes=[sk(mc, cti)])
                chk("ol")
                for i in range(2):
                    w, wk = wneed(hf, l, "wo%d" % i)
                    for (c0, n, cti) in cts:
                        for mi in range(4):
                            mc = 4 * i + mi
                            ps, pk = nps()
                            mm(ps[:, 0:n], [(w[:, kc * 512 + mi * 128: kc * 512 + (mi + 1) * 128], S[:, kc, c0:c0 + n]) for kc in range(8)],
                               [wk] + [sk(kc, cti) for kc in range(8)], pk)
                            P.dve(lambda e, ps=ps, mc=mc, c0=c0, n=n: e.tensor_tensor(out=xres[:, mc, c0:c0 + n], in0=ps[:, 0:n], in1=xres[:, mc, c0:c0 + n], op=ALU.add),
                                  reads=[pk, xk(mc, cti)], writes=[xk(mc, cti)])
                chk("wo")
                rmsnorm(1)
                for i in range(11):
                    w, wk = wneed(hf, l, "ff%d" % i)
                    for (c0, n, cti) in cts:
                        for mi in range(2):
                            f = 2 * i + mi
                            psg, pkg = nps()
                            psu, pku = nps()
                            mm(psg[:, 0:n], [(w[:, kc * 256 + mi * 128: kc * 256 + (mi + 1) * 128], hb[:, kc, c0:c0 + n]) for kc in range(8)], [wk] + hreads(cti), pkg)
                            mm(psu[:, 0:n], [(w[:, 2048 + kc * 256 + mi * 128: 2048 + kc * 256 + (mi + 1) * 128], hb[:, kc, c0:c0 + n]) for kc in range(8)], [wk] + hreads(cti), pku)
                            sg_, sgk_ = nt32()
                            P.act(lambda e, psg=psg, sg_=sg_, n=n: e.activation(out=sg_[:, 0:n], in_=psg[:, 0:n], func=AF.Silu), reads=[pkg], writes=[sgk_])
                            wkeys = qk(f, cti) if 16 <= f < 20 else [sk(f, cti)]
                            P.dve(lambda e, psu=psu, sg_=sg_, f=f, c0=c0, n=n: e.tensor_tensor(out=S[:, f, c0:c0 + n], in0=psu[:, 0:n], in1=sg_[:, 0:n], op=ALU.mult),
                                  reads=[pku, sgk_], writes=wkeys)
                for m in range(8):
                    w, wk = wneed(hf, l, "dn%d" % m)
                    for (c0, n, cti) in cts:
                        ps, pk = nps()
                        rd = [wk]
                        for f in range(KFF):
                            rd += qk(f, cti) if 16 <= f < 20 else [sk(f, cti)]
                        mm(ps[:, 0:n], [(w[:, kc * 128:(kc + 1) * 128], S[:, kc, c0:c0 + n]) for kc in range(KFF)], rd, pk)
                        P.dve(lambda e, ps=ps, m=m, c0=c0, n=n: e.tensor_tensor(out=xres[:, m, c0:c0 + n], in0=ps[:, 0:n], in1=xres[:, m, c0:c0 + n], op=ALU.add),
                              reads=[pk, xk(m, cti)], writes=[xk(m, cti)])
                chk("ffn")
                rmsnorm(2)
                for b4 in range(2):
                    sg, sgk = nstage()
                    P.dma("sp", lambda e, sg=sg, b4=b4, l=l, t0=t0: e.dma_start(out=sg[:, :].rearrange("p (b x) -> p b x", b=4),
                                                                    in_=PP[l, t0 + b4 * 512: t0 + (b4 + 1) * 512, :].rearrange("(b p) x -> p b x", p=128)), writes=[sgk])
                    for kc in range(2):
                        ps, pk = nps()
                        for bi in range(4):
                            P.pe(lambda e, ps=ps, sg=sg, bi=bi, kc=kc: e.transpose(ps[:, bi * 128:(bi + 1) * 128], sg[:, bi * 256 + kc * 128: bi * 256 + (kc + 1) * 128], ident[:, :]),
                                 reads=[sgk, "ident"], writes=[pk])
                        P.act(lambda e, ps=ps, kc=kc, b4=b4: e.activation(out=pT[:, kc, b4 * 512:(b4 + 1) * 512], in_=ps[:, :], func=AF.Copy), reads=[pk], writes=[("pT", kc, b4), ("xrb", kc)])
                if has_s:
                    sg, sgk = nstage()
                    P.dma("sp", lambda e, sg=sg, l=l: e.dma_start(out=sg[0:NSAMP, 0:256], in_=PS_[l]), writes=[sgk])
                    ps, pk = nps()
                    for kc in range(2):
                        P.pe(lambda e, ps=ps, sg=sg, kc=kc: e.transpose(ps[:, kc * 128: kc * 128 + NSAMP], sg[0:NSAMP, kc * 128:(kc + 1) * 128], ident[0:NSAMP, 0:NSAMP]),
                             reads=[sgk, "ident"], writes=[pk])
                    for kc in range(2):
                        P.act(lambda e, ps=ps, kc=kc: e.activation(out=pT[:, kc, HALF:HALF + NSAMP], in_=ps[:, kc * 128: kc * 128 + NSAMP], func=AF.Copy), reads=[pk], writes=[("pT", kc, 2), ("xrb", kc)])
                for i in range(3):
                    w, wk = wneed(hf, l, "pl%d" % i)
                    c0_, c1_ = 3 * i, min(3 * i + 3, 8)
                    nch = c1_ - c0_
                    for (c0, n, cti) in cts:
                        for mi in range(nch):
                            mc = c0_ + mi
                            psg, pkg = nps()
                            psp, pkp = nps()
                            mm(psg[:, 0:n], [(w[:, kc * 128 * nch + mi * 128: kc * 128 * nch + (mi + 1) * 128], hb[:, kc, c0:c0 + n]) for kc in range(8)], [wk] + hreads(cti), pkg)
                            o2 = 8 * 128 * nch
                            mm(psp[:, 0:n], [(w[:, o2 + kc * 128 * nch + mi * 128: o2 + kc * 128 * nch + (mi + 1) * 128], pT[:, kc, c0:c0 + n]) for kc in range(2)],
                               [wk, ("pT", 0, cti), ("pT", 1, cti), ("xrb", 0), ("xrb", 1)], pkp)
                            sg_, sgk_ = nt32()
                            P.act(lambda e, psg=psg, sg_=sg_, n=n: e.activation(out=sg_[:, 0:n], in_=psg[:, 0:n], func=AF.Sigmoid), reads=[pkg], writes=[sgk_])
                            P.dve(lambda e, psp=psp, sg_=sg_, n=n: e.tensor_tensor(out=sg_[:, 0:n], in0=psp[:, 0:n], in1=sg_[:, 0:n], op=ALU.mult), reads=[pkp, sgk_], writes=[sgk_])
                            P.pool(lambda e, sg_=sg_, mc=mc, c0=c0, n=n: e.tensor_tensor(out=xres[:, mc, c0:c0 + n], in0=sg_[:, 0:n], in1=xres[:, mc, c0:c0 + n], op=ALU.add),
                                   reads=[sgk_, xk(mc, cti)], writes=[xk(mc, cti)])

            chk("ple")
            for b in range(8):
                cti = b // 4
                fm2tm(lambda c, b=b: xres[:, c, b * 128:(b + 1) * 128], [xk(c, cti) for c in range(8)], 128, YP[t0 + b * 128: t0 + (b + 1) * 128, :], ("yp", hf, b))
            if has_s:
                fm2tm(lambda c: xres[:, c, HALF:HALF + NSAMP], [xk(c, 2) for c in range(8)], NSAMP, YS[:, :], ("ys",))

        P.emit(final_wait_ops=[o for o in outs if o is not None])
    return nc


def _vec_layout(v):
    return np.ascontiguousarray(v.reshape(8, 128).T)


def make_in_maps(inp, nlayers=NL):
    f = lambda a: np.ascontiguousarray(np.asarray(a, dtype=np.float32))
    W = {k: f(inp[k]) for k in ["w_in", "w_o_attn", "w_a", "w_x", "w_o_lru", "w_out", "w_gate", "w_up", "w_down", "w_ple", "w_ple_gate"]}
    wall = np.zeros((NL, 128, TOTW), np.float32)
    for l in range(NL):
        for name, subs in _items(l, W):
            off, sz = ITEM_OFF[name]
            cat = np.concatenate(subs, axis=1)
            assert cat.shape == (128, sz), (name, cat.shape, sz)
            wall[l, :, off:off + sz] = cat
    vecs = np.zeros((128, NL, NV, 8), np.float32)
    cw = f(inp["conv_w"])
    srcs = {"ln1": f(inp["ln1"]), "ln2": f(inp["ln2"]), "ln3": f(inp["ln3"]), "cw0": cw[:, 0], "cw1": cw[:, 1], "cw2": cw[:, 2], "cw3": cw[:, 3],
            "conv_b": f(inp["conv_b"]), "b_a": f(inp["b_a"]), "b_x": f(inp["b_x"]), "lam": f(inp["lam"])}
    for vi, vn in enumerate(VEC_NAMES):
        for l in range(NL):
            vecs[:, l, vi, :] = _vec_layout(srcs[vn][l])
    vecs = vecs.reshape(128, NL * NV * 8)
    qg, kg = f(inp["q_gain"]), f(inp["k_gain"])
    qkg = np.zeros((128, 2 * NL), np.float32)
    for l in range(NL):
        qkg[:, l] = np.tile(qg[l], 2)
        qkg[:, NL + l] = np.tile(kg[l], 2)
    sinks = f(inp["sinks"]).reshape(1, NL * 8)
    t5 = f(inp["t5_table"])
    oh = _t5_onehot()
    ident = np.eye(128, dtype=np.float32)
    xp, xs = f(inp["x_prompt"]), f(inp["x_sample"])
    pp, psm = f(inp["p_prompt"]), f(inp["p_sample"])
    ck, cv = f(inp["cache_k_win"]), f(inp["cache_v_win"])
    slh, scv = f(inp["state_lru_h"]), f(inp["state_conv"])
    maps = []
    for c in range(NCORES):
        s0, s1 = c * NSAMP, (c + 1) * NSAMP
        maps.append(dict(
            xp=xp[c], xs=np.ascontiguousarray(xs[s0:s1, 0, :]), pp=np.ascontiguousarray(pp[:, c]), psm=np.ascontiguousarray(psm[:, s0:s1, 0, :]),
            ck=np.ascontiguousarray(ck[:, s0:s1].reshape(NL, NSAMP, 128, 128)), cv=np.ascontiguousarray(cv[:, s0:s1].reshape(NL, NSAMP, 128, 128)),
            slh=np.ascontiguousarray(slh[:, s0:s1]), scv=np.ascontiguousarray(scv[:, s0:s1]),
            wall=wall, vecs=vecs, qkg=qkg, sinks=sinks, t5=t5, oh=oh, ident=ident))
    return maps


def gather(results):
    r = results
    yp = np.stack([np.asarray(r[c]["yp"]) for c in range(NCORES)], 0)
    ys = np.concatenate([np.asarray(r[c]["ys"]) for c in range(NCORES)], 0)[:, None, :]
    nkp = np.stack([np.asarray(r[c]["nkp"]).reshape(NL, 128, 2, 64) for c in range(NCORES)], 1)
    nvp = np.stack([np.asarray(r[c]["nvp"]).reshape(NL, 128, 2, 64) for c in range(NCORES)], 1)
    nhp = np.stack([np.asarray(r[c]["nhp"]).reshape(NL, D) for c in range(NCORES)], 1)
    ncp = np.stack([np.asarray(r[c]["ncp"]) for c in range(NCORES)], 1)
    nks = np.concatenate([np.asarray(r[c]["nks"]).reshape(NL, NSAMP, 128, 2, 64) for c in range(NCORES)], 1)
    nvs = np.concatenate([np.asarray(r[c]["nvs"]).reshape(NL, NSAMP, 128, 2, 64) for c in range(NCORES)], 1)
    nhs = np.concatenate([np.asarray(r[c]["nhs"]) for c in range(NCORES)], 1)
    ncs = np.concatenate([np.asarray(r[c]["ncs"]) for c in range(NCORES)], 1)
    return tuple(np.ascontiguousarray(a, dtype=np.float32) for a in (yp, ys, nkp, nvp, nhp, ncp, nks, nvs, nhs, ncs))


def kernel(**inputs):
    nc = build()
    maps = make_in_maps(inputs)
    res = run_bass_kernel_spmd(nc, maps, core_ids=list(range(NCORES)))
    return gather(res.results)
```
